# Optimizing a Trainium2 kernel written in Bass

```python
import jax, jax.numpy as jnp
from jax import lax
import numpy as np

D_MODEL = 1024
BATCH = 32
SEQ = 256
DEPTH = 4
DEC_BATCH = 8
DEC_SEQ = 4096
PAST_LEN = 512

GRID_W = 64
N_MLA_LAYERS = (DEPTH + 1) // 2
N_POOL_LAYERS = DEPTH // 2
N_MOD = 9
FFN_HIDDEN = 2816
EPS = 1e-6
MLA_HEADS = D_MODEL // 128
Q_LORA = D_MODEL // 4
KV_LORA = D_MODEL // 8
QK_NOPE = 64
QK_ROPE = 32
V_HEAD = 64
QK_HEAD = QK_NOPE + QK_ROPE
ROPE_BASE = 10000.0
ATTN_BLOCK = 128
MLA_WIDTH = MLA_HEADS * V_HEAD
GMLP_GROUPS = 4
GMLP_CHUNK = 128
GMLP_WIDTH = D_MODEL // 2
GMLP_GROUP_CH = GMLP_WIDTH // GMLP_GROUPS
MIX_WIDTH = MLA_WIDTH + GMLP_WIDTH
OFF_KV = Q_LORA
OFF_KR = Q_LORA + KV_LORA
OFF_G = Q_LORA + KV_LORA + QK_ROPE
IN_DIM = OFF_G + 2 * GMLP_WIDTH
POOL_WINDOWS = (2, 4, 8, 16)
POOL_GROUP_CH = D_MODEL // len(POOL_WINDOWS)

kernel_name = 'hybrid_mla_gmlp_pool_diffusion_step'


def rms_norm(x, g):
    x32 = x.astype(jnp.float32)
    y = x32 * lax.rsqrt(jnp.mean(x32 * x32, axis=-1, keepdims=True) + EPS)
    return (y * g.astype(jnp.float32)).astype(x.dtype)


def modulate(x, g, m, k):
    return rms_norm(x, g) * (1 + m[:, 3 * k + 1, None, :]) + m[:, 3 * k, None, :]


def swiglu(h, w1, w3, w2):
    return (jax.nn.silu(h @ w1) * (h @ w3)) @ w2


def axial_rope_tables(rows):
    row = jnp.repeat(jnp.arange(rows), GRID_W).astype(jnp.float32)
    col = jnp.tile(jnp.arange(GRID_W), rows).astype(jnp.float32)
    per_axis = QK_ROPE // 2
    inv = ROPE_BASE ** (-jnp.arange(0, per_axis, 2, dtype=jnp.float32) / per_axis)
    ang = jnp.concatenate([row[:, None] * inv, col[:, None] * inv], axis=-1)
    return jnp.cos(ang), jnp.sin(ang)


def apply_rope(x, cos, sin):
    half = QK_ROPE // 4
    xr = x.astype(jnp.float32).reshape(x.shape[:-1] + (2, 2, half))
    x1, x2 = xr[..., 0, :], xr[..., 1, :]
    c = cos.reshape(cos.shape[0], 2, half)[None, :, None]
    s = sin.reshape(sin.shape[0], 2, half)[None, :, None]
    out = jnp.stack([x1 * c - x2 * s, x1 * s + x2 * c], axis=-2)
    return out.reshape(x.shape).astype(x.dtype)


def block_attention(q, k, v):
    B, Lq, H, Dh = q.shape
    nblk = Lq // ATTN_BLOCK
    qb = q.reshape(B, nblk, ATTN_BLOCK, H, Dh).transpose(1, 0, 2, 3, 4)
    scale = Dh ** -0.5

    def one(qblk):
        s = jnp.einsum('bqhd,bkhd->bhqk', qblk, k, preferred_element_type=jnp.float32) * scale
        p = jax.nn.softmax(s, axis=-1).astype(v.dtype)
        return jnp.einsum('bhqk,bkhd->bqhd', p, v)

    o = lax.map(one, qb)
    return o.transpose(1, 0, 2, 3, 4).reshape(B, Lq, H, v.shape[-1])


def mla_kv(kv_lat, k_rope, w_kvb, k_norm):
    B, L, _ = kv_lat.shape
    kv = (kv_lat @ w_kvb).reshape(B, L, MLA_HEADS, QK_NOPE + V_HEAD)
    k_nope, v = kv[..., :QK_NOPE], kv[..., QK_NOPE:]
    k = jnp.concatenate([k_nope, jnp.broadcast_to(k_rope[:, :, None, :], (B, L, MLA_HEADS, QK_ROPE))], axis=-1)
    return rms_norm(k, k_norm), v


def chunk_gmlp(g, v_norm, ws, bs):
    u, v = g[..., :GMLP_WIDTH], g[..., GMLP_WIDTH:]
    v = rms_norm(v, v_norm)
    B, L, _ = v.shape
    vr = v.reshape(B, L // GMLP_CHUNK, GMLP_CHUNK, GMLP_GROUPS, GMLP_GROUP_CH)
    mixed = jnp.einsum('gpq,bnqgc->bnpgc', ws, vr) + bs.T[:, :, None]
    return u * mixed.reshape(B, L, GMLP_WIDTH)


def multiscale_pool(h, pool_w, pool_scale):
    B, L, D = h.shape
    h32 = h.astype(jnp.float32)
    cs = jnp.concatenate([jnp.zeros((B, 1, D), jnp.float32), jnp.cumsum(h32, axis=1)], axis=1)
    t = jnp.arange(L)
    outs = []
    for gi, w in enumerate(POOL_WINDOWS):
        sl = slice(gi * POOL_GROUP_CH, (gi + 1) * POOL_GROUP_CH)
        lo = jnp.clip(t - w // 2, 0, L)
        hi = jnp.clip(t + w - w // 2, 0, L)
        csg = cs[:, :, sl]
        mean = (csg[:, hi] - csg[:, lo]) / (hi - lo).astype(jnp.float32)[None, :, None]
        outs.append((mean - h32[:, :, sl]).astype(h.dtype) @ pool_w[gi])
    return jnp.concatenate(outs, axis=-1) * pool_scale


def trunk(x, cvec, rope, cache_ckv, cache_krope, w_mod, b_mod, norm_g, ffn_w1, ffn_w3, ffn_w2,
          w_in, q_a_norm, kv_a_norm, w_qb, w_kvb, q_norm, k_norm, gmlp_v_norm, gmlp_ws, gmlp_b,
          w_out, pool_w, pool_scale):
    B, L, _ = x.shape
    ckv_out, krope_out = [], []
    for i in range(DEPTH):
        m = (jax.nn.silu(cvec) @ w_mod[i] + b_mod[i]).reshape(cvec.shape[0], N_MOD, D_MODEL)
        h = modulate(x, norm_g[i, 0], m, 0)
        x = x + 0.5 * m[:, 2, None, :] * swiglu(h, ffn_w1[i, 0], ffn_w3[i, 0], ffn_w2[i, 0])
        h = modulate(x, norm_g[i, 1], m, 1)
        if i % 2 == 0:
            j = i // 2
            proj = h @ w_in[j]
            q_lat = rms_norm(proj[..., :OFF_KV], q_a_norm[j])
            kv_lat = rms_norm(proj[..., OFF_KV:OFF_KR], kv_a_norm[j])
            k_rope = proj[..., OFF_KR:OFF_G]
            gm = jax.nn.gelu(proj[..., OFF_G:])
            q = rms_norm((q_lat @ w_qb[j]).reshape(B, L, MLA_HEADS, QK_HEAD), q_norm[j])
            k, v = mla_kv(kv_lat, k_rope, w_kvb[j], k_norm[j])
            if cache_ckv is None:
                attn = block_attention(q, k, v)
                ckv_out.append(kv_lat)
                krope_out.append(k_rope)
            else:
                cos, sin = rope
                q = jnp.concatenate([q[..., :QK_NOPE], apply_rope(q[..., QK_NOPE:], cos, sin)], axis=-1)
                k = jnp.concatenate([k[..., :QK_NOPE], apply_rope(k[..., QK_NOPE:], cos, sin)], axis=-1)
                kc, vc = mla_kv(cache_ckv[:, j], cache_krope[:, j], w_kvb[j], k_norm[j])
                attn = block_attention(q, jnp.concatenate([k, kc], axis=1), jnp.concatenate([v, vc], axis=1))
            gout = chunk_gmlp(gm, gmlp_v_norm[j], gmlp_ws[j], gmlp_b[j])
            mix = jnp.concatenate([attn.reshape(B, L, MLA_WIDTH), gout], axis=-1) @ w_out[j]
        else:
            j = i // 2
            mix = multiscale_pool(h, pool_w[j], pool_scale[j])
        x = x + m[:, 5, None, :] * mix
        h = modulate(x, norm_g[i, 2], m, 2)
        x = x + 0.5 * m[:, 8, None, :] * swiglu(h, ffn_w1[i, 1], ffn_w3[i, 1], ffn_w2[i, 1])
    return x, ckv_out, krope_out


def setup_inputs(seed: int = 0) -> dict:
    key = jax.random.key(seed)
    ks = jax.random.split(key, 32)
    nrm = lambda k, shape, s: jax.random.normal(k, shape, jnp.float32) * s
    gain = lambda k, shape: 1.0 + 0.02 * jax.random.normal(k, shape, jnp.float32)
    return {
        'x_prompt': nrm(ks[0], (BATCH, SEQ, D_MODEL), 1.0),
        'x_sample': nrm(ks[1], (DEC_BATCH, DEC_SEQ, D_MODEL), 1.0),
        'cache_ckv': nrm(ks[2], (DEC_BATCH, N_MLA_LAYERS, PAST_LEN, KV_LORA), 1.0),
        'cache_krope': nrm(ks[3], (DEC_BATCH, N_MLA_LAYERS, PAST_LEN, QK_ROPE), 1.0),
        'c': nrm(ks[4], (DEC_BATCH, D_MODEL), 1.0),
        'c_ctx': nrm(ks[5], (D_MODEL,), 1.0),
        'w_mod': nrm(ks[6], (DEPTH, D_MODEL, N_MOD * D_MODEL), 0.5 * D_MODEL ** -0.5),
        'b_mod': nrm(ks[7], (DEPTH, N_MOD * D_MODEL), 0.02),
        'norm_g': gain(ks[8], (DEPTH, 3, D_MODEL)),
        'ffn_w1': nrm(ks[9], (DEPTH, 2, D_MODEL, FFN_HIDDEN), D_MODEL ** -0.5),
        'ffn_w3': nrm(ks[10], (DEPTH, 2, D_MODEL, FFN_HIDDEN), D_MODEL ** -0.5),
        'ffn_w2': nrm(ks[11], (DEPTH, 2, FFN_HIDDEN, D_MODEL), FFN_HIDDEN ** -0.5),
        'w_in': nrm(ks[12], (N_MLA_LAYERS, D_MODEL, IN_DIM), D_MODEL ** -0.5),
        'q_a_norm': gain(ks[13], (N_MLA_LAYERS, Q_LORA)),
        'kv_a_norm': gain(ks[14], (N_MLA_LAYERS, KV_LORA)),
        'w_qb': nrm(ks[15], (N_MLA_LAYERS, Q_LORA, MLA_HEADS * QK_HEAD), Q_LORA ** -0.5),
        'w_kvb': nrm(ks[16], (N_MLA_LAYERS, KV_LORA, MLA_HEADS * (QK_NOPE + V_HEAD)), KV_LORA ** -0.5),
        'q_norm': gain(ks[17], (N_MLA_LAYERS, QK_HEAD)),
        'k_norm': gain(ks[18], (N_MLA_LAYERS, QK_HEAD)),
        'gmlp_v_norm': gain(ks[19], (N_MLA_LAYERS, GMLP_WIDTH)),
        'gmlp_ws': nrm(ks[20], (N_MLA_LAYERS, GMLP_GROUPS, GMLP_CHUNK, GMLP_CHUNK), GMLP_CHUNK ** -0.5),
        'gmlp_b': nrm(ks[21], (N_MLA_LAYERS, GMLP_GROUPS, GMLP_CHUNK), 0.02),
        'w_out': nrm(ks[22], (N_MLA_LAYERS, MIX_WIDTH, D_MODEL), MIX_WIDTH ** -0.5),
        'pool_w': nrm(ks[23], (N_POOL_LAYERS, len(POOL_WINDOWS), POOL_GROUP_CH, POOL_GROUP_CH), POOL_GROUP_CH ** -0.5),
        'pool_scale': gain(ks[24], (N_POOL_LAYERS, D_MODEL)),
    }


def reference(x_prompt, x_sample, cache_ckv, cache_krope, c, c_ctx, w_mod, b_mod, norm_g,
              ffn_w1, ffn_w3, ffn_w2, w_in, q_a_norm, kv_a_norm, w_qb, w_kvb, q_norm, k_norm,
              gmlp_v_norm, gmlp_ws, gmlp_b, w_out, pool_w, pool_scale):
    weights = (w_mod, b_mod, norm_g, ffn_w1, ffn_w3, ffn_w2, w_in, q_a_norm, kv_a_norm, w_qb,
               w_kvb, q_norm, k_norm, gmlp_v_norm, gmlp_ws, gmlp_b, w_out, pool_w, pool_scale)
    y_prompt, ckv_list, krope_list = trunk(x_prompt, c_ctx[None, :], None, None, None, *weights)
    new_ckv = jnp.stack(ckv_list, axis=1)
    new_krope = jnp.stack(krope_list, axis=1)
    ROWS = x_sample.shape[1] // GRID_W
    rope = axial_rope_tables(ROWS)
    y_sample, _, _ = trunk(x_sample, c, rope, cache_ckv, cache_krope, *weights)
    return (y_prompt, y_sample, new_ckv, new_krope)
```

```python
import numpy as np
from contextlib import ExitStack
import concourse.bass as bass
import concourse.mybir as mybir
from concourse.bass_utils import run_bass_kernel_spmd

F32 = mybir.dt.float32
BF16 = mybir.dt.bfloat16
AF = mybir.ActivationFunctionType
ALU = mybir.AluOpType
AX = mybir.AxisListType

D = 1024
NSUB = 40
NTOK = NSUB * 128
FH = 2816
NF = 22
EPS = 1e-6
WINS = (2, 4, 8, 16)
SAME_SYNC = True
import os as _os
HAM_FILL = bool(int(_os.environ.get('HAM_FILL', '0')))
HAM_N = int(_os.environ.get('HAM_N', '256'))


class Buf:
    __slots__ = ("name", "w", "r", "ld", "st")

    def __init__(self, name):
        self.name = name
        self.w = {}
        self.r = {}
        self.ld = None
        self.st = None


class Prog:
    def __init__(self, nc, es):
        self.nc = nc
        self.es = es
        self.eng = {"pe": nc.tensor, "act": nc.scalar, "dve": nc.vector, "pool": nc.gpsimd, "sp": nc.sync}
        self.semobj = {}
        self.cnt = {}
        for e in ("pe", "act", "dve", "pool"):
            self.semobj[e] = es.enter_context(nc.semaphore("s_" + e))
            self.cnt[e] = 0
        self.waited = {e: {} for e in self.eng}
        self.free_d = {"sp": [], "pool": []}
        self.dq = {}
        self.nd = 0

    def _newd(self, q):
        if self.free_d[q]:
            return self.free_d[q].pop()
        k = ("d", self.nd)
        self.dq[k] = q
        self.nd += 1
        self.semobj[k] = self.es.enter_context(self.nc.semaphore("s_d%d" % k[1]))
        self.cnt[k] = 0
        return k

    def release(self, bufs):
        for b in bufs:
            if b.ld is not None:
                self.free_d[self.dq[b.ld]].append(b.ld)
                b.ld = None
            if b.st is not None:
                self.free_d[self.dq[b.st]].append(b.st)
                b.st = None

    def _deps(self, R, W):
        d = {}
        for b in R:
            for k, v in b.w.items():
                if d.get(k, 0) < v:
                    d[k] = v
        for b in W:
            for k, v in b.w.items():
                if d.get(k, 0) < v:
                    d[k] = v
            for k, v in b.r.items():
                if d.get(k, 0) < v:
                    d[k] = v
        return d

    def _waits(self, e, deps):
        wd = self.waited[e]
        for k, v in deps.items():
            if k == e and (e == "pe" or not SAME_SYNC):
                continue
            if wd.get(k, 0) >= v:
                continue
            self.eng[e].wait_ge(self.semobj[k], v)
            wd[k] = v

    def op(self, e, fn, R=(), W=()):
        self._waits(e, self._deps(R, W))
        ins = fn()
        self.cnt[e] += 1
        ins.then_inc(self.semobj[e], 1)
        v = self.cnt[e]
        for b in R:
            b.r[e] = v
        for b in W:
            b.w[e] = v

    def dma(self, q, out, in_, R=(), W=(), owner=None, kind="ld", **kw):
        self._waits(q, self._deps(R, W))
        if kind == "ld":
            if owner.ld is None:
                owner.ld = self._newd(q)
            k = owner.ld
        else:
            if owner.st is None:
                owner.st = self._newd(q)
            k = owner.st
        assert self.dq[k] == q, (owner.name, kind, q)
        ins = self.eng[q].dma_start(out=out, in_=in_, **kw)
        self.cnt[k] += 16
        ins.then_inc(self.semobj[k], 16)
        v = self.cnt[k]
        for b in R:
            b.r[k] = v
        for b in W:
            b.w[k] = v

    def barrier(self):
        for e in self.eng:
            wd = self.waited[e]
            for k, v in self.cnt.items():
                if v > 0 and wd.get(k, 0) < v:
                    self.eng[e].wait_ge(self.semobj[k], v)
                    wd[k] = v


_UID = [0]


def uniq(name):
    _UID[0] += 1
    return "%s_u%d" % (name, _UID[0])


def cv_of(s):
    return 0 if s < 8 else 1


def build_nc():
    nc = bass.Bass("TRN2", target_bir_lowering=False)

    def din(name, shape):
        return nc.dram_tensor(name, list(shape), F32, kind="ExternalInput").ap()

    xin = din("xin", [NTOK, D])
    cckv = din("cckv", [2, 512, 128])
    ckro = din("ckro", [2, 512, 32])
    cvec = din("cvec", [2, D])
    w_mod = din("w_mod", [4, D, 9 * D])
    b_mod = din("b_mod", [4, 9 * D])
    norm_g = din("norm_g", [4, 3, D])
    ffn_w1 = din("ffn_w1", [4, 2, D, FH])
    ffn_w3 = din("ffn_w3", [4, 2, D, FH])
    ffn_w2 = din("ffn_w2", [4, 2, FH, D])
    w_in = din("w_in", [2, D, 1440])
    q_a_norm = din("q_a_norm", [2, 256])
    kv_a_norm = din("kv_a_norm", [2, 128])
    w_qb = din("w_qb", [2, 256, 768])
    w_kvb = din("w_kvb", [2, 128, 1024])
    q_norm = din("q_norm", [2, 96])
    k_norm = din("k_norm", [2, 96])
    gmlp_v_norm = din("gmlp_v_norm", [2, 512])
    gmlp_ws = din("gmlp_ws", [2, 4, 128, 128])
    gmlp_b = din("gmlp_b", [2, 4, 128])
    w_out = din("w_out", [2, D, D])
    pool_w = din("pool_w", [2, 4, 256, 256])
    pool_scale = din("pool_scale", [2, D])
    ropec = din("ropec", [4096, 16])
    ropes = din("ropes", [4096, 16])
    band = din("band", [128, 4, 5, 128])

    y = nc.dram_tensor("y", [NTOK, D], F32, kind="ExternalOutput").ap()
    ckv_o = nc.dram_tensor("ckv_o", [4, 2, 256, 128], F32, kind="ExternalOutput").ap()
    kro_o = nc.dram_tensor("kro_o", [4, 2, 256, 32], F32, kind="ExternalOutput").ap()
    xa = nc.dram_tensor("xa", [NTOK, D], F32, kind="Internal").ap()
    xb = nc.dram_tensor("xb", [NTOK, D], F32, kind="Internal").ap()
    gsc = nc.dram_tensor("gsc", [4, 3, 2, D], F32, kind="Internal").ap()
    klT_d = nc.dram_tensor("klT_d", [44, 128, 128], BF16, kind="Internal").ap()
    krf_d = nc.dram_tensor("krf_d", [44, 128, 32], F32, kind="Internal").ap()
    qlT_d = nc.dram_tensor("qlT_d", [40, 128, 256], BF16, kind="Internal").ap()

    with ExitStack() as es:
        P = Prog(nc, es)

        def sbp(name, shape, dt):
            return es.enter_context(nc.sbuf_tensor(name, list(shape), dt))

        ident = sbp("ident", [128, 128], BF16)
        identf = sbp("identf", [128, 128], F32)
        ones = sbp("ones", [128, 128], F32)
        AS = sbp("AS", [128, 12, 2, 8, 2], F32)
        psum_all = es.enter_context(nc.psum_tensor("psum_all", [128, 4096], F32))
        banks = [psum_all[:, i * 512:(i + 1) * 512] for i in range(8)]
        bb = [Buf("bank%d" % i) for i in range(8)]
        b_ident = Buf("ident")
        b_AS = Buf("AS")
        b_ones = Buf("ones")
        xbufs = {}
        for nm in ("xin", "xa", "xb", "y"):
            xbufs[nm] = [Buf("%s_%d" % (nm, s)) for s in range(NSUB)]
        xaps = {"xin": xin, "xa": xa, "xb": xb, "y": y}
        b_gsc = Buf("gsc")

        P.op("pool", lambda: nc.gpsimd.memset(identf[:], 0.0), W=[b_ident])
        P.op("pool", lambda: nc.gpsimd.affine_select(out=identf[:], in_=identf[:], pattern=[[-1, 128]],
                                                      compare_op=ALU.not_equal, fill=1.0, base=0,
                                                      channel_multiplier=1), R=[b_ident], W=[b_ident])
        P.op("dve", lambda: nc.vector.tensor_copy(ident[:], identf[:]), R=[b_ident], W=[b_ident])
        P.op("dve", lambda: nc.vector.memset(ones[:], 1.0), W=[b_ones])

        def prologue():
            with ExitStack() as st:
                def sb(name, shape, dt):
                    return st.enter_context(nc.sbuf_tensor(uniq(name), list(shape), dt))
                cT = sb("cT", [128, 2, 8], F32)
                sT = sb("sT", [128, 2, 8], F32)
                sTb = sb("sTb", [128, 2, 8], BF16)
                sbc = sb("sbc", [128, 16, 128], BF16)
                wblk = [sb("wblk%d" % i, [128, 8, 1024], BF16) for i in range(3)]
                brow = [sb("brow%d" % i, [1, 1024], F32) for i in range(2)]
                grow = [sb("grow%d" % i, [128, 1024], F32) for i in range(2)]
                psc = sb("psc", [128, 1024], F32)
                b_cT = Buf("cT"); b_sT = Buf("sT"); b_sbc = Buf("sbc")
                b_w = [Buf("wblk0"), Buf("wblk1"), Buf("wblk2")]
                b_br = [Buf("brow0"), Buf("brow1")]
                b_gr = [Buf("grow0"), Buf("grow1")]
                b_psc = Buf("psc")
                allb = [b_cT, b_sT, b_sbc, b_psc] + b_w + b_br + b_gr
                with nc.allow_non_contiguous_dma(reason="tiny transposed cvec load"):
                    for c in range(2):
                        P.dma("sp", cT[:, c, :], cvec[c, :].rearrange("(k p) -> p k", p=128), W=[b_cT], owner=b_cT)
                P.op("act", lambda: nc.scalar.activation(out=sT[:], in_=cT[:], func=AF.Silu), R=[b_cT], W=[b_sT])
                P.op("dve", lambda: nc.vector.tensor_copy(
                    sbc[:], sT[:].rearrange("p c k -> p (c k)").unsqueeze(2).to_broadcast([128, 16, 128])),
                    R=[b_sT], W=[b_sbc])
                P.op("dve", lambda: nc.vector.tensor_copy(sTb[:], sT[:]), R=[b_sT], W=[b_sbc])
                n = 0
                for i in range(4):
                    for k in range(3):
                        for v in range(3):
                            blk = 3 * k + v
                            wi = n % 3
                            bi = n % 2
                            n += 1
                            c0 = blk * 1024
                            P.dma("pool", wblk[wi][:], w_mod[i, :, c0:c0 + 1024].rearrange("(k p) n -> p k n", p=128),
                                  W=[b_w[wi]], owner=b_w[wi])
                            P.dma("sp", brow[bi][:], b_mod[i:i + 1, c0:c0 + 1024], W=[b_br[bi]], owner=b_br[bi])
                            if v < 2:
                                pm = banks[0][:, 0:16].rearrange("p (c v) -> p c v", v=2)
                                for ch in range(8):
                                    for kc in range(8):
                                        P.op("pe", lambda ch=ch, kc=kc: nc.tensor.matmul(
                                            pm[:, ch, :], lhsT=wblk[wi][:, kc, ch * 128:(ch + 1) * 128],
                                            rhs=sTb[:, :, kc], start=(kc == 0), stop=False),
                                            R=[b_w[wi], b_sbc], W=[bb[0]])
                                    P.op("pe", lambda ch=ch: nc.tensor.matmul(
                                        pm[:, ch, :], lhsT=brow[bi][0:1, ch * 128:(ch + 1) * 128],
                                        rhs=ones[0:1, 0:2], start=False, stop=True),
                                        R=[b_br[bi], b_ones], W=[bb[0]])
                                dst = AS[:, i * 3 + k, 1 - v, :, :]
                                if v == 0:
                                    P.op("dve", lambda dst=dst: nc.vector.tensor_copy(dst, pm), R=[bb[0]], W=[b_AS])
                                else:
                                    P.op("dve", lambda dst=dst: nc.vector.tensor_scalar(
                                        out=dst, in0=pm, scalar1=1.0, scalar2=None, op0=ALU.add), R=[bb[0]], W=[b_AS])
                            else:
                                fac = 1.0 if k == 1 else 0.5
                                for cvi in range(2):
                                    gi = cvi
                                    for half in range(2):
                                        bk = 1 + half
                                        for kc in range(8):
                                            P.op("pe", lambda kc=kc, half=half, bk=bk: nc.tensor.matmul(
                                                banks[bk][:], lhsT=sbc[:, cvi * 8 + kc, :],
                                                rhs=wblk[wi][:, kc, half * 512:(half + 1) * 512],
                                                start=(kc == 0), stop=False), R=[b_w[wi], b_sbc], W=[bb[bk]])
                                        P.op("pe", lambda half=half, bk=bk: nc.tensor.matmul(
                                            banks[bk][:], lhsT=ones[0:1, 0:128],
                                            rhs=brow[bi][0:1, half * 512:(half + 1) * 512],
                                            start=False, stop=True), R=[b_br[bi], b_ones], W=[bb[bk]])
                                        P.op("act", lambda half=half, bk=bk: nc.scalar.activation(
                                            out=grow[gi][:, half * 512:(half + 1) * 512], in_=banks[bk][:],
                                            func=AF.Copy, scale=fac), R=[bb[bk]], W=[b_gr[gi]])
                                    if k == 1 and i % 2 == 1:
                                        P.dma("sp", psc[:], pool_scale[i // 2:i // 2 + 1, :].partition_broadcast(128),
                                              W=[b_psc], owner=b_psc)
                                        P.op("dve", lambda: nc.vector.tensor_tensor(
                                            out=grow[gi][:], in0=grow[gi][:], in1=psc[:], op=ALU.mult),
                                            R=[b_gr[gi], b_psc], W=[b_gr[gi]])
                                    P.dma("sp", gsc[i, k, cvi:cvi + 1, :], grow[gi][0:1, :], R=[b_gr[gi]], W=[b_gsc],
                                          owner=b_gr[gi], kind="st")
                P.barrier()
                P.release(allb)

        def front_load(src, s, xt, b_xt):
            P.dma("sp", xt[:], xaps[src][s * 128:(s + 1) * 128, :], R=[xbufs[src][s]], W=[b_xt], owner=b_xt)

        def front_compute(xt, b_xt, xn, b_xn, gn, b_gn, st_, b_st, junk, b_junk):
            P.op("act", lambda: nc.scalar.activation(out=junk[:], in_=xt[:], func=AF.Square, accum_out=st_[:, 0:1]),
                 R=[b_xt], W=[b_junk, b_st])
            rstd(st_, b_st, 1, 1.0 / D)
            P.op("dve", lambda: nc.vector.scalar_tensor_tensor(out=xn[:], in0=xt[:], scalar=st_[:, 2:3], in1=gn[:],
                                                              op0=ALU.mult, op1=ALU.mult),
                 R=[b_xt, b_st, b_gn], W=[b_xn])

        def front(src, s, xt, b_xt, xn, b_xn, gn, b_gn, st_, b_st, junk, b_junk):
            front_load(src, s, xt, b_xt)
            front_compute(xt, b_xt, xn, b_xn, gn, b_gn, st_, b_st, junk, b_junk)

        def rstd(st_, b_st, n, inv):
            P.op("dve", lambda: nc.vector.tensor_scalar(out=st_[:, n:2 * n], in0=st_[:, 0:n], scalar1=inv, scalar2=EPS,
                                                        op0=ALU.mult, op1=ALU.add), R=[b_st], W=[b_st])
            P.op("act", lambda: nc.scalar.activation(out=st_[:, n:2 * n], in_=st_[:, n:2 * n], func=AF.Sqrt),
                 R=[b_st], W=[b_st])
            P.op("dve", lambda: nc.vector.reciprocal(out=st_[:, 2 * n:3 * n], in_=st_[:, n:2 * n]), R=[b_st], W=[b_st])

        def transpose_mod(xn, b_xn, hT_dst, b_hT, bank_i, ik):
            pT = banks[bank_i][:].bitcast(BF16).rearrange("p (k t) -> p k t", k=8)
            for kc in range(8):
                P.op("pe", lambda kc=kc: nc.tensor.transpose(out=pT[:, kc, :], in_=xn[:, kc * 128:(kc + 1) * 128],
                                                             identity=ident[:]), R=[b_xn, b_ident], W=[bb[bank_i]])
            return pT

        def residual(src, dst, s, cvi, psum_halves, G, b_G, xr, b_xr, tmp, b_tmp):
            P.dma("sp", xr[:], xaps[src][s * 128:(s + 1) * 128, :], R=[xbufs[src][s]], W=[b_xr], owner=b_xr)
            for (bk, half) in psum_halves:
                sl = slice(half * 512, (half + 1) * 512)
                P.op("dve", lambda bk=bk, sl=sl, half=half: nc.vector.tensor_tensor(
                    out=tmp[half][:], in0=banks[bk][:], in1=G[cvi][:, sl], op=ALU.mult),
                    R=[bb[bk], b_G], W=[b_tmp[half]])
                P.op("pool", lambda sl=sl, half=half: nc.gpsimd.tensor_tensor(
                    out=xr[:, sl], in0=tmp[half][:], in1=xr[:, sl], op=ALU.add),
                    R=[b_tmp[half], b_xr], W=[b_xr])
            P.dma("sp", xaps[dst][s * 128:(s + 1) * 128, :], xr[:], R=[b_xr], W=[xbufs[dst][s]], owner=b_xr, kind="st")

        def ffn_stage(i, which, src, dst):
            k = 0 if which == 0 else 2
            ik = i * 3 + k
            with ExitStack() as st:
                def sb(name, shape, dt):
                    return st.enter_context(nc.sbuf_tensor(uniq(name), list(shape), dt))
                w1 = sb("w1", [128, 8, FH], BF16)
                w3 = sb("w3", [128, 8, FH], BF16)
                w2 = sb("w2", [128, NF, D], BF16)
                G = [sb("G%d" % c, [128, D], F32) for c in range(2)]
                gn = sb("gn", [128, D], F32)
                xt = [sb("xt%d" % c, [128, D], F32) for c in range(2)]
                xr = [sb("xr%d" % c, [128, D], F32) for c in range(2)]
                xn = [sb("xn%d" % c, [128, D], BF16) for c in range(4)]
                sts = [sb("st%d" % c, [128, 4], F32) for c in range(2)]
                hT = sb("hT", [128, 8, 512], BF16)
                gT = sb("gT", [128, NF, 512], BF16)
                sa = [sb("sa%d" % c, [128, 512], F32) for c in range(2)]
                tmp0 = sb("tmp0", [128, 512], F32)
                tmp = [tmp0, tmp0]
                b_w1b = [Buf("w1_%d" % c) for c in range(4)]; b_w3b = [Buf("w3_%d" % c) for c in range(4)]
                b_w2 = Buf("w2"); b_G = Buf("G"); b_gn = Buf("gn")

                def wblks(bl, j):
                    return [bl[b] for b in sorted({(j * 128) // 704, (j * 128 + 127) // 704})]
                b_xt = [Buf("xt0"), Buf("xt1")]; b_xr = [Buf("xr0"), Buf("xr1")]; b_xn = [Buf("xn%d" % c) for c in range(4)]
                b_st = [Buf("st0"), Buf("st1")]
                b_hT = [Buf("hT%d" % c) for c in range(4)]
                b_gT = [Buf("gT%d" % c) for c in range(NF)]
                b_t0 = Buf("tmp0")
                b_sa = [Buf("sa0"), Buf("sa1")]; b_tmp = [b_t0, b_t0]
                allb = b_w1b + b_w3b + [b_w2, b_G, b_gn] + b_xt + b_xr + b_xn + b_st + b_hT + b_gT + b_sa + b_tmp
                w1v = ffn_w1[i, which].rearrange("(k p) n -> p k n", p=128)
                w3v = ffn_w3[i, which].rearrange("(k p) n -> p k n", p=128)
                w2v = ffn_w2[i, which].rearrange("(j p) n -> p j n", p=128)
                for blk in range(4):
                    cs = slice(blk * 704, (blk + 1) * 704)
                    P.dma("pool", w1[:, :, cs], w1v[:, :, cs], W=[b_w1b[blk]], owner=b_w1b[blk])
                    P.dma("pool", w3[:, :, cs], w3v[:, :, cs], W=[b_w3b[blk]], owner=b_w3b[blk])
                for jb in range(2):
                    js = slice(jb * 11, (jb + 1) * 11)
                    P.dma("pool", w2[:, js, :], w2v[:, js, :], W=[b_w2], owner=b_w2)
                for c in range(2):
                    P.dma("sp", G[c][:], gsc[i, k, c:c + 1, :].partition_broadcast(128), R=[b_gsc], W=[b_G], owner=b_G)
                P.dma("sp", gn[:], norm_g[i, k:k + 1, :].partition_broadcast(128), W=[b_gn], owner=b_gn)
                fc = [0]
                nr = [0]
                NT = NSUB // 4

                def do_front(t):
                    for q in range(4):
                        c = fc[0] % 2
                        fc[0] += 1
                        front(src, 4 * t + q, xt[c], b_xt[c], xn[q], b_xn[q], gn, b_gn, sts[c], b_st[c], xn[q], b_xn[q])

                def do_front_load(t, q):
                    front_load(src, 4 * t + q, xt[q % 2], b_xt[q % 2])

                def do_front_compute(t, q):
                    c = q % 2
                    front_compute(xt[c], b_xt[c], xn[q], b_xn[q], gn, b_gn, sts[c], b_st[c], xn[q], b_xn[q])

                def do_tr(t, q):
                    cvi = cv_of(4 * t)
                    pT = transpose_mod(xn[q], b_xn[q], None, None, 0, ik)
                    for kc in range(8):
                        P.op("act", lambda kc=kc, q=q: nc.scalar.activation(
                            out=hT[:, kc, q * 128:(q + 1) * 128], in_=pT[:, kc, :], func=AF.Identity,
                            scale=AS[:, ik, 0, kc, cvi:cvi + 1], bias=AS[:, ik, 1, kc, cvi:cvi + 1]),
                            R=[bb[0], b_AS], W=[b_hT[q]])

                do_front(0)
                for q in range(4):
                    do_tr(0, q)
                for t in range(NT):
                    cvi = cv_of(4 * t)
                    for j in range(NF):
                        if t + 1 < NT and j < 16:
                            if j % 4 == 0:
                                do_front_load(t + 1, j // 4)
                            elif j % 4 == 3:
                                do_front_compute(t + 1, j // 4)
                        pa = 1 + 2 * (j % 2)
                        pb = pa + 1
                        for kc in range(8):
                            P.op("pe", lambda kc=kc, j=j, pa=pa: nc.tensor.matmul(
                                banks[pa][:], lhsT=w1[:, kc, j * 128:(j + 1) * 128], rhs=hT[:, kc, :],
                                start=(kc == 0), stop=(kc == 7)), R=wblks(b_w1b, j) + b_hT, W=[bb[pa]])
                        for kc in range(8):
                            P.op("pe", lambda kc=kc, j=j, pb=pb: nc.tensor.matmul(
                                banks[pb][:], lhsT=w3[:, kc, j * 128:(j + 1) * 128], rhs=hT[:, kc, :],
                                start=(kc == 0), stop=(kc == 7)), R=wblks(b_w3b, j) + b_hT, W=[bb[pb]])
                        sc = j % 2
                        P.op("act", lambda pa=pa, sc=sc: nc.scalar.activation(out=sa[sc][:], in_=banks[pa][:], func=AF.Silu),
                             R=[bb[pa]], W=[b_sa[sc]])
                        P.op("dve", lambda pb=pb, sc=sc, j=j: nc.vector.tensor_tensor(
                            out=gT[:, j, :], in0=banks[pb][:], in1=sa[sc][:], op=ALU.mult),
                            R=[bb[pb], b_sa[sc]], W=[b_gT[j]])
                    for q in range(4):
                        s = 4 * t + q
                        c = nr[0] % 2
                        nr[0] += 1
                        if t + 1 < NT:
                            do_tr(t + 1, q)
                        halves = []
                        for half in range(2):
                            bk = 5 + half
                            for j in range(NF):
                                P.op("pe", lambda j=j, q=q, half=half, bk=bk: nc.tensor.matmul(
                                    banks[bk][:], lhsT=gT[:, j, q * 128:(q + 1) * 128],
                                    rhs=w2[:, j, half * 512:(half + 1) * 512], start=(j == 0), stop=(j == NF - 1)),
                                    R=[b_gT[j], b_w2], W=[bb[bk]])
                            halves.append((bk, half))
                        residual(src, dst, s, cvi, halves, G, b_G, xr[c], b_xr[c], tmp, b_tmp)
                P.barrier()
                P.release(allb)

        def pool_stage(i, src, dst):
            j = i // 2
            ik = i * 3 + 1
            with ExitStack() as st:
                def sb(name, shape, dt):
                    return st.enter_context(nc.sbuf_tensor(uniq(name), list(shape), dt))
                bandb = sb("bandb", [128, 4, 5, 128], BF16)
                pw = sb("pw", [128, 4, 2, 256], BF16)
                G = [sb("G%d" % c, [128, D], F32) for c in range(2)]
                gn = sb("gn", [128, D], F32)
                xt = [sb("xt%d" % c, [128, D], F32) for c in range(2)]
                xr = [sb("xr%d" % c, [128, D], F32) for c in range(2)]
                xn = sb("xnall", [128, 32, D], BF16)
                junk = sb("junk", [128, D], BF16)
                sts = [sb("st%d" % c, [128, 4], F32) for c in range(2)]
                dT = [sb("dT%d" % c, [128, 8, 128], BF16) for c in range(2)]
                tmp = [sb("tmp%d" % c, [128, 512], F32) for c in range(2)]
                b_band = Buf("band"); b_pw = Buf("pw"); b_G = Buf("G"); b_gn = Buf("gn")
                b_xt = [Buf("xt0"), Buf("xt1")]; b_xr = [Buf("xr0"), Buf("xr1")]
                b_xn = [Buf("xn%d" % c) for c in range(32)]
                b_junk = Buf("junk"); b_st = [Buf("st0"), Buf("st1")]
                b_dT = [Buf("dT0"), Buf("dT1")]; b_tmp = [Buf("tmp0"), Buf("tmp1")]
                allb = [b_band, b_pw, b_G, b_gn, b_junk] + b_xt + b_xr + b_xn + b_st + b_dT + b_tmp
                P.dma("pool", bandb[:].rearrange("p a b c -> p (a b c)"), band.rearrange("p a b c -> p (a b c)"),
                      W=[b_band], owner=b_band)
                for g in range(4):
                    P.dma("pool", pw[:, g, :, :], pool_w[j, g].rearrange("(k p) n -> p k n", p=128), W=[b_pw], owner=b_pw)
                for c in range(2):
                    P.dma("sp", G[c][:], gsc[i, 1, c:c + 1, :].partition_broadcast(128), R=[b_gsc], W=[b_G], owner=b_G)
                P.dma("sp", gn[:], norm_g[i, 1:2, :].partition_broadcast(128), W=[b_gn], owner=b_gn)
                cnt = [0, 0]
                seqs = [[2 * q, 2 * q + 1] for q in range(4)] + [list(range(8, 40))]
                for subs in seqs:
                    n = len(subs)
                    cvi = cv_of(subs[0])

                    def fr(idx):
                        c = cnt[0] % 2
                        cnt[0] += 1
                        front(src, subs[idx], xt[c], b_xt[c], xn[:, idx, :], b_xn[idx], gn, b_gn, sts[c], b_st[c], junk, b_junk)
                    def g_sub(idx, c):
                        first = idx == 0
                        last = idx == n - 1
                        bo = 4 * c
                        for cc in range(8):
                            g = cc // 2
                            bk = bo + cc // 4
                            outp = banks[bk][:, (cc % 4) * 128:(cc % 4 + 1) * 128]
                            nbs = []
                            if not first:
                                nbs.append((idx - 1, 3))
                            nbs.append((idx, 1 if first else (2 if last else 0)))
                            if not last:
                                nbs.append((idx + 1, 4))
                            for m, (ni, var) in enumerate(nbs):
                                P.op("pe", lambda ni=ni, var=var, cc=cc, g=g, outp=outp, m=m, L=len(nbs): nc.tensor.matmul(
                                    outp, lhsT=xn[:, ni, cc * 128:(cc + 1) * 128], rhs=bandb[:, g, var, :],
                                    start=(m == 0), stop=(m == L - 1)), R=[b_xn[ni], b_band], W=[bb[bk]])
                            if cc % 4 == 3:
                                yield
                        for cc in range(8):
                            bk = bo + cc // 4
                            P.op("act", lambda cc=cc, bk=bk, c=c: nc.scalar.activation(
                                out=dT[c][:, cc, :], in_=banks[bk][:, (cc % 4) * 128:(cc % 4 + 1) * 128], func=AF.Copy,
                                scale=AS[:, ik, 0, cc, cvi:cvi + 1]), R=[bb[bk], b_AS], W=[b_dT[c]])
                            if cc % 2 == 1:
                                yield
                        halves = []
                        for half in range(2):
                            bk = bo + 2 + half
                            for gg in range(2):
                                g = half * 2 + gg
                                for kk in range(2):
                                    P.op("pe", lambda g=g, gg=gg, kk=kk, bk=bk, c=c: nc.tensor.matmul(
                                        banks[bk][:, gg * 256:(gg + 1) * 256], lhsT=dT[c][:, 2 * g + kk, :],
                                        rhs=pw[:, g, kk, :], start=(kk == 0), stop=(kk == 1)),
                                        R=[b_dT[c], b_pw], W=[bb[bk]])
                            halves.append((bk, half))
                        yield
                        residual(src, dst, subs[idx], cvi, halves, G, b_G, xr[c], b_xr[c], tmp, b_tmp)
                        yield

                    for k0 in range(min(4, n)):
                        fr(k0)
                    for idx in range(0, n, 2):
                        for k0 in (idx + 4, idx + 5):
                            if k0 < n:
                                fr(k0)
                        gens = [g_sub(idx, 0)]
                        if idx + 1 < n:
                            gens.append(g_sub(idx + 1, 1))
                        while gens:
                            for gg_ in list(gens):
                                try:
                                    next(gg_)
                                except StopIteration:
                                    gens.remove(gg_)
                P.barrier()
                P.release(allb)

        def interleave(gens):
            gens = list(gens)
            while gens:
                for g in list(gens):
                    try:
                        next(g)
                    except StopIteration:
                        gens.remove(g)

        def mla_stage(i, src, dst):
            j = i // 2
            ik = i * 3 + 1
            with ExitStack() as st:
                def sb(name, shape, dt):
                    return st.enter_context(nc.sbuf_tensor(uniq(name), list(shape), dt))
                win = sb("win", [128, 8, 1440], BF16)
                wqb = sb("wqb", [128, 2, 768], BF16)
                wkvb = sb("wkvb", [128, 1024], BF16)
                woA = sb("woA", [64, 8, D], BF16)
                woG = sb("woG", [128, 4, D], BF16)
                wsf = sb("wsf", [128, 4, 128], BF16)
                wsT = sb("wsT", [128, 4, 128], BF16)
                qan = sb("qan", [128, 256], F32)
                kvan = sb("kvan", [128, 128], F32)
                qnr = sb("qnr", [128, 96], F32)
                knr = sb("knr", [128, 96], F32)
                vnr = sb("vnr", [128, 512], F32)
                gb = sb("gb", [128, 4], F32)
                cst = sb("cst", [128, 8], F32)
                rC = sb("rC", [128, 32, 16], F32)
                rS = sb("rS", [128, 32, 16], F32)
                Gt = sb("Gt", [128, D], F32)
                G = [Gt, Gt]
                gn = sb("gn", [128, D], F32)
                KT = sb("KT", [96, 4, 4608], BF16)
                V = sb("V", [128, 36, 4, 65], BF16)
                xr = sb("xr", [128, D], F32)
                QT = sb("QT", [96, 4, 512], BF16)
                goutT = sb("goutT", [128, 4, 512], BF16)
                attnT = sb("attnT", [64, 4, 512], BF16)
                pT = [sb("pT%d" % c, [128, 1024], BF16) for c in range(2)]
                tmp0 = sb("tmp0", [128, 512], F32)
                tmp = [tmp0, tmp0]
                rr = tmp0
                ob = xr[0:64, 0:512]
                names = ["win", "wqb", "wkvb", "woA", "woG", "wsf", "wsT", "rows", "gb", "cst", "rope", "G", "gn", "KT", "V",
                         "xr", "QT", "goutT", "attnT", "pT0", "pT1", "pT2", "ob", "rr", "tmp0"]
                B = {nm: Buf(nm) for nm in names}
                b_tmp = [B["tmp0"], B["tmp0"]]
                b_pT = [B["pT0"], B["pT1"]]
                B["rr"] = B["tmp0"]
                B["ob"] = B["xr"]

                class WS:
                    pass
                wss = []
                for wi in range(2):
                    w = WS()
                    w.xt = sb("xt", [128, D], F32)
                    w.xn = sb("xn", [128, D], BF16)
                    w.sts = sb("st", [128, 4], F32)
                    w.stg = sb("stg", [128, 4], F32)
                    w.st4 = sb("st4", [128, 12], F32)
                    w.hTs = sb("hTs", [128, 8, 128], BF16)
                    w.kvf = sb("kvf", [128, 128], F32)
                    w.krf = sb("krf", [128, 32], F32)
                    w.kvb = sb("kvb", [128, 256], BF16)
                    w.lT = sb("lT", [128, 2, 128], BF16)
                    w.kf = sb("kf", [128, 4, 96], F32)
                    w.kb = sb("kb", [128, 4, 96], BF16)
                    w.rt = [sb("rt%d" % c, [128, 4, 2, 8], F32) for c in range(4)]
                    w.ug = sb("ug", [128, 512], F32)
                    w.g1 = sb("g1", [128, 512], F32)
                    w.g2 = sb("g2", [128, 512], F32)
                    w.vg = w.g1
                    w.junk = sb("junk", [128, 384], F32)
                    w.vnb = sb("vnb", [128, 512], BF16)
                    w.gout = sb("gout", [128, 512], BF16)
                    w.B = {nm: Buf(nm + str(wi)) for nm in ("xt", "xn", "st", "st4", "hTs", "kvf", "krf", "kvb", "lT",
                                                            "kf", "kb", "rt", "ug", "g1", "g2", "vnb", "gout")}
                    w.B["vg"] = w.B["g1"]
                    w.B["junk"] = Buf("junk%d" % wi)
                    w.B["stg"] = Buf("stg%d" % wi)
                    wss.append(w)
                for wi in range(2, 4):
                    w = WS()
                    w.junk = sb("junk", [128, 384], F32)
                    w.st4 = sb("st4", [128, 12], F32)
                    w.krf = sb("krf", [128, 32], F32)
                    w.lT = sb("lT", [128, 2, 128], BF16)
                    w.kf = sb("kf", [128, 4, 96], F32)
                    w.kb = sb("kb", [128, 4, 96], BF16)
                    w.rt = [sb("rt%d" % c, [128, 4, 2, 8], F32) for c in range(4)]
                    w.B = {nm: Buf(nm + str(wi)) for nm in ("junk", "st4", "krf", "lT", "kf", "kb", "rt")}
                    wss.append(w)
                kd_bufs = [Buf("kd%d" % c) for c in range(44)]
                qd_bufs = [Buf("qd%d" % c) for c in range(40)]

                def set_banks(pass_no):
                    for wi, w in enumerate(wss):
                        if pass_no == 0:
                            if wi < 2:
                                w.bT, w.bP, w.bU, w.bQ = (0, 1, 2, 3) if wi == 0 else (4, 5, 6, 7)
                            else:
                                continue
                        else:
                            w.bT, w.bQ = 2 * wi, 2 * wi + 1
                        w.pTb = banks[w.bT][:].bitcast(BF16).rearrange("p (k t) -> p k t", k=8)
                set_banks(0)
                for kc in range(8):
                    P.dma("pool", win[:, kc, :], w_in[j, kc * 128:(kc + 1) * 128, :], W=[B["win"]], owner=B["win"])
                P.dma("pool", wqb[:], w_qb[j].rearrange("(k p) n -> p k n", p=128), W=[B["wqb"]], owner=B["wqb"])
                P.dma("pool", wkvb[:], w_kvb[j], W=[B["wkvb"]], owner=B["wkvb"])
                P.dma("pool", woA[:], w_out[j, 0:512, :].rearrange("(h p) n -> p h n", p=64), W=[B["woA"]], owner=B["woA"])
                P.dma("pool", woG[:], w_out[j, 512:1024, :].rearrange("(k p) n -> p k n", p=128), W=[B["woG"]], owner=B["woG"])
                P.dma("pool", wsf[:], gmlp_ws[j].rearrange("g p q -> p g q"), W=[B["wsf"]], owner=B["wsf"])
                P.dma("sp", qan[:], q_a_norm[j:j + 1, :].partition_broadcast(128), W=[B["rows"]], owner=B["rows"])
                P.dma("sp", kvan[:], kv_a_norm[j:j + 1, :].partition_broadcast(128), W=[B["rows"]], owner=B["rows"])
                P.dma("sp", qnr[:], q_norm[j:j + 1, :].partition_broadcast(128), W=[B["rows"]], owner=B["rows"])
                P.dma("sp", knr[:], k_norm[j:j + 1, :].partition_broadcast(128), W=[B["rows"]], owner=B["rows"])
                P.dma("sp", vnr[:], gmlp_v_norm[j:j + 1, :].partition_broadcast(128), W=[B["rows"]], owner=B["rows"])
                with nc.allow_non_contiguous_dma(reason="tiny transposed bias load"):
                    P.dma("sp", gb[:], gmlp_b[j].rearrange("g p -> p g"), W=[B["gb"]], owner=B["gb"])
                P.dma("sp", rC[:], ropec.rearrange("(s p) f -> p s f", p=128), W=[B["rope"]], owner=B["rope"])
                P.dma("sp", rS[:], ropes.rearrange("(s p) f -> p s f", p=128), W=[B["rope"]], owner=B["rope"])
                P.dma("sp", Gt[:], gsc[i, 1, 0:1, :].partition_broadcast(128), R=[b_gsc], W=[B["G"]], owner=B["G"])
                P.dma("sp", gn[:], norm_g[i, 1:2, :].partition_broadcast(128), W=[B["gn"]], owner=B["gn"])
                w0 = wss[0]
                for g in range(4):
                    P.op("pe", lambda g=g: nc.tensor.transpose(out=w0.pTb[:, g, :], in_=wsf[:, g, :], identity=ident[:]),
                         R=[B["wsf"], b_ident], W=[bb[w0.bT]])
                P.op("dve", lambda: nc.vector.tensor_copy(wsT[:], w0.pTb[:, 0:4, :]), R=[bb[w0.bT]], W=[B["wsT"]])
                P.op("act", lambda: nc.scalar.activation(out=w0.junk[:, 0:96], in_=qnr[:], func=AF.Square), R=[B["rows"]], W=[w0.B["junk"]])
                P.op("act", lambda: nc.scalar.activation(out=w0.junk[:, 96:192], in_=knr[:], func=AF.Square), R=[B["rows"]], W=[w0.B["junk"]])
                P.op("dve", lambda: nc.vector.tensor_reduce(out=cst[:, 0:2], in_=w0.junk[:, 0:192].rearrange("p (a b) -> p a b", a=2),
                                                            axis=AX.X, op=ALU.max), R=[w0.B["junk"]], W=[B["cst"]])
                P.op("dve", lambda: nc.vector.tensor_tensor(out=cst[:, 2:3], in0=cst[:, 0:1], in1=cst[:, 1:2], op=ALU.mult),
                     R=[B["cst"]], W=[B["cst"]])
                P.op("act", lambda: nc.scalar.activation(out=cst[:, 3:4], in_=cst[:, 2:3], func=AF.Sqrt), R=[B["cst"]], W=[B["cst"]])
                P.op("dve", lambda: nc.vector.tensor_scalar(out=cst[:, 4:5], in0=cst[:, 3:4], scalar1=-(96.0 ** 0.5), scalar2=None,
                                                            op0=ALU.mult), R=[B["cst"]], W=[B["cst"]])
                P.op("dve", lambda: nc.vector.tensor_scalar(out=qnr[:], in0=qnr[:], scalar1=96.0 ** -0.5, scalar2=None, op0=ALU.mult),
                     R=[B["rows"], w0.B["junk"]], W=[B["rows"]])
                P.op("pool", lambda: nc.gpsimd.memset(V[:], 1.0), W=[B["V"]])

                def g_rstd(st_, b_st, n, inv):
                    P.op("dve", lambda: nc.vector.tensor_scalar(out=st_[:, n:2 * n], in0=st_[:, 0:n], scalar1=inv, scalar2=EPS,
                                                                op0=ALU.mult, op1=ALU.add), R=[b_st], W=[b_st])
                    yield
                    P.op("act", lambda: nc.scalar.activation(out=st_[:, n:2 * n], in_=st_[:, n:2 * n], func=AF.Sqrt),
                         R=[b_st], W=[b_st])
                    yield
                    P.op("dve", lambda: nc.vector.reciprocal(out=st_[:, 2 * n:3 * n], in_=st_[:, n:2 * n]), R=[b_st], W=[b_st])
                    yield

                def g_front(w, s, cvi):
                    WB = w.B
                    P.dma("sp", w.xt[:], xaps[src][s * 128:(s + 1) * 128, :], R=[xbufs[src][s]], W=[WB["xt"]], owner=WB["xt"])
                    yield
                    P.op("act", lambda: nc.scalar.activation(out=w.xn[:], in_=w.xt[:], func=AF.Square, accum_out=w.sts[:, 0:1]),
                         R=[WB["xt"]], W=[WB["xn"], WB["st"]])
                    yield
                    yield from g_rstd(w.sts, WB["st"], 1, 1.0 / D)
                    P.op("dve", lambda: nc.vector.scalar_tensor_tensor(out=w.xn[:], in0=w.xt[:], scalar=w.sts[:, 2:3], in1=gn[:],
                                                                      op0=ALU.mult, op1=ALU.mult),
                         R=[WB["xt"], WB["st"], B["gn"]], W=[WB["xn"]])
                    yield
                    for kc in range(8):
                        P.op("pe", lambda kc=kc: nc.tensor.transpose(out=w.pTb[:, kc, :], in_=w.xn[:, kc * 128:(kc + 1) * 128],
                                                                     identity=ident[:]), R=[WB["xn"], b_ident], W=[bb[w.bT]])
                    yield
                    for kc in range(8):
                        eng = "act" if kc % 2 == 0 else "dve"
                        if eng == "act":
                            P.op("act", lambda kc=kc: nc.scalar.activation(
                                out=w.hTs[:, kc, :], in_=w.pTb[:, kc, :], func=AF.Identity,
                                scale=AS[:, ik, 0, kc, cvi:cvi + 1], bias=AS[:, ik, 1, kc, cvi:cvi + 1]),
                                R=[bb[w.bT], b_AS], W=[WB["hTs"]])
                        else:
                            P.op("dve", lambda kc=kc: nc.vector.tensor_scalar(
                                out=w.hTs[:, kc, :], in0=w.pTb[:, kc, :], scalar1=AS[:, ik, 0, kc, cvi:cvi + 1],
                                scalar2=AS[:, ik, 1, kc, cvi:cvi + 1], op0=ALU.mult, op1=ALU.add),
                                R=[bb[w.bT], b_AS], W=[WB["hTs"]])
                        if kc % 2 == 1:
                            yield

                def g_headnorm(w, row, rope_idx):
                    WB = w.B
                    T = w.kf
                    bT_ = WB["kf"]
                    P.op("act", lambda: nc.scalar.activation(out=w.junk[:, 0:384], in_=T[:].rearrange("p h d -> p (h d)"), func=AF.Square),
                         R=[bT_], W=[WB["junk"]])
                    yield
                    P.op("dve", lambda: nc.vector.tensor_reduce(out=w.st4[:, 0:4], in_=w.junk[:, 0:384].rearrange("p (h d) -> p h d", h=4),
                                                                axis=AX.X, op=ALU.add), R=[WB["junk"]], W=[WB["st4"]])
                    yield
                    yield from g_rstd(w.st4, WB["st4"], 4, 1.0 / 96)
                    P.op("dve", lambda: nc.vector.tensor_tensor(out=T[:], in0=T[:], in1=w.st4[:, 8:12].unsqueeze(2).to_broadcast([128, 4, 96]),
                                                                op=ALU.mult), R=[bT_, WB["st4"]], W=[bT_])
                    yield
                    P.op("dve", lambda: nc.vector.tensor_tensor(out=T[:], in0=T[:], in1=row[:].unsqueeze(1).to_broadcast([128, 4, 96]),
                                                                op=ALU.mult), R=[bT_, B["rows"]], W=[bT_])
                    yield
                    if rope_idx is not None:
                        r = T[:, :, 64:96].rearrange("p h (a t f) -> p h a t f", a=2, t=2)
                        x1 = r[:, :, :, 0, :]
                        x2 = r[:, :, :, 1, :]
                        cb = rC[:, rope_idx, :].rearrange("p (a f) -> p a f", a=2).unsqueeze(1).to_broadcast([128, 4, 2, 8])
                        sbk = rS[:, rope_idx, :].rearrange("p (a f) -> p a f", a=2).unsqueeze(1).to_broadcast([128, 4, 2, 8])
                        rt = w.rt
                        P.op("dve", lambda: nc.vector.tensor_tensor(out=rt[0][:], in0=x1, in1=cb, op=ALU.mult), R=[bT_, B["rope"]], W=[WB["rt"]])
                        P.op("pool", lambda: nc.gpsimd.tensor_tensor(out=rt[1][:], in0=x2, in1=sbk, op=ALU.mult), R=[bT_, B["rope"]], W=[WB["rt"]])
                        P.op("dve", lambda: nc.vector.tensor_tensor(out=rt[2][:], in0=x1, in1=sbk, op=ALU.mult), R=[bT_, B["rope"]], W=[WB["rt"]])
                        P.op("pool", lambda: nc.gpsimd.tensor_tensor(out=rt[3][:], in0=x2, in1=cb, op=ALU.mult), R=[bT_, B["rope"]], W=[WB["rt"]])
                        yield
                        P.op("dve", lambda: nc.vector.tensor_tensor(out=x1, in0=rt[0][:], in1=rt[1][:], op=ALU.subtract), R=[WB["rt"]], W=[bT_])
                        P.op("dve", lambda: nc.vector.tensor_tensor(out=x2, in0=rt[2][:], in1=rt[3][:], op=ALU.add), R=[WB["rt"]], W=[bT_])
                        yield

                def g_kpath(w, chunk, rope_idx, h0, save_idx=None, load_idx=None):
                    WB = w.B
                    if load_idx is None:
                        P.op("act", lambda: nc.scalar.copy(out=w.kvb[:, 0:128], in_=w.kvf[:]), R=[WB["kvf"]], W=[WB["kvb"]])
                        yield
                        P.op("pe", lambda: nc.tensor.transpose(out=w.pTb[:, 0, :], in_=w.kvb[:, 0:128], identity=ident[:]),
                             R=[WB["kvb"], b_ident], W=[bb[w.bT]])
                        yield
                        P.op("dve", lambda: nc.vector.tensor_copy(w.lT[:, 0, :], w.pTb[:, 0, :]), R=[bb[w.bT]], W=[WB["lT"]])
                        yield
                        if save_idx is not None:
                            P.dma("sp", klT_d[save_idx], w.lT[:, 0, :], R=[WB["lT"]], W=[kd_bufs[save_idx]], owner=WB["lT"], kind="st")
                            P.dma("sp", krf_d[save_idx], w.krf[:], R=[WB["krf"]], W=[kd_bufs[save_idx]], owner=WB["krf"], kind="st")
                    else:
                        P.dma("sp", w.lT[:, 0, :], klT_d[load_idx], R=[kd_bufs[load_idx]], W=[WB["lT"]], owner=WB["lT"])
                        P.dma("sp", w.krf[:], krf_d[load_idx], R=[kd_bufs[load_idx]], W=[WB["krf"]], owner=WB["krf"])
                        yield
                    P.op("pe", lambda: nc.tensor.matmul(banks[w.bQ][:], lhsT=w.lT[:, 0, :], rhs=wkvb[:, h0 * 128:(h0 + 4) * 128],
                                                       start=True, stop=True), R=[WB["lT"], B["wkvb"]], W=[bb[w.bQ]])
                    yield
                    kvv = banks[w.bQ][:].rearrange("p (h d) -> p h d", h=4)
                    P.op("act", lambda: nc.scalar.copy(out=w.kf[:, :, 0:64], in_=kvv[:, :, 0:64]), R=[bb[w.bQ]], W=[WB["kf"]])
                    P.op("dve", lambda: nc.vector.tensor_copy(w.kf[:, :, 64:96], w.krf[:].unsqueeze(1).to_broadcast([128, 4, 32])),
                         R=[WB["krf"]], W=[WB["kf"]])
                    P.op("dve", lambda: nc.vector.tensor_copy(V[:, chunk, :, 0:64], kvv[:, :, 64:128]), R=[bb[w.bQ]], W=[B["V"]])
                    yield
                    yield from g_headnorm(w, knr, rope_idx)
                    P.op("act", lambda: nc.scalar.copy(out=w.kb[:], in_=w.kf[:]), R=[WB["kf"]], W=[WB["kb"]])
                    yield
                    for hh in range(4):
                        P.op("pe", lambda hh=hh: nc.tensor.transpose(out=w.pTb[0:96, hh, :], in_=w.kb[:, hh, :], identity=ident[:]),
                             R=[WB["kb"], b_ident], W=[bb[w.bT]])
                    yield
                    P.op("dve", lambda: nc.vector.tensor_copy(KT[:, :, chunk * 128:(chunk + 1) * 128], w.pTb[0:96, 0:4, :]),
                         R=[bb[w.bT]], W=[B["KT"]])
                    yield

                def g_cache(w, c, h0):
                    WB = w.B
                    P.dma("sp", w.kvf[:], cckv[j, c * 128:(c + 1) * 128, :], W=[WB["kvf"]], owner=WB["kvf"])
                    P.dma("sp", w.krf[:], ckro[j, c * 128:(c + 1) * 128, :], W=[WB["krf"]], owner=WB["krf"])
                    yield
                    yield from g_kpath(w, c, None, h0, save_idx=40 + c)

                def g_phaseA(w, s, idx, chunk, cvi, rope_idx, h0, out_seq):
                    WB = w.B
                    yield from g_front(w, s, cvi)
                    for kc in range(8):
                        P.op("pe", lambda kc=kc: nc.tensor.matmul(banks[w.bP][:, 0:160], lhsT=w.hTs[:, kc, :], rhs=win[:, kc, 256:416],
                                                                  start=(kc == 0), stop=(kc == 7)), R=[WB["hTs"], B["win"]], W=[bb[w.bP]])
                    yield
                    P.op("act", lambda: nc.scalar.activation(out=w.junk[:, 0:128], in_=banks[w.bP][:, 0:128], func=AF.Square,
                                                             accum_out=w.sts[:, 0:1]), R=[bb[w.bP]], W=[WB["junk"], WB["st"]])
                    yield
                    yield from g_rstd(w.sts, WB["st"], 1, 1.0 / 128)
                    P.op("dve", lambda: nc.vector.scalar_tensor_tensor(out=w.kvf[:], in0=banks[w.bP][:, 0:128], scalar=w.sts[:, 2:3],
                                                                      in1=kvan[:], op0=ALU.mult, op1=ALU.mult),
                         R=[bb[w.bP], WB["st"], B["rows"]], W=[WB["kvf"]])
                    P.op("act", lambda: nc.scalar.copy(out=w.krf[:], in_=banks[w.bP][:, 128:160]), R=[bb[w.bP]], W=[WB["krf"]])
                    yield
                    if out_seq is not None:
                        P.dma("sp", ckv_o[out_seq, j, idx * 128:(idx + 1) * 128, :], w.kvf[:], R=[WB["kvf"]], owner=WB["kvf"], kind="st")
                        P.dma("sp", kro_o[out_seq, j, idx * 128:(idx + 1) * 128, :], w.krf[:], R=[WB["krf"]], owner=WB["krf"], kind="st")
                        yield
                    yield from g_kpath(w, chunk, rope_idx, h0, save_idx=s)

                def g_phaseB1(w, s, q, rope_idx, h0):
                    WB = w.B
                    P.dma("sp", w.lT[:].rearrange("p k t -> p (k t)"), qlT_d[s], R=[qd_bufs[s]], W=[WB["lT"]], owner=WB["lT"])
                    yield
                    yield from g_qtail(w, q, rope_idx, h0)

                def g_qtail(w, q, rope_idx, h0):
                    WB = w.B
                    for kc in range(2):
                        P.op("pe", lambda kc=kc: nc.tensor.matmul(banks[w.bQ][:, 0:384], lhsT=w.lT[:, kc, :],
                                                                  rhs=wqb[:, kc, h0 * 96:(h0 + 4) * 96],
                                                                  start=(kc == 0), stop=(kc == 1)), R=[WB["lT"], B["wqb"]], W=[bb[w.bQ]])
                    yield
                    P.op("act", lambda: nc.scalar.copy(out=w.kf[:].rearrange("p h d -> p (h d)"), in_=banks[w.bQ][:, 0:384]),
                         R=[bb[w.bQ]], W=[WB["kf"]])
                    yield
                    yield from g_headnorm(w, qnr, rope_idx)
                    P.op("act", lambda: nc.scalar.copy(out=w.kb[:], in_=w.kf[:]), R=[WB["kf"]], W=[WB["kb"]])
                    yield
                    for hh in range(4):
                        P.op("pe", lambda hh=hh: nc.tensor.transpose(out=w.pTb[0:96, hh, :], in_=w.kb[:, hh, :], identity=ident[:]),
                             R=[WB["kb"], b_ident], W=[bb[w.bT]])
                    yield
                    P.op("dve", lambda: nc.vector.tensor_copy(QT[:, :, q * 128:(q + 1) * 128], w.pTb[0:96, 0:4, :]),
                         R=[bb[w.bT]], W=[B["QT"]])
                    yield

                def g_qpart(w, s, q, rope_idx, h0):
                    WB = w.B
                    for kc in range(8):
                        P.op("pe", lambda kc=kc: nc.tensor.matmul(banks[w.bP][:, 0:256], lhsT=w.hTs[:, kc, :], rhs=win[:, kc, 0:256],
                                                                  start=(kc == 0), stop=(kc == 7)), R=[WB["hTs"], B["win"]], W=[bb[w.bP]])
                    yield
                    P.op("act", lambda: nc.scalar.activation(out=w.junk[:, 0:256], in_=banks[w.bP][:, 0:256], func=AF.Square,
                                                             accum_out=w.sts[:, 0:1]), R=[bb[w.bP]], W=[WB["junk"], WB["st"]])
                    yield
                    yield from g_rstd(w.sts, WB["st"], 1, 1.0 / 256)
                    P.op("dve", lambda: nc.vector.scalar_tensor_tensor(out=w.kvb[:], in0=banks[w.bP][:, 0:256], scalar=w.sts[:, 2:3],
                                                                      in1=qan[:], op0=ALU.mult, op1=ALU.mult),
                         R=[bb[w.bP], WB["st"], B["rows"]], W=[WB["kvb"]])
                    yield
                    for kc in range(2):
                        P.op("pe", lambda kc=kc: nc.tensor.transpose(out=w.pTb[:, kc, :], in_=w.kvb[:, kc * 128:(kc + 1) * 128],
                                                                     identity=ident[:]), R=[WB["kvb"], b_ident], W=[bb[w.bT]])
                    yield
                    P.op("dve", lambda: nc.vector.tensor_copy(w.lT[:], w.pTb[:, 0:2, :]), R=[bb[w.bT]], W=[WB["lT"]])
                    yield
                    P.dma("sp", qlT_d[s], w.lT[:].rearrange("p k t -> p (k t)"), R=[WB["lT"]], W=[qd_bufs[s]], owner=WB["lT"], kind="st")
                    yield from g_qtail(w, q, rope_idx, h0)

                def g_gpart(w, s, q):
                    WB = w.B
                    for blk, dstt, bd in ((0, w.ug, WB["ug"]), (1, w.vg, WB["vg"])):
                        for kc in range(8):
                            P.op("pe", lambda kc=kc, blk=blk: nc.tensor.matmul(
                                banks[w.bU][:], lhsT=w.hTs[:, kc, :], rhs=win[:, kc, 416 + blk * 512:928 + blk * 512],
                                start=(kc == 0), stop=(kc == 7)), R=[WB["hTs"], B["win"]], W=[bb[w.bU]])
                        yield
                        P.op("act", lambda: nc.scalar.activation(out=w.g1[:], in_=banks[w.bU][:], func=AF.Square),
                             R=[bb[w.bU]], W=[WB["g1"]])
                        yield
                        P.op("dve", lambda: nc.vector.tensor_scalar(out=w.g1[:], in0=w.g1[:], scalar1=0.044715, scalar2=1.0,
                                                                    op0=ALU.mult, op1=ALU.add), R=[WB["g1"]], W=[WB["g1"]])
                        yield
                        P.op("dve", lambda: nc.vector.tensor_tensor(out=w.g2[:], in0=banks[w.bU][:], in1=w.g1[:], op=ALU.mult),
                             R=[bb[w.bU], WB["g1"]], W=[WB["g2"]])
                        yield
                        P.op("act", lambda: nc.scalar.activation(out=w.g2[:], in_=w.g2[:], func=AF.Sigmoid, scale=1.5957691216057308),
                             R=[WB["g2"]], W=[WB["g2"]])
                        yield
                        P.op("dve", lambda dstt=dstt: nc.vector.tensor_tensor(out=dstt[:], in0=banks[w.bU][:], in1=w.g2[:], op=ALU.mult),
                             R=[bb[w.bU], WB["g2"]], W=[bd])
                        yield
                    P.op("act", lambda: nc.scalar.activation(out=w.g2[:], in_=w.vg[:], func=AF.Square, accum_out=w.stg[:, 0:1]),
                         R=[WB["vg"]], W=[WB["g2"], WB["stg"]])
                    yield
                    yield from g_rstd(w.stg, WB["stg"], 1, 1.0 / 512)
                    P.op("dve", lambda: nc.vector.scalar_tensor_tensor(out=w.vnb[:], in0=w.vg[:], scalar=w.stg[:, 2:3], in1=vnr[:],
                                                                      op0=ALU.mult, op1=ALU.mult),
                         R=[WB["vg"], WB["stg"], B["rows"]], W=[WB["vnb"]])
                    yield
                    for g in range(4):
                        P.op("pe", lambda g=g: nc.tensor.matmul(banks[w.bU][:, g * 128:(g + 1) * 128], lhsT=wsT[:, g, :],
                                                                rhs=w.vnb[:, g * 128:(g + 1) * 128], start=True, stop=True),
                             R=[B["wsT"], WB["vnb"]], W=[bb[w.bU]])
                    yield
                    for g in range(4):
                        P.op("dve", lambda g=g: nc.vector.scalar_tensor_tensor(
                            out=w.gout[:, g * 128:(g + 1) * 128], in0=banks[w.bU][:, g * 128:(g + 1) * 128], scalar=gb[:, g:g + 1],
                            in1=w.ug[:, g * 128:(g + 1) * 128], op0=ALU.add, op1=ALU.mult),
                            R=[bb[w.bU], B["gb"], WB["ug"]], W=[WB["gout"]])
                    yield
                    for g in range(4):
                        P.op("pe", lambda g=g: nc.tensor.transpose(out=w.pTb[:, 4 + g, :], in_=w.gout[:, g * 128:(g + 1) * 128],
                                                                   identity=ident[:]), R=[WB["gout"], b_ident], W=[bb[w.bT]])
                    yield
                    P.op("dve", lambda: nc.vector.tensor_copy(goutT[:, :, q * 128:(q + 1) * 128], w.pTb[:, 4:8, :]),
                         R=[bb[w.bT]], W=[B["goutT"]])
                    yield


                def g_phaseB(w, s, q, cvi, rope_idx, h0, pass_no):
                    yield from g_front(w, s, cvi)
                    parts = [g_qpart(w, s, q, rope_idx, h0)]
                    if pass_no == 0:
                        parts.append(g_gpart(w, s, q))
                    while parts:
                        for g in list(parts):
                            try:
                                next(g)
                            except StopIteration:
                                parts.remove(g)
                        yield

                def run_seq(subs, ncache, use_rope, pass_no, seq_id):
                    h0 = 4 * pass_no
                    cvi = cv_of(subs[0])
                    n = len(subs)
                    nchunks = n + ncache
                    base = src if pass_no == 0 else dst
                    set_banks(pass_no)
                    jobs = []
                    for c in range(ncache):
                        jobs.append(("c", c))
                    for idx, s in enumerate(subs):
                        jobs.append(("s", idx))
                    if pass_no == 0:
                        for p0 in range(0, len(jobs), 2):
                            gens = []
                            for wi, jb in enumerate(jobs[p0:p0 + 2]):
                                if jb[0] == "c":
                                    gens.append(g_cache(wss[wi], jb[1], h0))
                                else:
                                    idx = jb[1]
                                    s = subs[idx]
                                    gens.append(g_phaseA(wss[wi], s, idx, ncache + idx, cvi, (s - 8) if use_rope else None, h0,
                                                         seq_id if ncache == 0 else None))
                            interleave(gens)
                    else:
                        for p0 in range(0, len(jobs), 4):
                            gens = []
                            for wi, jb in enumerate(jobs[p0:p0 + 4]):
                                if jb[0] == "c":
                                    gens.append(g_kpath(wss[wi], jb[1], None, h0, load_idx=40 + jb[1]))
                                else:
                                    idx = jb[1]
                                    s = subs[idx]
                                    gens.append(g_kpath(wss[wi], ncache + idx, (s - 8) if use_rope else None, h0, load_idx=s))
                            interleave(gens)
                    QW = min(512, n * 128)
                    nq = QW // 128
                    for t0 in range(0, n, nq):
                        if pass_no == 0:
                            for q0 in range(0, nq, 2):
                                gens = []
                                for wi in range(2):
                                    q = q0 + wi
                                    s = subs[t0 + q]
                                    gens.append(g_phaseB(wss[wi], s, q, cvi, (s - 8) if use_rope else None, h0, pass_no))
                                if pend_wout[0] is not None:
                                    gens.append(pend_wout[0])
                                    pend_wout[0] = None
                                interleave(gens)
                        else:
                            gens = []
                            for q in range(nq):
                                s = subs[t0 + q]
                                gens.append(g_phaseB1(wss[q], s, q, (s - 8) if use_rope else None, h0))
                            if pend_wout[0] is not None:
                                gens.append(pend_wout[0])
                                pend_wout[0] = None
                            interleave(gens)
                        items = [(hh, c) for hh in range(4) for c in range(0, nchunks, 2)]
                        sc_pairs = [(5, 6), (1, 2)]

                        def emit_score(ii):
                            hh, c = items[ii]
                            pr = sc_pairs[ii % 2]
                            pc = ii % 2
                            for u in range(2):
                                P.op("pe", lambda u=u: nc.tensor.matmul(
                                    banks[pr[u]][:, 0:QW], lhsT=KT[:, hh, (c + u) * 128:(c + u + 1) * 128], rhs=QT[:, hh, 0:QW],
                                    start=True, stop=True), R=[B["KT"], B["QT"]], W=[bb[pr[u]]])
                            if HAM_FILL and QW == 512:
                                P.op("pe", lambda: nc.tensor.matmul(
                                    banks[0][:, 0:HAM_N], lhsT=KT[:, hh, c * 128:(c + 1) * 128], rhs=QT[:, hh, 0:HAM_N],
                                    start=True, stop=True), R=[B["KT"], B["QT"]], W=[bb[0]])
                            src2 = psum_all[:, pr[0] * 512:(pr[0] + 2) * 512].rearrange("p (b n) -> p b n", b=2)[:, :, 0:QW]
                            dst2 = pT[pc][:].rearrange("p (b n) -> p b n", b=2)[:, :, 0:QW]
                            P.op("act", lambda: nc.scalar.activation(out=dst2, in_=src2, func=AF.Exp, bias=cst[:, 4:5], scale=1.0),
                                 R=[bb[pr[0]], bb[pr[1]], B["cst"]], W=[b_pT[pc]])

                        def emit_pv(ii):
                            hh, c = items[ii]
                            pc = ii % 2
                            obk = 7 if hh % 2 == 0 else 3
                            for u in range(2):
                                P.op("pe", lambda u=u: nc.tensor.matmul(
                                    banks[obk][0:65, 0:QW], lhsT=V[:, c + u, hh, :], rhs=pT[pc][:, u * 512:u * 512 + QW],
                                    start=(c + u == 0), stop=(c + u == nchunks - 1)), R=[B["V"], b_pT[pc]], W=[bb[obk]])
                            if c + 2 >= nchunks:
                                def fin(hh=hh, obk=obk):
                                    P.op("dve", lambda: nc.vector.reciprocal(out=rr[64:65, 0:QW], in_=banks[obk][64:65, 0:QW]),
                                         R=[bb[obk]], W=[B["rr"]])
                                    P.op("pe", lambda: nc.tensor.matmul(banks[4][0:64, 0:QW], lhsT=ones[64:65, 0:64], rhs=rr[64:65, 0:QW],
                                                                       start=True, stop=True), R=[B["rr"], b_ones], W=[bb[4]])
                                    P.op("dve", lambda: nc.vector.tensor_copy(ob[:, 0:QW], banks[obk][0:64, 0:QW]), R=[bb[obk]], W=[B["ob"]])
                                    P.op("dve", lambda: nc.vector.tensor_tensor(out=attnT[:, hh, 0:QW], in0=banks[4][0:64, 0:QW],
                                                                                in1=ob[:, 0:QW], op=ALU.mult),
                                         R=[bb[4], B["ob"]], W=[B["attnT"]])
                                pending.append([fin, min(2, nchunks // 2), ii_cur[0]])

                        pending = []
                        ii_cur = [0]
                        emit_score(0)
                        for ii in range(len(items)):
                            ii_cur[0] = ii
                            if ii + 1 < len(items):
                                emit_score(ii + 1)
                            emit_pv(ii)
                            for pn in list(pending):
                                if pn[2] == ii:
                                    continue
                                pn[1] -= 1
                                if pn[1] <= 0:
                                    pn[0]()
                                    pending.remove(pn)
                        for pn in pending:
                            pn[0]()
                        interleave([g_wout(t0, nq, subs, h0, pass_no, base, cvi)])
                    if pend_wout[0] is not None:
                        interleave([pend_wout[0]])
                        pend_wout[0] = None

                def g_wout(t0, nq, subs, h0, pass_no, base, cvi):
                    for q in range(nq):
                        s = subs[t0 + q]
                        halves = []
                        for half in range(2):
                            bk = 1 + half
                            nmm = 4 + (4 if pass_no == 0 else 0)
                            m = 0
                            for hh in range(4):
                                P.op("pe", lambda hh=hh, q=q, half=half, bk=bk, m=m, nmm=nmm: nc.tensor.matmul(
                                    banks[bk][:], lhsT=attnT[:, hh, q * 128:(q + 1) * 128],
                                    rhs=woA[:, h0 + hh, half * 512:(half + 1) * 512], start=(m == 0), stop=(m == nmm - 1)),
                                    R=[B["attnT"], B["woA"]], W=[bb[bk]])
                                m += 1
                            if pass_no == 0:
                                for cg in range(4):
                                    P.op("pe", lambda cg=cg, q=q, half=half, bk=bk, m=m, nmm=nmm: nc.tensor.matmul(
                                        banks[bk][:], lhsT=goutT[:, cg, q * 128:(q + 1) * 128],
                                        rhs=woG[:, cg, half * 512:(half + 1) * 512], start=(m == 0), stop=(m == nmm - 1)),
                                        R=[B["goutT"], B["woG"]], W=[bb[bk]])
                                    m += 1
                            halves.append((bk, half))
                        residual(base, dst, s, cvi, halves, G, B["G"], xr, B["xr"], tmp, b_tmp)
                        yield

                pend_wout = [None]

                seqs = [([2 * q, 2 * q + 1], 0, False, q) for q in range(4)] + [(list(range(8, 40)), 4, True, -1)]
                for (subs, ncache, use_rope, seq_id) in seqs:
                    if ncache > 0:
                        P.dma("sp", Gt[:], gsc[i, 1, 1:2, :].partition_broadcast(128), R=[b_gsc], W=[B["G"]], owner=B["G"])
                    for pass_no in range(2):
                        run_seq(subs, ncache, use_rope, pass_no, seq_id)
                P.barrier()
                allw = []
                for w in wss:
                    allw += list(w.B.values())
                P.release(list(B.values()) + allw)

        prologue()
        chain = ["xin"] + ["xa" if (n % 2 == 0) else "xb" for n in range(11)] + ["y"]
        stage_no = [0]

        def nxt():
            n = stage_no[0]
            stage_no[0] += 1
            return chain[n], chain[n + 1]

        import os
        nst = int(os.environ.get("MK_STAGES", "12"))
        plan = []
        for i in range(4):
            plan += [("ffn", i, 0), ("mla" if i % 2 == 0 else "pool", i, 0), ("ffn", i, 1)]
        plan = plan[:nst]
        for idx, (kind, i, which) in enumerate(plan):
            src, dst = nxt()
            if idx == len(plan) - 1:
                dst = "y"
            if kind == "ffn":
                ffn_stage(i, which, src, dst)
            elif kind == "mla":
                mla_stage(i, src, dst)
            else:
                pool_stage(i, src, dst)
        P.barrier()
    return nc


def _consts():
    rows = np.repeat(np.arange(64), 64).astype(np.float32)
    cols = np.tile(np.arange(64), 64).astype(np.float32)
    inv = (10000.0 ** (-np.arange(0, 16, 2, dtype=np.float32) / 16)).astype(np.float32)
    ang = np.concatenate([rows[:, None] * inv, cols[:, None] * inv], axis=-1).astype(np.float32)
    ropec = np.cos(ang).astype(np.float32)
    ropes = np.sin(ang).astype(np.float32)
    band = np.zeros((128, 4, 5, 128), np.float32)
    for wi, w in enumerate(WINS):
        h = w // 2
        for t in range(128):
            for var in range(5):
                if var in (0, 3, 4):
                    lo, hi = t - h, t + w - h
                    cnt = w
                elif var == 1:
                    lo, hi = max(t - h, 0), t + w - h
                    cnt = hi - lo
                else:
                    lo, hi = t - h, min(t + w - h, 128)
                    cnt = hi - lo
                for tp in range(lo, hi):
                    if var in (0, 1, 2):
                        if 0 <= tp < 128:
                            band[tp, wi, var, t] += 1.0 / cnt
                    elif var == 3:
                        if tp < 0:
                            band[tp + 128, wi, var, t] += 1.0 / cnt
                    else:
                        if tp >= 128:
                            band[tp - 128, wi, var, t] += 1.0 / cnt
                if var in (0, 1, 2):
                    band[t, wi, var, t] -= 1.0
    return ropec, ropes, band


_NC_CACHE = {}


def kernel(**inp):
    f = lambda a: np.ascontiguousarray(np.asarray(a, dtype=np.float32))
    if "nc" not in _NC_CACHE:
        _NC_CACHE["nc"] = build_nc()
    nc = _NC_CACHE["nc"]
    ropec, ropes, band = _consts()
    shared = {k: f(inp[k]) for k in ("w_mod", "b_mod", "norm_g", "ffn_w1", "ffn_w3", "ffn_w2", "w_in", "q_a_norm",
                                     "kv_a_norm", "w_qb", "w_kvb", "q_norm", "k_norm", "gmlp_v_norm", "gmlp_ws",
                                     "gmlp_b", "w_out", "pool_w", "pool_scale")}
    shared.update(ropec=ropec, ropes=ropes, band=band)
    xp = f(inp["x_prompt"]); xs = f(inp["x_sample"])
    in_maps = []
    for c in range(8):
        m = dict(shared)
        m["xin"] = np.ascontiguousarray(np.concatenate([xp[4 * c:4 * c + 4].reshape(1024, D), xs[c]], axis=0))
        m["cckv"] = f(inp["cache_ckv"][c])
        m["ckro"] = f(inp["cache_krope"][c])
        m["cvec"] = np.ascontiguousarray(np.stack([f(inp["c_ctx"]), f(inp["c"][c])], axis=0))
        in_maps.append(m)
    res = run_bass_kernel_spmd(nc, in_maps, core_ids=list(range(8)))
    r = res.results
    y_prompt = np.concatenate([r[c]["y"][:1024].reshape(4, 256, D) for c in range(8)], axis=0)
    y_sample = np.stack([r[c]["y"][1024:] for c in range(8)], axis=0)
    new_ckv = np.concatenate([r[c]["ckv_o"] for c in range(8)], axis=0)
    new_krope = np.concatenate([r[c]["kro_o"] for c in range(8)], axis=0)
    return (y_prompt.astype(np.float32), y_sample.astype(np.float32), new_ckv.astype(np.float32),
            new_krope.astype(np.float32))
```

```python
import numpy as np
from contextlib import ExitStack
import concourse.bass as bass
import concourse.mybir as mybir
from concourse.bass_utils import run_bass_kernel_spmd

F32 = mybir.dt.float32
BF16 = mybir.dt.bfloat16
AF = mybir.ActivationFunctionType
ALU = mybir.AluOpType
AX = mybir.AxisListType

D = 1024
NSUB = 40
NTOK = NSUB * 128
FH = 2816
NF = 22
EPS = 1e-6
WINS = (2, 4, 8, 16)
SAME_SYNC = True
import os as _os
HAM_FILL = bool(int(_os.environ.get('HAM_FILL', '0')))
HAM_N = int(_os.environ.get('HAM_N', '256'))


class Buf:
    __slots__ = ("name", "w", "r", "ld", "st")

    def __init__(self, name):
        self.name = name
        self.w = {}
        self.r = {}
        self.ld = None
        self.st = None


class Prog:
    def __init__(self, nc, es):
        self.nc = nc
        self.es = es
        self.eng = {"pe": nc.tensor, "act": nc.scalar, "dve": nc.vector, "pool": nc.gpsimd, "sp": nc.sync}
        self.semobj = {}
        self.cnt = {}
        for e in ("pe", "act", "dve", "pool"):
            self.semobj[e] = es.enter_context(nc.semaphore("s_" + e))
            self.cnt[e] = 0
        self.waited = {e: {} for e in self.eng}
        self.free_d = {"sp": [], "pool": []}
        self.dq = {}
        self.nd = 0

    def _newd(self, q):
        if self.free_d[q]:
            return self.free_d[q].pop()
        k = ("d", self.nd)
        self.dq[k] = q
        self.nd += 1
        self.semobj[k] = self.es.enter_context(self.nc.semaphore("s_d%d" % k[1]))
        self.cnt[k] = 0
        return k

    def release(self, bufs):
        for b in bufs:
            if b.ld is not None:
                self.free_d[self.dq[b.ld]].append(b.ld)
                b.ld = None
            if b.st is not None:
                self.free_d[self.dq[b.st]].append(b.st)
                b.st = None

    def _deps(self, R, W):
        d = {}
        for b in R:
            for k, v in b.w.items():
                if d.get(k, 0) < v:
                    d[k] = v
        for b in W:
            for k, v in b.w.items():
                if d.get(k, 0) < v:
                    d[k] = v
            for k, v in b.r.items():
                if d.get(k, 0) < v:
                    d[k] = v
        return d

    def _waits(self, e, deps):
        wd = self.waited[e]
        for k, v in deps.items():
            if k == e and (e == "pe" or not SAME_SYNC):
                continue
            if wd.get(k, 0) >= v:
                continue
            self.eng[e].wait_ge(self.semobj[k], v)
            wd[k] = v

    def op(self, e, fn, R=(), W=()):
        self._waits(e, self._deps(R, W))
        ins = fn()
        self.cnt[e] += 1
        ins.then_inc(self.semobj[e], 1)
        v = self.cnt[e]
        for b in R:
            b.r[e] = v
        for b in W:
            b.w[e] = v

    def dma(self, q, out, in_, R=(), W=(), owner=None, kind="ld", **kw):
        self._waits(q, self._deps(R, W))
        if kind == "ld":
            if owner.ld is None:
                owner.ld = self._newd(q)
            k = owner.ld
        else:
            if owner.st is None:
                owner.st = self._newd(q)
            k = owner.st
        assert self.dq[k] == q, (owner.name, kind, q)
        ins = self.eng[q].dma_start(out=out, in_=in_, **kw)
        self.cnt[k] += 16
        ins.then_inc(self.semobj[k], 16)
        v = self.cnt[k]
        for b in R:
            b.r[k] = v
        for b in W:
            b.w[k] = v

    def barrier(self):
        for e in self.eng:
            wd = self.waited[e]
            for k, v in self.cnt.items():
                if v > 0 and wd.get(k, 0) < v:
                    self.eng[e].wait_ge(self.semobj[k], v)
                    wd[k] = v


_UID = [0]


def uniq(name):
    _UID[0] += 1
    return "%s_u%d" % (name, _UID[0])


def cv_of(s):
    return 0 if s < 8 else 1


def build_nc():
    nc = bass.Bass("TRN2", target_bir_lowering=False)

    def din(name, shape):
        return nc.dram_tensor(name, list(shape), F32, kind="ExternalInput").ap()

    xin = din("xin", [NTOK, D])
    cckv = din("cckv", [2, 512, 128])
    ckro = din("ckro", [2, 512, 32])
    cvec = din("cvec", [2, D])
    w_mod = din("w_mod", [4, D, 9 * D])
    b_mod = din("b_mod", [4, 9 * D])
    norm_g = din("norm_g", [4, 3, D])
    ffn_w1 = din("ffn_w1", [4, 2, D, FH])
    ffn_w3 = din("ffn_w3", [4, 2, D, FH])
    ffn_w2 = din("ffn_w2", [4, 2, FH, D])
    w_in = din("w_in", [2, D, 1440])
    q_a_norm = din("q_a_norm", [2, 256])
    kv_a_norm = din("kv_a_norm", [2, 128])
    w_qb = din("w_qb", [2, 256, 768])
    w_kvb = din("w_kvb", [2, 128, 1024])
    q_norm = din("q_norm", [2, 96])
    k_norm = din("k_norm", [2, 96])
    gmlp_v_norm = din("gmlp_v_norm", [2, 512])
    gmlp_ws = din("gmlp_ws", [2, 4, 128, 128])
    gmlp_b = din("gmlp_b", [2, 4, 128])
    w_out = din("w_out", [2, D, D])
    pool_w = din("pool_w", [2, 4, 256, 256])
    pool_scale = din("pool_scale", [2, D])
    ropec = din("ropec", [4096, 16])
    ropes = din("ropes", [4096, 16])
    band = din("band", [128, 4, 5, 128])

    y = nc.dram_tensor("y", [NTOK, D], F32, kind="ExternalOutput").ap()
    ckv_o = nc.dram_tensor("ckv_o", [4, 2, 256, 128], F32, kind="ExternalOutput").ap()
    kro_o = nc.dram_tensor("kro_o", [4, 2, 256, 32], F32, kind="ExternalOutput").ap()
    xa = nc.dram_tensor("xa", [NTOK, D], F32, kind="Internal").ap()
    xb = nc.dram_tensor("xb", [NTOK, D], F32, kind="Internal").ap()
    gsc = nc.dram_tensor("gsc", [4, 3, 2, D], F32, kind="Internal").ap()
    klT_d = nc.dram_tensor("klT_d", [44, 128, 128], BF16, kind="Internal").ap()
    krf_d = nc.dram_tensor("krf_d", [44, 128, 32], F32, kind="Internal").ap()
    qlT_d = nc.dram_tensor("qlT_d", [40, 128, 256], BF16, kind="Internal").ap()

    with ExitStack() as es:
        P = Prog(nc, es)

        def sbp(name, shape, dt):
            return es.enter_context(nc.sbuf_tensor(name, list(shape), dt))

        ident = sbp("ident", [128, 128], BF16)
        identf = sbp("identf", [128, 128], F32)
        ones = sbp("ones", [128, 128], F32)
        AS = sbp("AS", [128, 12, 2, 8, 2], F32)
        psum_all = es.enter_context(nc.psum_tensor("psum_all", [128, 4096], F32))
        banks = [psum_all[:, i * 512:(i + 1) * 512] for i in range(8)]
        bb = [Buf("bank%d" % i) for i in range(8)]
        b_ident = Buf("ident")
        b_AS = Buf("AS")
        b_ones = Buf("ones")
        xbufs = {}
        for nm in ("xin", "xa", "xb", "y"):
            xbufs[nm] = [Buf("%s_%d" % (nm, s)) for s in range(NSUB)]
        xaps = {"xin": xin, "xa": xa, "xb": xb, "y": y}
        b_gsc = Buf("gsc")

        P.op("pool", lambda: nc.gpsimd.memset(identf[:], 0.0), W=[b_ident])
        P.op("pool", lambda: nc.gpsimd.affine_select(out=identf[:], in_=identf[:], pattern=[[-1, 128]],
                                                      compare_op=ALU.not_equal, fill=1.0, base=0,
                                                      channel_multiplier=1), R=[b_ident], W=[b_ident])
        P.op("dve", lambda: nc.vector.tensor_copy(ident[:], identf[:]), R=[b_ident], W=[b_ident])
        P.op("dve", lambda: nc.vector.memset(ones[:], 1.0), W=[b_ones])

        def prologue():
            with ExitStack() as st:
                def sb(name, shape, dt):
                    return st.enter_context(nc.sbuf_tensor(uniq(name), list(shape), dt))
                cT = sb("cT", [128, 2, 8], F32)
                sT = sb("sT", [128, 2, 8], F32)
                sTb = sb("sTb", [128, 2, 8], BF16)
                sbc = sb("sbc", [128, 16, 128], BF16)
                wblk = [sb("wblk%d" % i, [128, 8, 1024], BF16) for i in range(3)]
                brow = [sb("brow%d" % i, [1, 1024], F32) for i in range(2)]
                grow = [sb("grow%d" % i, [128, 1024], F32) for i in range(2)]
                psc = sb("psc", [128, 1024], F32)
                b_cT = Buf("cT"); b_sT = Buf("sT"); b_sbc = Buf("sbc")
                b_w = [Buf("wblk0"), Buf("wblk1"), Buf("wblk2")]
                b_br = [Buf("brow0"), Buf("brow1")]
                b_gr = [Buf("grow0"), Buf("grow1")]
                b_psc = Buf("psc")
                allb = [b_cT, b_sT, b_sbc, b_psc] + b_w + b_br + b_gr
                with nc.allow_non_contiguous_dma(reason="tiny transposed cvec load"):
                    for c in range(2):
                        P.dma("sp", cT[:, c, :], cvec[c, :].rearrange("(k p) -> p k", p=128), W=[b_cT], owner=b_cT)
                P.op("act", lambda: nc.scalar.activation(out=sT[:], in_=cT[:], func=AF.Silu), R=[b_cT], W=[b_sT])
                P.op("dve", lambda: nc.vector.tensor_copy(
                    sbc[:], sT[:].rearrange("p c k -> p (c k)").unsqueeze(2).to_broadcast([128, 16, 128])),
                    R=[b_sT], W=[b_sbc])
                P.op("dve", lambda: nc.vector.tensor_copy(sTb[:], sT[:]), R=[b_sT], W=[b_sbc])
                n = 0
                for i in range(4):
                    for k in range(3):
                        for v in range(3):
                            blk = 3 * k + v
                            wi = n % 3
                            bi = n % 2
                            n += 1
                            c0 = blk * 1024
                            P.dma("pool", wblk[wi][:], w_mod[i, :, c0:c0 + 1024].rearrange("(k p) n -> p k n", p=128),
                                  W=[b_w[wi]], owner=b_w[wi])
                            P.dma("sp", brow[bi][:], b_mod[i:i + 1, c0:c0 + 1024], W=[b_br[bi]], owner=b_br[bi])
                            if v < 2:
                                pm = banks[0][:, 0:16].rearrange("p (c v) -> p c v", v=2)
                                for ch in range(8):
                                    for kc in range(8):
                                        P.op("pe", lambda ch=ch, kc=kc: nc.tensor.matmul(
                                            pm[:, ch, :], lhsT=wblk[wi][:, kc, ch * 128:(ch + 1) * 128],
                                            rhs=sTb[:, :, kc], start=(kc == 0), stop=False),
                                            R=[b_w[wi], b_sbc], W=[bb[0]])
                                    P.op("pe", lambda ch=ch: nc.tensor.matmul(
                                        pm[:, ch, :], lhsT=brow[bi][0:1, ch * 128:(ch + 1) * 128],
                                        rhs=ones[0:1, 0:2], start=False, stop=True),
                                        R=[b_br[bi], b_ones], W=[bb[0]])
                                dst = AS[:, i * 3 + k, 1 - v, :, :]
                                if v == 0:
                                    P.op("dve", lambda dst=dst: nc.vector.tensor_copy(dst, pm), R=[bb[0]], W=[b_AS])
                                else:
                                    P.op("dve", lambda dst=dst: nc.vector.tensor_scalar(
                                        out=dst, in0=pm, scalar1=1.0, scalar2=None, op0=ALU.add), R=[bb[0]], W=[b_AS])
                            else:
                                fac = 1.0 if k == 1 else 0.5
                                for cvi in range(2):
                                    gi = cvi
                                    for half in range(2):
                                        bk = 1 + half
                                        for kc in range(8):
                                            P.op("pe", lambda kc=kc, half=half, bk=bk: nc.tensor.matmul(
                                                banks[bk][:], lhsT=sbc[:, cvi * 8 + kc, :],
                                                rhs=wblk[wi][:, kc, half * 512:(half + 1) * 512],
                                                start=(kc == 0), stop=False), R=[b_w[wi], b_sbc], W=[bb[bk]])
                                        P.op("pe", lambda half=half, bk=bk: nc.tensor.matmul(
                                            banks[bk][:], lhsT=ones[0:1, 0:128],
                                            rhs=brow[bi][0:1, half * 512:(half + 1) * 512],
                                            start=False, stop=True), R=[b_br[bi], b_ones], W=[bb[bk]])
                                        P.op("act", lambda half=half, bk=bk: nc.scalar.activation(
                                            out=grow[gi][:, half * 512:(half + 1) * 512], in_=banks[bk][:],
                                            func=AF.Copy, scale=fac), R=[bb[bk]], W=[b_gr[gi]])
                                    if k == 1 and i % 2 == 1:
                                        P.dma("sp", psc[:], pool_scale[i // 2:i // 2 + 1, :].partition_broadcast(128),
                                              W=[b_psc], owner=b_psc)
                                        P.op("dve", lambda: nc.vector.tensor_tensor(
                                            out=grow[gi][:], in0=grow[gi][:], in1=psc[:], op=ALU.mult),
                                            R=[b_gr[gi], b_psc], W=[b_gr[gi]])
                                    P.dma("sp", gsc[i, k, cvi:cvi + 1, :], grow[gi][0:1, :], R=[b_gr[gi]], W=[b_gsc],
                                          owner=b_gr[gi], kind="st")
                P.barrier()
                P.release(allb)

        def front_load(src, s, xt, b_xt):
            P.dma("sp", xt[:], xaps[src][s * 128:(s + 1) * 128, :], R=[xbufs[src][s]], W=[b_xt], owner=b_xt)

        def front_compute(xt, b_xt, xn, b_xn, gn, b_gn, st_, b_st, junk, b_junk):
            P.op("act", lambda: nc.scalar.activation(out=junk[:], in_=xt[:], func=AF.Square, accum_out=st_[:, 0:1]),
                 R=[b_xt], W=[b_junk, b_st])
            rstd(st_, b_st, 1, 1.0 / D)
            P.op("dve", lambda: nc.vector.scalar_tensor_tensor(out=xn[:], in0=xt[:], scalar=st_[:, 2:3], in1=gn[:],
                                                              op0=ALU.mult, op1=ALU.mult),
                 R=[b_xt, b_st, b_gn], W=[b_xn])

        def front(src, s, xt, b_xt, xn, b_xn, gn, b_gn, st_, b_st, junk, b_junk):
            front_load(src, s, xt, b_xt)
            front_compute(xt, b_xt, xn, b_xn, gn, b_gn, st_, b_st, junk, b_junk)

        def rstd(st_, b_st, n, inv):
            P.op("dve", lambda: nc.vector.tensor_scalar(out=st_[:, n:2 * n], in0=st_[:, 0:n], scalar1=inv, scalar2=EPS,
                                                        op0=ALU.mult, op1=ALU.add), R=[b_st], W=[b_st])
            P.op("act", lambda: nc.scalar.activation(out=st_[:, n:2 * n], in_=st_[:, n:2 * n], func=AF.Sqrt),
                 R=[b_st], W=[b_st])
            P.op("dve", lambda: nc.vector.reciprocal(out=st_[:, 2 * n:3 * n], in_=st_[:, n:2 * n]), R=[b_st], W=[b_st])

        def transpose_mod(xn, b_xn, hT_dst, b_hT, bank_i, ik):
            pT = banks[bank_i][:].bitcast(BF16).rearrange("p (k t) -> p k t", k=8)
            for kc in range(8):
                P.op("pe", lambda kc=kc: nc.tensor.transpose(out=pT[:, kc, :], in_=xn[:, kc * 128:(kc + 1) * 128],
                                                             identity=ident[:]), R=[b_xn, b_ident], W=[bb[bank_i]])
            return pT

        def residual_load(src, s, xr, b_xr):
            P.dma("sp", xr[:], xaps[src][s * 128:(s + 1) * 128, :], R=[xbufs[src][s]], W=[b_xr], owner=b_xr)

        def residual_cs(dst, s, cvi, psum_halves, G, b_G, xr, b_xr, tmp, b_tmp):
            for (bk, half) in psum_halves:
                sl = slice(half * 512, (half + 1) * 512)
                P.op("dve", lambda bk=bk, sl=sl, half=half: nc.vector.tensor_tensor(
                    out=tmp[half][:], in0=banks[bk][:], in1=G[cvi][:, sl], op=ALU.mult),
                    R=[bb[bk], b_G], W=[b_tmp[half]])
                P.op("pool", lambda sl=sl, half=half: nc.gpsimd.tensor_tensor(
                    out=xr[:, sl], in0=tmp[half][:], in1=xr[:, sl], op=ALU.add),
                    R=[b_tmp[half], b_xr], W=[b_xr])
            P.dma("sp", xaps[dst][s * 128:(s + 1) * 128, :], xr[:], R=[b_xr], W=[xbufs[dst][s]], owner=b_xr, kind="st")

        def residual(src, dst, s, cvi, psum_halves, G, b_G, xr, b_xr, tmp, b_tmp):
            residual_load(src, s, xr, b_xr)
            residual_cs(dst, s, cvi, psum_halves, G, b_G, xr, b_xr, tmp, b_tmp)

        def ffn_stage(stages):
            with ExitStack() as st:
                def sb(name, shape, dt):
                    return st.enter_context(nc.sbuf_tensor(uniq(name), list(shape), dt))
                w1 = sb("w1", [128, 8, FH], BF16)
                w3 = sb("w3", [128, 8, FH], BF16)
                w2 = sb("w2", [128, NF, D], BF16)
                G = [sb("G%d" % c, [128, D], F32) for c in range(2)]
                gn = sb("gn", [128, D], F32)
                xt = [sb("xt%d" % c, [128, D], F32) for c in range(2)]
                xr = [sb("xr%d" % c, [128, D], F32) for c in range(2)]
                xn = [sb("xn%d" % c, [128, D], BF16) for c in range(4)]
                sts = [sb("st%d" % c, [128, 4], F32) for c in range(2)]
                hT = sb("hT", [128, 8, 512], BF16)
                gT = sb("gT", [128, NF, 512], BF16)
                sa = [sb("sa%d" % c, [128, 512], F32) for c in range(2)]
                tmp0 = sb("tmp0", [128, 512], F32)
                tmp = [tmp0, tmp0]
                b_w1b = [Buf("w1_%d" % c) for c in range(4)]; b_w3b = [Buf("w3_%d" % c) for c in range(4)]
                b_w2 = Buf("w2"); b_G = Buf("G"); b_gn = Buf("gn")

                def wblks(bl, j):
                    return [bl[b] for b in sorted({(j * 128) // 704, (j * 128 + 127) // 704})]
                b_xt = [Buf("xt0"), Buf("xt1")]; b_xr = [Buf("xr0"), Buf("xr1")]; b_xn = [Buf("xn%d" % c) for c in range(4)]
                b_st = [Buf("st0"), Buf("st1")]
                b_hT = [Buf("hT%d" % c) for c in range(4)]
                b_gT = [Buf("gT%d" % c) for c in range(NF)]
                b_t0 = Buf("tmp0")
                b_sa = [Buf("sa0"), Buf("sa1")]; b_tmp = [b_t0, b_t0]
                allb = b_w1b + b_w3b + [b_w2, b_G, b_gn] + b_xt + b_xr + b_xn + b_st + b_hT + b_gT + b_sa + b_tmp
                NT = NSUB // 4
                NG = NT * len(stages)

                def ik_of(n):
                    i, which, _, _ = stages[n]
                    return i * 3 + (0 if which == 0 else 2)

                def load_w13(n):
                    i, which, _, _ = stages[n]
                    w1v = ffn_w1[i, which].rearrange("(k p) n -> p k n", p=128)
                    w3v = ffn_w3[i, which].rearrange("(k p) n -> p k n", p=128)
                    for blk in range(4):
                        cs = slice(blk * 704, (blk + 1) * 704)
                        P.dma("pool", w1[:, :, cs], w1v[:, :, cs], W=[b_w1b[blk]], owner=b_w1b[blk])
                        P.dma("pool", w3[:, :, cs], w3v[:, :, cs], W=[b_w3b[blk]], owner=b_w3b[blk])

                def load_w2(n):
                    i, which, _, _ = stages[n]
                    w2v = ffn_w2[i, which].rearrange("(j p) n -> p j n", p=128)
                    for jb in range(2):
                        js = slice(jb * 11, (jb + 1) * 11)
                        P.dma("pool", w2[:, js, :], w2v[:, js, :], W=[b_w2], owner=b_w2)

                def load_G(n):
                    i, which, _, _ = stages[n]
                    k = 0 if which == 0 else 2
                    for c in range(2):
                        P.dma("sp", G[c][:], gsc[i, k, c:c + 1, :].partition_broadcast(128), R=[b_gsc], W=[b_G], owner=b_G)

                def load_gn(n):
                    i, which, _, _ = stages[n]
                    k = 0 if which == 0 else 2
                    P.dma("sp", gn[:], norm_g[i, k:k + 1, :].partition_broadcast(128), W=[b_gn], owner=b_gn)

                def do_front_load(gt, q):
                    n, t = divmod(gt, NT)
                    front_load(stages[n][2], 4 * t + q, xt[q % 2], b_xt[q % 2])

                def do_front_compute(gt, q):
                    c = q % 2
                    front_compute(xt[c], b_xt[c], xn[q], b_xn[q], gn, b_gn, sts[c], b_st[c], xn[q], b_xn[q])

                def do_tr(gt, q):
                    n, t = divmod(gt, NT)
                    ik = ik_of(n)
                    cvi = cv_of(4 * t)
                    pT = transpose_mod(xn[q], b_xn[q], None, None, 0, ik)
                    for kc in range(8):
                        P.op("act", lambda kc=kc, q=q: nc.scalar.activation(
                            out=hT[:, kc, q * 128:(q + 1) * 128], in_=pT[:, kc, :], func=AF.Identity,
                            scale=AS[:, ik, 0, kc, cvi:cvi + 1], bias=AS[:, ik, 1, kc, cvi:cvi + 1]),
                            R=[bb[0], b_AS], W=[b_hT[q]])

                load_w13(0)
                load_w2(0)
                load_G(0)
                load_gn(0)
                for q in range(4):
                    do_front_load(0, q)
                    do_front_compute(0, q)
                for q in range(4):
                    do_tr(0, q)
                nr = [0]
                for gt in range(NG):
                    n, t = divmod(gt, NT)
                    _, _, src, dst = stages[n]
                    cvi = cv_of(4 * t)
                    nxt_ = gt + 1 < NG
                    if nxt_ and t == NT - 1:
                        load_gn(n + 1)
                    for j in range(NF):
                        if nxt_ and j < 16:
                            if j % 4 == 0:
                                do_front_load(gt + 1, j // 4)
                            elif j % 4 == 3:
                                do_front_compute(gt + 1, j // 4)
                        pa = 1 + 2 * (j % 2)
                        pb = pa + 1
                        for kc in range(8):
                            P.op("pe", lambda kc=kc, j=j, pa=pa: nc.tensor.matmul(
                                banks[pa][:], lhsT=w1[:, kc, j * 128:(j + 1) * 128], rhs=hT[:, kc, :],
                                start=(kc == 0), stop=(kc == 7)), R=wblks(b_w1b, j) + b_hT, W=[bb[pa]])
                        for kc in range(8):
                            P.op("pe", lambda kc=kc, j=j, pb=pb: nc.tensor.matmul(
                                banks[pb][:], lhsT=w3[:, kc, j * 128:(j + 1) * 128], rhs=hT[:, kc, :],
                                start=(kc == 0), stop=(kc == 7)), R=wblks(b_w3b, j) + b_hT, W=[bb[pb]])
                        sc = j % 2
                        P.op("act", lambda pa=pa, sc=sc: nc.scalar.activation(out=sa[sc][:], in_=banks[pa][:], func=AF.Silu),
                             R=[bb[pa]], W=[b_sa[sc]])
                        P.op("dve", lambda pb=pb, sc=sc, j=j: nc.vector.tensor_tensor(
                            out=gT[:, j, :], in0=banks[pb][:], in1=sa[sc][:], op=ALU.mult),
                            R=[bb[pb], b_sa[sc]], W=[b_gT[j]])
                    if nxt_ and t == NT - 1:
                        load_w13(n + 1)
                    for q in range(4):
                        s = 4 * t + q
                        c = nr[0] % 2
                        nr[0] += 1
                        if nxt_:
                            do_tr(gt + 1, q)
                        halves = []
                        for half in range(2):
                            bk = 5 + half
                            for j in range(NF):
                                P.op("pe", lambda j=j, q=q, half=half, bk=bk: nc.tensor.matmul(
                                    banks[bk][:], lhsT=gT[:, j, q * 128:(q + 1) * 128],
                                    rhs=w2[:, j, half * 512:(half + 1) * 512], start=(j == 0), stop=(j == NF - 1)),
                                    R=[b_gT[j], b_w2], W=[bb[bk]])
                            halves.append((bk, half))
                        residual(src, dst, s, cvi, halves, G, b_G, xr[c], b_xr[c], tmp, b_tmp)
                    if nxt_ and t == NT - 1:
                        load_w2(n + 1)
                        load_G(n + 1)
                P.barrier()
                P.release(allb)

        def pool_stage(i, src, dst):
            j = i // 2
            ik = i * 3 + 1
            with ExitStack() as st:
                def sb(name, shape, dt):
                    return st.enter_context(nc.sbuf_tensor(uniq(name), list(shape), dt))
                bandb = sb("bandb", [128, 4, 5, 128], BF16)
                pw = sb("pw", [128, 4, 2, 256], BF16)
                G = [sb("G%d" % c, [128, D], F32) for c in range(2)]
                gn = sb("gn", [128, D], F32)
                xt = [sb("xt%d" % c, [128, D], F32) for c in range(2)]
                xr = [sb("xr%d" % c, [128, D], F32) for c in range(2)]
                xn = sb("xnall", [128, 32, D], BF16)
                junk = sb("junk", [128, D], BF16)
                sts = [sb("st%d" % c, [128, 4], F32) for c in range(2)]
                dT = [sb("dT%d" % c, [128, 8, 128], BF16) for c in range(2)]
                tmp = [sb("tmp%d" % c, [128, 512], F32) for c in range(2)]
                b_band = Buf("band"); b_pw = Buf("pw"); b_G = Buf("G"); b_gn = Buf("gn")
                b_xt = [Buf("xt0"), Buf("xt1")]; b_xr = [Buf("xr0"), Buf("xr1")]
                b_xn = [Buf("xn%d" % c) for c in range(32)]
                b_junk = Buf("junk"); b_st = [Buf("st0"), Buf("st1")]
                b_dT = [Buf("dT0"), Buf("dT1")]; b_tmp = [Buf("tmp0"), Buf("tmp1")]
                allb = [b_band, b_pw, b_G, b_gn, b_junk] + b_xt + b_xr + b_xn + b_st + b_dT + b_tmp
                P.dma("pool", bandb[:].rearrange("p a b c -> p (a b c)"), band.rearrange("p a b c -> p (a b c)"),
                      W=[b_band], owner=b_band)
                for g in range(4):
                    P.dma("pool", pw[:, g, :, :], pool_w[j, g].rearrange("(k p) n -> p k n", p=128), W=[b_pw], owner=b_pw)
                for c in range(2):
                    P.dma("sp", G[c][:], gsc[i, 1, c:c + 1, :].partition_broadcast(128), R=[b_gsc], W=[b_G], owner=b_G)
                P.dma("sp", gn[:], norm_g[i, 1:2, :].partition_broadcast(128), W=[b_gn], owner=b_gn)
                cnt = [0, 0]
                seqs = [[2 * q, 2 * q + 1] for q in range(4)] + [list(range(8, 40))]
                for subs in seqs:
                    n = len(subs)
                    cvi = cv_of(subs[0])

                    def fr(idx):
                        c = cnt[0] % 2
                        cnt[0] += 1
                        front(src, subs[idx], xt[c], b_xt[c], xn[:, idx, :], b_xn[idx], gn, b_gn, sts[c], b_st[c], junk, b_junk)
                    def g_sub(idx, c):
                        first = idx == 0
                        last = idx == n - 1
                        bo = 4 * c
                        for cc in range(8):
                            g = cc // 2
                            bk = bo + cc // 4
                            outp = banks[bk][:, (cc % 4) * 128:(cc % 4 + 1) * 128]
                            nbs = []
                            if not first:
                                nbs.append((idx - 1, 3))
                            nbs.append((idx, 1 if first else (2 if last else 0)))
                            if not last:
                                nbs.append((idx + 1, 4))
                            for m, (ni, var) in enumerate(nbs):
                                P.op("pe", lambda ni=ni, var=var, cc=cc, g=g, outp=outp, m=m, L=len(nbs): nc.tensor.matmul(
                                    outp, lhsT=xn[:, ni, cc * 128:(cc + 1) * 128], rhs=bandb[:, g, var, :],
                                    start=(m == 0), stop=(m == L - 1)), R=[b_xn[ni], b_band], W=[bb[bk]])
                            if cc % 4 == 3:
                                yield
                        for cc in range(8):
                            bk = bo + cc // 4
                            P.op("act", lambda cc=cc, bk=bk, c=c: nc.scalar.activation(
                                out=dT[c][:, cc, :], in_=banks[bk][:, (cc % 4) * 128:(cc % 4 + 1) * 128], func=AF.Copy,
                                scale=AS[:, ik, 0, cc, cvi:cvi + 1]), R=[bb[bk], b_AS], W=[b_dT[c]])
                            if cc % 2 == 1:
                                yield
                        halves = []
                        for half in range(2):
                            bk = bo + 2 + half
                            for gg in range(2):
                                g = half * 2 + gg
                                for kk in range(2):
                                    P.op("pe", lambda g=g, gg=gg, kk=kk, bk=bk, c=c: nc.tensor.matmul(
                                        banks[bk][:, gg * 256:(gg + 1) * 256], lhsT=dT[c][:, 2 * g + kk, :],
                                        rhs=pw[:, g, kk, :], start=(kk == 0), stop=(kk == 1)),
                                        R=[b_dT[c], b_pw], W=[bb[bk]])
                            halves.append((bk, half))
                        yield
                        residual(src, dst, subs[idx], cvi, halves, G, b_G, xr[c], b_xr[c], tmp, b_tmp)
                        yield

                    for k0 in range(min(4, n)):
                        fr(k0)
                    for idx in range(0, n, 2):
                        for k0 in (idx + 4, idx + 5):
                            if k0 < n:
                                fr(k0)
                        gens = [g_sub(idx, 0)]
                        if idx + 1 < n:
                            gens.append(g_sub(idx + 1, 1))
                        while gens:
                            for gg_ in list(gens):
                                try:
                                    next(gg_)
                                except StopIteration:
                                    gens.remove(gg_)
                P.barrier()
                P.release(allb)

        def interleave(gens):
            gens = list(gens)
            while gens:
                for g in list(gens):
                    try:
                        next(g)
                    except StopIteration:
                        gens.remove(g)

        def mla_stage(i, src, dst):
            j = i // 2
            ik = i * 3 + 1
            with ExitStack() as st:
                def sb(name, shape, dt):
                    return st.enter_context(nc.sbuf_tensor(uniq(name), list(shape), dt))
                win = sb("win", [128, 8, 1440], BF16)
                wqb = sb("wqb", [128, 2, 768], BF16)
                wkvb = sb("wkvb", [128, 1024], BF16)
                woA = sb("woA", [64, 8, D], BF16)
                woG = sb("woG", [128, 4, D], BF16)
                wsf = sb("wsf", [128, 4, 128], BF16)
                wsT = sb("wsT", [128, 4, 128], BF16)
                qan = sb("qan", [128, 256], F32)
                kvan = sb("kvan", [128, 128], F32)
                qnr = sb("qnr", [128, 96], F32)
                knr = sb("knr", [128, 96], F32)
                vnr = sb("vnr", [128, 512], F32)
                gb = sb("gb", [128, 4], F32)
                cst = sb("cst", [128, 8], F32)
                rC = sb("rC", [128, 32, 16], F32)
                rS = sb("rS", [128, 32, 16], F32)
                Gt = sb("Gt", [128, D], F32)
                G = [Gt, Gt]
                gn = sb("gn", [128, D], F32)
                KT = sb("KT", [96, 4, 4608], BF16)
                V = sb("V", [128, 36, 4, 65], BF16)
                xr = sb("xr", [128, D], F32)
                QT = sb("QT", [96, 4, 512], BF16)
                goutT = sb("goutT", [128, 4, 512], BF16)
                attnT = sb("attnT", [64, 4, 512], BF16)
                pT = [sb("pT%d" % c, [128, 1024], BF16) for c in range(2)]
                tmp0 = sb("tmp0", [128, 512], F32)
                tmp = [tmp0, tmp0]
                rr = tmp0
                ob = xr[0:64, 0:512]
                names = ["win", "wqb", "wkvb", "woA", "woG", "wsf", "wsT", "rows", "gb", "cst", "rope", "G", "gn", "KT", "V",
                         "xr", "QT", "goutT", "attnT", "pT0", "pT1", "pT2", "ob", "rr", "tmp0"]
                B = {nm: Buf(nm) for nm in names}
                b_tmp = [B["tmp0"], B["tmp0"]]
                b_pT = [B["pT0"], B["pT1"]]
                B["rr"] = B["tmp0"]
                B["ob"] = B["xr"]

                class WS:
                    pass
                wss = []
                for wi in range(2):
                    w = WS()
                    w.xt = sb("xt", [128, D], F32)
                    w.xn = sb("xn", [128, D], BF16)
                    w.sts = sb("st", [128, 4], F32)
                    w.stg = sb("stg", [128, 4], F32)
                    w.st4 = sb("st4", [128, 12], F32)
                    w.hTs = sb("hTs", [128, 8, 128], BF16)
                    w.kvf = sb("kvf", [128, 128], F32)
                    w.krf = sb("krf", [128, 32], F32)
                    w.kvb = sb("kvb", [128, 256], BF16)
                    w.lT = sb("lT", [128, 2, 128], BF16)
                    w.kf = sb("kf", [128, 4, 96], F32)
                    w.kb = sb("kb", [128, 4, 96], BF16)
                    w.rt = [sb("rt%d" % c, [128, 4, 2, 8], F32) for c in range(4)]
                    w.ug = sb("ug", [128, 512], F32)
                    w.g1 = sb("g1", [128, 512], F32)
                    w.g2 = sb("g2", [128, 512], F32)
                    w.vg = w.g1
                    w.junk = sb("junk", [128, 384], F32)
                    w.vnb = sb("vnb", [128, 512], BF16)
                    w.gout = sb("gout", [128, 512], BF16)
                    w.B = {nm: Buf(nm + str(wi)) for nm in ("xt", "xn", "st", "st4", "hTs", "kvf", "krf", "kvb", "lT",
                                                            "kf", "kb", "rt", "ug", "g1", "g2", "vnb", "gout")}
                    w.B["vg"] = w.B["g1"]
                    w.B["junk"] = Buf("junk%d" % wi)
                    w.B["stg"] = Buf("stg%d" % wi)
                    wss.append(w)
                for wi in range(2, 4):
                    w = WS()
                    w.junk = sb("junk", [128, 384], F32)
                    w.st4 = sb("st4", [128, 12], F32)
                    w.krf = sb("krf", [128, 32], F32)
                    w.lT = sb("lT", [128, 2, 128], BF16)
                    w.kf = sb("kf", [128, 4, 96], F32)
                    w.kb = sb("kb", [128, 4, 96], BF16)
                    w.rt = [sb("rt%d" % c, [128, 4, 2, 8], F32) for c in range(4)]
                    w.B = {nm: Buf(nm + str(wi)) for nm in ("junk", "st4", "krf", "lT", "kf", "kb", "rt")}
                    wss.append(w)
                kd_bufs = [Buf("kd%d" % c) for c in range(44)]
                qd_bufs = [Buf("qd%d" % c) for c in range(40)]

                def set_banks(pass_no):
                    for wi, w in enumerate(wss):
                        if pass_no == 0:
                            if wi < 2:
                                w.bT, w.bP, w.bU, w.bQ = (0, 1, 2, 3) if wi == 0 else (4, 5, 6, 7)
                            else:
                                continue
                        else:
                            w.bT, w.bQ = 2 * wi, 2 * wi + 1
                        w.pTb = banks[w.bT][:].bitcast(BF16).rearrange("p (k t) -> p k t", k=8)
                set_banks(0)
                for kc in range(8):
                    P.dma("pool", win[:, kc, :], w_in[j, kc * 128:(kc + 1) * 128, :], W=[B["win"]], owner=B["win"])
                P.dma("pool", wqb[:], w_qb[j].rearrange("(k p) n -> p k n", p=128), W=[B["wqb"]], owner=B["wqb"])
                P.dma("pool", wkvb[:], w_kvb[j], W=[B["wkvb"]], owner=B["wkvb"])
                P.dma("pool", woA[:], w_out[j, 0:512, :].rearrange("(h p) n -> p h n", p=64), W=[B["woA"]], owner=B["woA"])
                P.dma("pool", woG[:], w_out[j, 512:1024, :].rearrange("(k p) n -> p k n", p=128), W=[B["woG"]], owner=B["woG"])
                P.dma("pool", wsf[:], gmlp_ws[j].rearrange("g p q -> p g q"), W=[B["wsf"]], owner=B["wsf"])
                P.dma("sp", qan[:], q_a_norm[j:j + 1, :].partition_broadcast(128), W=[B["rows"]], owner=B["rows"])
                P.dma("sp", kvan[:], kv_a_norm[j:j + 1, :].partition_broadcast(128), W=[B["rows"]], owner=B["rows"])
                P.dma("sp", qnr[:], q_norm[j:j + 1, :].partition_broadcast(128), W=[B["rows"]], owner=B["rows"])
                P.dma("sp", knr[:], k_norm[j:j + 1, :].partition_broadcast(128), W=[B["rows"]], owner=B["rows"])
                P.dma("sp", vnr[:], gmlp_v_norm[j:j + 1, :].partition_broadcast(128), W=[B["rows"]], owner=B["rows"])
                with nc.allow_non_contiguous_dma(reason="tiny transposed bias load"):
                    P.dma("sp", gb[:], gmlp_b[j].rearrange("g p -> p g"), W=[B["gb"]], owner=B["gb"])
                P.dma("sp", rC[:], ropec.rearrange("(s p) f -> p s f", p=128), W=[B["rope"]], owner=B["rope"])
                P.dma("sp", rS[:], ropes.rearrange("(s p) f -> p s f", p=128), W=[B["rope"]], owner=B["rope"])
                P.dma("sp", Gt[:], gsc[i, 1, 0:1, :].partition_broadcast(128), R=[b_gsc], W=[B["G"]], owner=B["G"])
                P.dma("sp", gn[:], norm_g[i, 1:2, :].partition_broadcast(128), W=[B["gn"]], owner=B["gn"])
                w0 = wss[0]
                for g in range(4):
                    P.op("pe", lambda g=g: nc.tensor.transpose(out=w0.pTb[:, g, :], in_=wsf[:, g, :], identity=ident[:]),
                         R=[B["wsf"], b_ident], W=[bb[w0.bT]])
                P.op("dve", lambda: nc.vector.tensor_copy(wsT[:], w0.pTb[:, 0:4, :]), R=[bb[w0.bT]], W=[B["wsT"]])
                P.op("act", lambda: nc.scalar.activation(out=w0.junk[:, 0:96], in_=qnr[:], func=AF.Square), R=[B["rows"]], W=[w0.B["junk"]])
                P.op("act", lambda: nc.scalar.activation(out=w0.junk[:, 96:192], in_=knr[:], func=AF.Square), R=[B["rows"]], W=[w0.B["junk"]])
                P.op("dve", lambda: nc.vector.tensor_reduce(out=cst[:, 0:2], in_=w0.junk[:, 0:192].rearrange("p (a b) -> p a b", a=2),
                                                            axis=AX.X, op=ALU.max), R=[w0.B["junk"]], W=[B["cst"]])
                P.op("dve", lambda: nc.vector.tensor_tensor(out=cst[:, 2:3], in0=cst[:, 0:1], in1=cst[:, 1:2], op=ALU.mult),
                     R=[B["cst"]], W=[B["cst"]])
                P.op("act", lambda: nc.scalar.activation(out=cst[:, 3:4], in_=cst[:, 2:3], func=AF.Sqrt), R=[B["cst"]], W=[B["cst"]])
                P.op("dve", lambda: nc.vector.tensor_scalar(out=cst[:, 4:5], in0=cst[:, 3:4], scalar1=-(96.0 ** 0.5), scalar2=None,
                                                            op0=ALU.mult), R=[B["cst"]], W=[B["cst"]])
                P.op("dve", lambda: nc.vector.tensor_scalar(out=qnr[:], in0=qnr[:], scalar1=96.0 ** -0.5, scalar2=None, op0=ALU.mult),
                     R=[B["rows"], w0.B["junk"]], W=[B["rows"]])
                P.op("pool", lambda: nc.gpsimd.memset(V[:], 1.0), W=[B["V"]])

                def g_rstd(st_, b_st, n, inv):
                    P.op("dve", lambda: nc.vector.tensor_scalar(out=st_[:, n:2 * n], in0=st_[:, 0:n], scalar1=inv, scalar2=EPS,
                                                                op0=ALU.mult, op1=ALU.add), R=[b_st], W=[b_st])
                    yield
                    P.op("act", lambda: nc.scalar.activation(out=st_[:, n:2 * n], in_=st_[:, n:2 * n], func=AF.Sqrt),
                         R=[b_st], W=[b_st])
                    yield
                    P.op("dve", lambda: nc.vector.reciprocal(out=st_[:, 2 * n:3 * n], in_=st_[:, n:2 * n]), R=[b_st], W=[b_st])
                    yield

                def g_front(w, s, cvi):
                    WB = w.B
                    P.dma("sp", w.xt[:], xaps[src][s * 128:(s + 1) * 128, :], R=[xbufs[src][s]], W=[WB["xt"]], owner=WB["xt"])
                    yield
                    P.op("act", lambda: nc.scalar.activation(out=w.xn[:], in_=w.xt[:], func=AF.Square, accum_out=w.sts[:, 0:1]),
                         R=[WB["xt"]], W=[WB["xn"], WB["st"]])
                    yield
                    yield from g_rstd(w.sts, WB["st"], 1, 1.0 / D)
                    P.op("dve", lambda: nc.vector.scalar_tensor_tensor(out=w.xn[:], in0=w.xt[:], scalar=w.sts[:, 2:3], in1=gn[:],
                                                                      op0=ALU.mult, op1=ALU.mult),
                         R=[WB["xt"], WB["st"], B["gn"]], W=[WB["xn"]])
                    yield
                    for kc in range(8):
                        P.op("pe", lambda kc=kc: nc.tensor.transpose(out=w.pTb[:, kc, :], in_=w.xn[:, kc * 128:(kc + 1) * 128],
                                                                     identity=ident[:]), R=[WB["xn"], b_ident], W=[bb[w.bT]])
                    yield
                    for kc in range(8):
                        eng = "act" if kc % 2 == 0 else "dve"
                        if eng == "act":
                            P.op("act", lambda kc=kc: nc.scalar.activation(
                                out=w.hTs[:, kc, :], in_=w.pTb[:, kc, :], func=AF.Identity,
                                scale=AS[:, ik, 0, kc, cvi:cvi + 1], bias=AS[:, ik, 1, kc, cvi:cvi + 1]),
                                R=[bb[w.bT], b_AS], W=[WB["hTs"]])
                        else:
                            P.op("dve", lambda kc=kc: nc.vector.tensor_scalar(
                                out=w.hTs[:, kc, :], in0=w.pTb[:, kc, :], scalar1=AS[:, ik, 0, kc, cvi:cvi + 1],
                                scalar2=AS[:, ik, 1, kc, cvi:cvi + 1], op0=ALU.mult, op1=ALU.add),
                                R=[bb[w.bT], b_AS], W=[WB["hTs"]])
                        if kc % 2 == 1:
                            yield

                def g_headnorm(w, row, rope_idx):
                    WB = w.B
                    T = w.kf
                    bT_ = WB["kf"]
                    P.op("act", lambda: nc.scalar.activation(out=w.junk[:, 0:384], in_=T[:].rearrange("p h d -> p (h d)"), func=AF.Square),
                         R=[bT_], W=[WB["junk"]])
                    yield
                    P.op("dve", lambda: nc.vector.tensor_reduce(out=w.st4[:, 0:4], in_=w.junk[:, 0:384].rearrange("p (h d) -> p h d", h=4),
                                                                axis=AX.X, op=ALU.add), R=[WB["junk"]], W=[WB["st4"]])
                    yield
                    yield from g_rstd(w.st4, WB["st4"], 4, 1.0 / 96)
                    P.op("dve", lambda: nc.vector.tensor_tensor(out=T[:], in0=T[:], in1=w.st4[:, 8:12].unsqueeze(2).to_broadcast([128, 4, 96]),
                                                                op=ALU.mult), R=[bT_, WB["st4"]], W=[bT_])
                    yield
                    P.op("dve", lambda: nc.vector.tensor_tensor(out=T[:], in0=T[:], in1=row[:].unsqueeze(1).to_broadcast([128, 4, 96]),
                                                                op=ALU.mult), R=[bT_, B["rows"]], W=[bT_])
                    yield
                    if rope_idx is not None:
                        r = T[:, :, 64:96].rearrange("p h (a t f) -> p h a t f", a=2, t=2)
                        x1 = r[:, :, :, 0, :]
                        x2 = r[:, :, :, 1, :]
                        cb = rC[:, rope_idx, :].rearrange("p (a f) -> p a f", a=2).unsqueeze(1).to_broadcast([128, 4, 2, 8])
                        sbk = rS[:, rope_idx, :].rearrange("p (a f) -> p a f", a=2).unsqueeze(1).to_broadcast([128, 4, 2, 8])
                        rt = w.rt
                        P.op("dve", lambda: nc.vector.tensor_tensor(out=rt[0][:], in0=x1, in1=cb, op=ALU.mult), R=[bT_, B["rope"]], W=[WB["rt"]])
                        P.op("pool", lambda: nc.gpsimd.tensor_tensor(out=rt[1][:], in0=x2, in1=sbk, op=ALU.mult), R=[bT_, B["rope"]], W=[WB["rt"]])
                        P.op("dve", lambda: nc.vector.tensor_tensor(out=rt[2][:], in0=x1, in1=sbk, op=ALU.mult), R=[bT_, B["rope"]], W=[WB["rt"]])
                        P.op("pool", lambda: nc.gpsimd.tensor_tensor(out=rt[3][:], in0=x2, in1=cb, op=ALU.mult), R=[bT_, B["rope"]], W=[WB["rt"]])
                        yield
                        P.op("dve", lambda: nc.vector.tensor_tensor(out=x1, in0=rt[0][:], in1=rt[1][:], op=ALU.subtract), R=[WB["rt"]], W=[bT_])
                        P.op("dve", lambda: nc.vector.tensor_tensor(out=x2, in0=rt[2][:], in1=rt[3][:], op=ALU.add), R=[WB["rt"]], W=[bT_])
                        yield

                def g_kpath(w, chunk, rope_idx, h0, save_idx=None, load_idx=None):
                    WB = w.B
                    if load_idx is None:
                        P.op("act", lambda: nc.scalar.copy(out=w.kvb[:, 0:128], in_=w.kvf[:]), R=[WB["kvf"]], W=[WB["kvb"]])
                        yield
                        P.op("pe", lambda: nc.tensor.transpose(out=w.pTb[:, 0, :], in_=w.kvb[:, 0:128], identity=ident[:]),
                             R=[WB["kvb"], b_ident], W=[bb[w.bT]])
                        yield
                        P.op("dve", lambda: nc.vector.tensor_copy(w.lT[:, 0, :], w.pTb[:, 0, :]), R=[bb[w.bT]], W=[WB["lT"]])
                        yield
                        if save_idx is not None:
                            P.dma("sp", klT_d[save_idx], w.lT[:, 0, :], R=[WB["lT"]], W=[kd_bufs[save_idx]], owner=WB["lT"], kind="st")
                            P.dma("sp", krf_d[save_idx], w.krf[:], R=[WB["krf"]], W=[kd_bufs[save_idx]], owner=WB["krf"], kind="st")
                    else:
                        P.dma("sp", w.lT[:, 0, :], klT_d[load_idx], R=[kd_bufs[load_idx]], W=[WB["lT"]], owner=WB["lT"])
                        P.dma("sp", w.krf[:], krf_d[load_idx], R=[kd_bufs[load_idx]], W=[WB["krf"]], owner=WB["krf"])
                        yield
                    P.op("pe", lambda: nc.tensor.matmul(banks[w.bQ][:], lhsT=w.lT[:, 0, :], rhs=wkvb[:, h0 * 128:(h0 + 4) * 128],
                                                       start=True, stop=True), R=[WB["lT"], B["wkvb"]], W=[bb[w.bQ]])
                    yield
                    kvv = banks[w.bQ][:].rearrange("p (h d) -> p h d", h=4)
                    P.op("act", lambda: nc.scalar.copy(out=w.kf[:, :, 0:64], in_=kvv[:, :, 0:64]), R=[bb[w.bQ]], W=[WB["kf"]])
                    P.op("dve", lambda: nc.vector.tensor_copy(w.kf[:, :, 64:96], w.krf[:].unsqueeze(1).to_broadcast([128, 4, 32])),
                         R=[WB["krf"]], W=[WB["kf"]])
                    P.op("dve", lambda: nc.vector.tensor_copy(V[:, chunk, :, 0:64], kvv[:, :, 64:128]), R=[bb[w.bQ]], W=[B["V"]])
                    yield
                    yield from g_headnorm(w, knr, rope_idx)
                    P.op("act", lambda: nc.scalar.copy(out=w.kb[:], in_=w.kf[:]), R=[WB["kf"]], W=[WB["kb"]])
                    yield
                    for hh in range(4):
                        P.op("pe", lambda hh=hh: nc.tensor.transpose(out=w.pTb[0:96, hh, :], in_=w.kb[:, hh, :], identity=ident[:]),
                             R=[WB["kb"], b_ident], W=[bb[w.bT]])
                    yield
                    P.op("dve", lambda: nc.vector.tensor_copy(KT[:, :, chunk * 128:(chunk + 1) * 128], w.pTb[0:96, 0:4, :]),
                         R=[bb[w.bT]], W=[B["KT"]])
                    yield

                def g_cache(w, c, h0):
                    WB = w.B
                    P.dma("sp", w.kvf[:], cckv[j, c * 128:(c + 1) * 128, :], W=[WB["kvf"]], owner=WB["kvf"])
                    P.dma("sp", w.krf[:], ckro[j, c * 128:(c + 1) * 128, :], W=[WB["krf"]], owner=WB["krf"])
                    yield
                    yield from g_kpath(w, c, None, h0, save_idx=40 + c)

                def g_phaseA(w, s, idx, chunk, cvi, rope_idx, h0, out_seq):
                    WB = w.B
                    yield from g_front(w, s, cvi)
                    for kc in range(8):
                        P.op("pe", lambda kc=kc: nc.tensor.matmul(banks[w.bP][:, 0:160], lhsT=w.hTs[:, kc, :], rhs=win[:, kc, 256:416],
                                                                  start=(kc == 0), stop=(kc == 7)), R=[WB["hTs"], B["win"]], W=[bb[w.bP]])
                    yield
                    P.op("act", lambda: nc.scalar.activation(out=w.junk[:, 0:128], in_=banks[w.bP][:, 0:128], func=AF.Square,
                                                             accum_out=w.sts[:, 0:1]), R=[bb[w.bP]], W=[WB["junk"], WB["st"]])
                    yield
                    yield from g_rstd(w.sts, WB["st"], 1, 1.0 / 128)
                    P.op("dve", lambda: nc.vector.scalar_tensor_tensor(out=w.kvf[:], in0=banks[w.bP][:, 0:128], scalar=w.sts[:, 2:3],
                                                                      in1=kvan[:], op0=ALU.mult, op1=ALU.mult),
                         R=[bb[w.bP], WB["st"], B["rows"]], W=[WB["kvf"]])
                    P.op("act", lambda: nc.scalar.copy(out=w.krf[:], in_=banks[w.bP][:, 128:160]), R=[bb[w.bP]], W=[WB["krf"]])
                    yield
                    if out_seq is not None:
                        P.dma("sp", ckv_o[out_seq, j, idx * 128:(idx + 1) * 128, :], w.kvf[:], R=[WB["kvf"]], owner=WB["kvf"], kind="st")
                        P.dma("sp", kro_o[out_seq, j, idx * 128:(idx + 1) * 128, :], w.krf[:], R=[WB["krf"]], owner=WB["krf"], kind="st")
                        yield
                    yield from g_kpath(w, chunk, rope_idx, h0, save_idx=s)

                def g_phaseB1(w, s, q, rope_idx, h0):
                    WB = w.B
                    P.dma("sp", w.lT[:].rearrange("p k t -> p (k t)"), qlT_d[s], R=[qd_bufs[s]], W=[WB["lT"]], owner=WB["lT"])
                    yield
                    yield from g_qtail(w, q, rope_idx, h0)

                def g_qtail(w, q, rope_idx, h0):
                    WB = w.B
                    for kc in range(2):
                        P.op("pe", lambda kc=kc: nc.tensor.matmul(banks[w.bQ][:, 0:384], lhsT=w.lT[:, kc, :],
                                                                  rhs=wqb[:, kc, h0 * 96:(h0 + 4) * 96],
                                                                  start=(kc == 0), stop=(kc == 1)), R=[WB["lT"], B["wqb"]], W=[bb[w.bQ]])
                    yield
                    P.op("act", lambda: nc.scalar.copy(out=w.kf[:].rearrange("p h d -> p (h d)"), in_=banks[w.bQ][:, 0:384]),
                         R=[bb[w.bQ]], W=[WB["kf"]])
                    yield
                    yield from g_headnorm(w, qnr, rope_idx)
                    P.op("act", lambda: nc.scalar.copy(out=w.kb[:], in_=w.kf[:]), R=[WB["kf"]], W=[WB["kb"]])
                    yield
                    for hh in range(4):
                        P.op("pe", lambda hh=hh: nc.tensor.transpose(out=w.pTb[0:96, hh, :], in_=w.kb[:, hh, :], identity=ident[:]),
                             R=[WB["kb"], b_ident], W=[bb[w.bT]])
                    yield
                    P.op("dve", lambda: nc.vector.tensor_copy(QT[:, :, q * 128:(q + 1) * 128], w.pTb[0:96, 0:4, :]),
                         R=[bb[w.bT]], W=[B["QT"]])
                    yield

                def g_qpart(w, s, q, rope_idx, h0):
                    WB = w.B
                    for kc in range(8):
                        P.op("pe", lambda kc=kc: nc.tensor.matmul(banks[w.bP][:, 0:256], lhsT=w.hTs[:, kc, :], rhs=win[:, kc, 0:256],
                                                                  start=(kc == 0), stop=(kc == 7)), R=[WB["hTs"], B["win"]], W=[bb[w.bP]])
                    yield
                    P.op("act", lambda: nc.scalar.activation(out=w.junk[:, 0:256], in_=banks[w.bP][:, 0:256], func=AF.Square,
                                                             accum_out=w.sts[:, 0:1]), R=[bb[w.bP]], W=[WB["junk"], WB["st"]])
                    yield
                    yield from g_rstd(w.sts, WB["st"], 1, 1.0 / 256)
                    P.op("dve", lambda: nc.vector.scalar_tensor_tensor(out=w.kvb[:], in0=banks[w.bP][:, 0:256], scalar=w.sts[:, 2:3],
                                                                      in1=qan[:], op0=ALU.mult, op1=ALU.mult),
                         R=[bb[w.bP], WB["st"], B["rows"]], W=[WB["kvb"]])
                    yield
                    for kc in range(2):
                        P.op("pe", lambda kc=kc: nc.tensor.transpose(out=w.pTb[:, kc, :], in_=w.kvb[:, kc * 128:(kc + 1) * 128],
                                                                     identity=ident[:]), R=[WB["kvb"], b_ident], W=[bb[w.bT]])
                    yield
                    P.op("dve", lambda: nc.vector.tensor_copy(w.lT[:], w.pTb[:, 0:2, :]), R=[bb[w.bT]], W=[WB["lT"]])
                    yield
                    P.dma("sp", qlT_d[s], w.lT[:].rearrange("p k t -> p (k t)"), R=[WB["lT"]], W=[qd_bufs[s]], owner=WB["lT"], kind="st")
                    yield from g_qtail(w, q, rope_idx, h0)

                def g_gpart(w, s, q):
                    WB = w.B
                    for blk, dstt, bd in ((0, w.ug, WB["ug"]), (1, w.vg, WB["vg"])):
                        for kc in range(8):
                            P.op("pe", lambda kc=kc, blk=blk: nc.tensor.matmul(
                                banks[w.bU][:], lhsT=w.hTs[:, kc, :], rhs=win[:, kc, 416 + blk * 512:928 + blk * 512],
                                start=(kc == 0), stop=(kc == 7)), R=[WB["hTs"], B["win"]], W=[bb[w.bU]])
                        yield
                        P.op("act", lambda: nc.scalar.activation(out=w.g1[:], in_=banks[w.bU][:], func=AF.Square),
                             R=[bb[w.bU]], W=[WB["g1"]])
                        yield
                        P.op("dve", lambda: nc.vector.tensor_scalar(out=w.g1[:], in0=w.g1[:], scalar1=0.044715, scalar2=1.0,
                                                                    op0=ALU.mult, op1=ALU.add), R=[WB["g1"]], W=[WB["g1"]])
                        yield
                        P.op("dve", lambda: nc.vector.tensor_tensor(out=w.g2[:], in0=banks[w.bU][:], in1=w.g1[:], op=ALU.mult),
                             R=[bb[w.bU], WB["g1"]], W=[WB["g2"]])
                        yield
                        P.op("act", lambda: nc.scalar.activation(out=w.g2[:], in_=w.g2[:], func=AF.Sigmoid, scale=1.5957691216057308),
                             R=[WB["g2"]], W=[WB["g2"]])
                        yield
                        P.op("dve", lambda dstt=dstt: nc.vector.tensor_tensor(out=dstt[:], in0=banks[w.bU][:], in1=w.g2[:], op=ALU.mult),
                             R=[bb[w.bU], WB["g2"]], W=[bd])
                        yield
                    P.op("act", lambda: nc.scalar.activation(out=w.g2[:], in_=w.vg[:], func=AF.Square, accum_out=w.stg[:, 0:1]),
                         R=[WB["vg"]], W=[WB["g2"], WB["stg"]])
                    yield
                    yield from g_rstd(w.stg, WB["stg"], 1, 1.0 / 512)
                    P.op("dve", lambda: nc.vector.scalar_tensor_tensor(out=w.vnb[:], in0=w.vg[:], scalar=w.stg[:, 2:3], in1=vnr[:],
                                                                      op0=ALU.mult, op1=ALU.mult),
                         R=[WB["vg"], WB["stg"], B["rows"]], W=[WB["vnb"]])
                    yield
                    for g in range(4):
                        P.op("pe", lambda g=g: nc.tensor.matmul(banks[w.bU][:, g * 128:(g + 1) * 128], lhsT=wsT[:, g, :],
                                                                rhs=w.vnb[:, g * 128:(g + 1) * 128], start=True, stop=True),
                             R=[B["wsT"], WB["vnb"]], W=[bb[w.bU]])
                    yield
                    for g in range(4):
                        P.op("dve", lambda g=g: nc.vector.scalar_tensor_tensor(
                            out=w.gout[:, g * 128:(g + 1) * 128], in0=banks[w.bU][:, g * 128:(g + 1) * 128], scalar=gb[:, g:g + 1],
                            in1=w.ug[:, g * 128:(g + 1) * 128], op0=ALU.add, op1=ALU.mult),
                            R=[bb[w.bU], B["gb"], WB["ug"]], W=[WB["gout"]])
                    yield
                    for g in range(4):
                        P.op("pe", lambda g=g: nc.tensor.transpose(out=w.pTb[:, 4 + g, :], in_=w.gout[:, g * 128:(g + 1) * 128],
                                                                   identity=ident[:]), R=[WB["gout"], b_ident], W=[bb[w.bT]])
                    yield
                    P.op("dve", lambda: nc.vector.tensor_copy(goutT[:, :, q * 128:(q + 1) * 128], w.pTb[:, 4:8, :]),
                         R=[bb[w.bT]], W=[B["goutT"]])
                    yield


                def g_phaseB(w, s, q, cvi, rope_idx, h0, pass_no):
                    yield from g_front(w, s, cvi)
                    parts = [g_qpart(w, s, q, rope_idx, h0)]
                    if pass_no == 0:
                        parts.append(g_gpart(w, s, q))
                    while parts:
                        for g in list(parts):
                            try:
                                next(g)
                            except StopIteration:
                                parts.remove(g)
                        yield

                def run_seq(subs, ncache, use_rope, pass_no, seq_id):
                    h0 = 4 * pass_no
                    cvi = cv_of(subs[0])
                    n = len(subs)
                    nchunks = n + ncache
                    base = src if pass_no == 0 else dst
                    set_banks(pass_no)
                    jobs = []
                    for c in range(ncache):
                        jobs.append(("c", c))
                    for idx, s in enumerate(subs):
                        jobs.append(("s", idx))
                    if pass_no == 0:
                        for p0 in range(0, len(jobs), 2):
                            gens = []
                            for wi, jb in enumerate(jobs[p0:p0 + 2]):
                                if jb[0] == "c":
                                    gens.append(g_cache(wss[wi], jb[1], h0))
                                else:
                                    idx = jb[1]
                                    s = subs[idx]
                                    gens.append(g_phaseA(wss[wi], s, idx, ncache + idx, cvi, (s - 8) if use_rope else None, h0,
                                                         seq_id if ncache == 0 else None))
                            interleave(gens)
                    else:
                        for p0 in range(0, len(jobs), 4):
                            gens = []
                            for wi, jb in enumerate(jobs[p0:p0 + 4]):
                                if jb[0] == "c":
                                    gens.append(g_kpath(wss[wi], jb[1], None, h0, load_idx=40 + jb[1]))
                                else:
                                    idx = jb[1]
                                    s = subs[idx]
                                    gens.append(g_kpath(wss[wi], ncache + idx, (s - 8) if use_rope else None, h0, load_idx=s))
                            interleave(gens)
                    QW = min(512, n * 128)
                    nq = QW // 128
                    for t0 in range(0, n, nq):
                        if pass_no == 0:
                            for q0 in range(0, nq, 2):
                                gens = []
                                for wi in range(2):
                                    q = q0 + wi
                                    s = subs[t0 + q]
                                    gens.append(g_phaseB(wss[wi], s, q, cvi, (s - 8) if use_rope else None, h0, pass_no))
                                if pend_wout[0] is not None:
                                    gens.append(pend_wout[0])
                                    pend_wout[0] = None
                                interleave(gens)
                        else:
                            gens = []
                            for q in range(nq):
                                s = subs[t0 + q]
                                gens.append(g_phaseB1(wss[q], s, q, (s - 8) if use_rope else None, h0))
                            if pend_wout[0] is not None:
                                gens.append(pend_wout[0])
                                pend_wout[0] = None
                            interleave(gens)
                        items = [(hh, c) for hh in range(4) for c in range(0, nchunks, 2)]
                        sc_pairs = [(5, 6), (1, 2)]

                        def emit_score(ii):
                            hh, c = items[ii]
                            pr = sc_pairs[ii % 2]
                            pc = ii % 2
                            for u in range(2):
                                P.op("pe", lambda u=u: nc.tensor.matmul(
                                    banks[pr[u]][:, 0:QW], lhsT=KT[:, hh, (c + u) * 128:(c + u + 1) * 128], rhs=QT[:, hh, 0:QW],
                                    start=True, stop=True), R=[B["KT"], B["QT"]], W=[bb[pr[u]]])
                            if HAM_FILL and QW == 512:
                                P.op("pe", lambda: nc.tensor.matmul(
                                    banks[0][:, 0:HAM_N], lhsT=KT[:, hh, c * 128:(c + 1) * 128], rhs=QT[:, hh, 0:HAM_N],
                                    start=True, stop=True), R=[B["KT"], B["QT"]], W=[bb[0]])
                            src2 = psum_all[:, pr[0] * 512:(pr[0] + 2) * 512].rearrange("p (b n) -> p b n", b=2)[:, :, 0:QW]
                            dst2 = pT[pc][:].rearrange("p (b n) -> p b n", b=2)[:, :, 0:QW]
                            P.op("act", lambda: nc.scalar.activation(out=dst2, in_=src2, func=AF.Exp, bias=cst[:, 4:5], scale=1.0),
                                 R=[bb[pr[0]], bb[pr[1]], B["cst"]], W=[b_pT[pc]])

                        def emit_pv(ii):
                            hh, c = items[ii]
                            pc = ii % 2
                            obk = 7 if hh % 2 == 0 else 3
                            for u in range(2):
                                P.op("pe", lambda u=u: nc.tensor.matmul(
                                    banks[obk][0:65, 0:QW], lhsT=V[:, c + u, hh, :], rhs=pT[pc][:, u * 512:u * 512 + QW],
                                    start=(c + u == 0), stop=(c + u == nchunks - 1)), R=[B["V"], b_pT[pc]], W=[bb[obk]])
                            if c + 2 >= nchunks:
                                def fin(hh=hh, obk=obk):
                                    P.op("dve", lambda: nc.vector.reciprocal(out=rr[64:65, 0:QW], in_=banks[obk][64:65, 0:QW]),
                                         R=[bb[obk]], W=[B["rr"]])
                                    P.op("pe", lambda: nc.tensor.matmul(banks[4][0:64, 0:QW], lhsT=ones[64:65, 0:64], rhs=rr[64:65, 0:QW],
                                                                       start=True, stop=True), R=[B["rr"], b_ones], W=[bb[4]])
                                    P.op("dve", lambda: nc.vector.tensor_copy(ob[:, 0:QW], banks[obk][0:64, 0:QW]), R=[bb[obk]], W=[B["ob"]])
                                    P.op("dve", lambda: nc.vector.tensor_tensor(out=attnT[:, hh, 0:QW], in0=banks[4][0:64, 0:QW],
                                                                                in1=ob[:, 0:QW], op=ALU.mult),
                                         R=[bb[4], B["ob"]], W=[B["attnT"]])
                                pending.append([fin, min(2, nchunks // 2), ii_cur[0]])

                        pending = []
                        ii_cur = [0]
                        emit_score(0)
                        for ii in range(len(items)):
                            ii_cur[0] = ii
                            if ii + 1 < len(items):
                                emit_score(ii + 1)
                            emit_pv(ii)
                            for pn in list(pending):
                                if pn[2] == ii:
                                    continue
                                pn[1] -= 1
                                if pn[1] <= 0:
                                    pn[0]()
                                    pending.remove(pn)
                        for pn in pending:
                            pn[0]()
                        interleave([g_wout(t0, nq, subs, h0, pass_no, base, cvi)])
                    if pend_wout[0] is not None:
                        interleave([pend_wout[0]])
                        pend_wout[0] = None

                def g_wout(t0, nq, subs, h0, pass_no, base, cvi):
                    xr_l = [wss[0].xt, xr]
                    bxr_l = [wss[0].B["xt"], B["xr"]]
                    residual_load(base, subs[t0], xr_l[0], bxr_l[0])
                    for q in range(nq):
                        s = subs[t0 + q]
                        if q + 1 < nq:
                            residual_load(base, subs[t0 + q + 1], xr_l[(q + 1) % 2], bxr_l[(q + 1) % 2])
                        halves = []
                        for half in range(2):
                            bk = 1 + half
                            nmm = 4 + (4 if pass_no == 0 else 0)
                            m = 0
                            for hh in range(4):
                                P.op("pe", lambda hh=hh, q=q, half=half, bk=bk, m=m, nmm=nmm: nc.tensor.matmul(
                                    banks[bk][:], lhsT=attnT[:, hh, q * 128:(q + 1) * 128],
                                    rhs=woA[:, h0 + hh, half * 512:(half + 1) * 512], start=(m == 0), stop=(m == nmm - 1)),
                                    R=[B["attnT"], B["woA"]], W=[bb[bk]])
                                m += 1
                            if pass_no == 0:
                                for cg in range(4):
                                    P.op("pe", lambda cg=cg, q=q, half=half, bk=bk, m=m, nmm=nmm: nc.tensor.matmul(
                                        banks[bk][:], lhsT=goutT[:, cg, q * 128:(q + 1) * 128],
                                        rhs=woG[:, cg, half * 512:(half + 1) * 512], start=(m == 0), stop=(m == nmm - 1)),
                                        R=[B["goutT"], B["woG"]], W=[bb[bk]])
                                    m += 1
                            halves.append((bk, half))
                        residual_cs(dst, s, cvi, halves, G, B["G"], xr_l[q % 2], bxr_l[q % 2], tmp, b_tmp)
                        yield

                pend_wout = [None]

                seqs = [([2 * q, 2 * q + 1], 0, False, q) for q in range(4)] + [(list(range(8, 40)), 4, True, -1)]
                for (subs, ncache, use_rope, seq_id) in seqs:
                    if ncache > 0:
                        P.dma("sp", Gt[:], gsc[i, 1, 1:2, :].partition_broadcast(128), R=[b_gsc], W=[B["G"]], owner=B["G"])
                    for pass_no in range(2):
                        run_seq(subs, ncache, use_rope, pass_no, seq_id)
                P.barrier()
                allw = []
                for w in wss:
                    allw += list(w.B.values())
                P.release(list(B.values()) + allw)

        prologue()
        chain = ["xin"] + ["xa" if (n % 2 == 0) else "xb" for n in range(11)] + ["y"]
        stage_no = [0]

        def nxt():
            n = stage_no[0]
            stage_no[0] += 1
            return chain[n], chain[n + 1]

        import os
        nst = int(os.environ.get("MK_STAGES", "12"))
        plan = []
        for i in range(4):
            plan += [("ffn", i, 0), ("mla" if i % 2 == 0 else "pool", i, 0), ("ffn", i, 1)]
        plan = plan[:nst]
        resolved = []
        for idx, (kind, i, which) in enumerate(plan):
            src, dst = nxt()
            if idx == len(plan) - 1:
                dst = "y"
            resolved.append((kind, i, which, src, dst))
        idx = 0
        while idx < len(resolved):
            kind, i, which, src, dst = resolved[idx]
            if kind == "ffn":
                grp = [(i, which, src, dst)]
                while idx + 1 < len(resolved) and resolved[idx + 1][0] == "ffn":
                    idx += 1
                    grp.append((resolved[idx][1], resolved[idx][2], resolved[idx][3], resolved[idx][4]))
                ffn_stage(grp)
            elif kind == "mla":
                mla_stage(i, src, dst)
            else:
                pool_stage(i, src, dst)
            idx += 1
        P.barrier()
    return nc


def _consts():
    rows = np.repeat(np.arange(64), 64).astype(np.float32)
    cols = np.tile(np.arange(64), 64).astype(np.float32)
    inv = (10000.0 ** (-np.arange(0, 16, 2, dtype=np.float32) / 16)).astype(np.float32)
    ang = np.concatenate([rows[:, None] * inv, cols[:, None] * inv], axis=-1).astype(np.float32)
    ropec = np.cos(ang).astype(np.float32)
    ropes = np.sin(ang).astype(np.float32)
    band = np.zeros((128, 4, 5, 128), np.float32)
    for wi, w in enumerate(WINS):
        h = w // 2
        for t in range(128):
            for var in range(5):
                if var in (0, 3, 4):
                    lo, hi = t - h, t + w - h
                    cnt = w
                elif var == 1:
                    lo, hi = max(t - h, 0), t + w - h
                    cnt = hi - lo
                else:
                    lo, hi = t - h, min(t + w - h, 128)
                    cnt = hi - lo
                for tp in range(lo, hi):
                    if var in (0, 1, 2):
                        if 0 <= tp < 128:
                            band[tp, wi, var, t] += 1.0 / cnt
                    elif var == 3:
                        if tp < 0:
                            band[tp + 128, wi, var, t] += 1.0 / cnt
                    else:
                        if tp >= 128:
                            band[tp - 128, wi, var, t] += 1.0 / cnt
                if var in (0, 1, 2):
                    band[t, wi, var, t] -= 1.0
    return ropec, ropes, band


_NC_CACHE = {}


def kernel(**inp):
    f = lambda a: np.ascontiguousarray(np.asarray(a, dtype=np.float32))
    if "nc" not in _NC_CACHE:
        _NC_CACHE["nc"] = build_nc()
    nc = _NC_CACHE["nc"]
    ropec, ropes, band = _consts()
    shared = {k: f(inp[k]) for k in ("w_mod", "b_mod", "norm_g", "ffn_w1", "ffn_w3", "ffn_w2", "w_in", "q_a_norm",
                                     "kv_a_norm", "w_qb", "w_kvb", "q_norm", "k_norm", "gmlp_v_norm", "gmlp_ws",
                                     "gmlp_b", "w_out", "pool_w", "pool_scale")}
    shared.update(ropec=ropec, ropes=ropes, band=band)
    xp = f(inp["x_prompt"]); xs = f(inp["x_sample"])
    in_maps = []
    for c in range(8):
        m = dict(shared)
        m["xin"] = np.ascontiguousarray(np.concatenate([xp[4 * c:4 * c + 4].reshape(1024, D), xs[c]], axis=0))
        m["cckv"] = f(inp["cache_ckv"][c])
        m["ckro"] = f(inp["cache_krope"][c])
        m["cvec"] = np.ascontiguousarray(np.stack([f(inp["c_ctx"]), f(inp["c"][c])], axis=0))
        in_maps.append(m)
    res = run_bass_kernel_spmd(nc, in_maps, core_ids=list(range(8)))
    r = res.results
    y_prompt = np.concatenate([r[c]["y"][:1024].reshape(4, 256, D) for c in range(8)], axis=0)
    y_sample = np.stack([r[c]["y"][1024:] for c in range(8)], axis=0)
    new_ckv = np.concatenate([r[c]["ckv_o"] for c in range(8)], axis=0)
    new_krope = np.concatenate([r[c]["kro_o"] for c in range(8)], axis=0)
    return (y_prompt.astype(np.float32), y_sample.astype(np.float32), new_ckv.astype(np.float32),
            new_krope.astype(np.float32))
```

```python
import numpy as np
from contextlib import ExitStack
import concourse.bass as bass
import concourse.mybir as mybir
from concourse.bass_utils import run_bass_kernel_spmd

F32 = mybir.dt.float32
BF16 = mybir.dt.bfloat16
AF = mybir.ActivationFunctionType
ALU = mybir.AluOpType
AX = mybir.AxisListType

D = 1024
NSUB = 40
NTOK = NSUB * 128
FH = 2816
NF = 22
EPS = 1e-6
WINS = (2, 4, 8, 16)
SAME_SYNC = True
import os as _os
HAM_FILL = bool(int(_os.environ.get('HAM_FILL', '0')))
HAM_N = int(_os.environ.get('HAM_N', '256'))


class Buf:
    __slots__ = ("name", "w", "r", "ld", "st")

    def __init__(self, name):
        self.name = name
        self.w = {}
        self.r = {}
        self.ld = None
        self.st = None


class Prog:
    def __init__(self, nc, es):
        self.nc = nc
        self.es = es
        self.eng = {"pe": nc.tensor, "act": nc.scalar, "dve": nc.vector, "pool": nc.gpsimd, "sp": nc.sync}
        self.semobj = {}
        self.cnt = {}
        for e in ("pe", "act", "dve", "pool"):
            self.semobj[e] = es.enter_context(nc.semaphore("s_" + e))
            self.cnt[e] = 0
        self.waited = {e: {} for e in self.eng}
        self.free_d = {"sp": [], "pool": []}
        self.dq = {}
        self.nd = 0

    def _newd(self, q):
        if self.free_d[q]:
            return self.free_d[q].pop()
        k = ("d", self.nd)
        self.dq[k] = q
        self.nd += 1
        self.semobj[k] = self.es.enter_context(self.nc.semaphore("s_d%d" % k[1]))
        self.cnt[k] = 0
        return k

    def release(self, bufs):
        for b in bufs:
            if b.ld is not None:
                self.free_d[self.dq[b.ld]].append(b.ld)
                b.ld = None
            if b.st is not None:
                self.free_d[self.dq[b.st]].append(b.st)
                b.st = None

    def _deps(self, R, W):
        d = {}
        for b in R:
            for k, v in b.w.items():
                if d.get(k, 0) < v:
                    d[k] = v
        for b in W:
            for k, v in b.w.items():
                if d.get(k, 0) < v:
                    d[k] = v
            for k, v in b.r.items():
                if d.get(k, 0) < v:
                    d[k] = v
        return d

    def _waits(self, e, deps):
        wd = self.waited[e]
        for k, v in deps.items():
            if k == e and (e == "pe" or not SAME_SYNC):
                continue
            if wd.get(k, 0) >= v:
                continue
            self.eng[e].wait_ge(self.semobj[k], v)
            wd[k] = v

    def op(self, e, fn, R=(), W=()):
        self._waits(e, self._deps(R, W))
        ins = fn()
        self.cnt[e] += 1
        ins.then_inc(self.semobj[e], 1)
        v = self.cnt[e]
        for b in R:
            b.r[e] = v
        for b in W:
            b.w[e] = v

    def dma(self, q, out, in_, R=(), W=(), owner=None, kind="ld", **kw):
        self._waits(q, self._deps(R, W))
        if kind == "ld":
            if owner.ld is None:
                owner.ld = self._newd(q)
            k = owner.ld
        else:
            if owner.st is None:
                owner.st = self._newd(q)
            k = owner.st
        assert self.dq[k] == q, (owner.name, kind, q)
        ins = self.eng[q].dma_start(out=out, in_=in_, **kw)
        self.cnt[k] += 16
        ins.then_inc(self.semobj[k], 16)
        v = self.cnt[k]
        for b in R:
            b.r[k] = v
        for b in W:
            b.w[k] = v

    def barrier(self):
        for e in self.eng:
            wd = self.waited[e]
            for k, v in self.cnt.items():
                if v > 0 and wd.get(k, 0) < v:
                    self.eng[e].wait_ge(self.semobj[k], v)
                    wd[k] = v


_UID = [0]


def uniq(name):
    _UID[0] += 1
    return "%s_u%d" % (name, _UID[0])


def cv_of(s):
    return 0 if s < 8 else 1


def build_nc():
    _UID[0] = 0
    nc = bass.Bass("TRN2", target_bir_lowering=False)

    def din(name, shape):
        return nc.dram_tensor(name, list(shape), F32, kind="ExternalInput").ap()

    xin = din("xin", [NTOK, D])
    cckv = din("cckv", [2, 512, 128])
    ckro = din("ckro", [2, 512, 32])
    cvec = din("cvec", [2, D])
    w_mod = din("w_mod", [4, D, 9 * D])
    b_mod = din("b_mod", [4, 9 * D])
    norm_g = din("norm_g", [4, 3, D])
    ffn_w1 = din("ffn_w1", [4, 2, D, FH])
    ffn_w3 = din("ffn_w3", [4, 2, D, FH])
    ffn_w2 = din("ffn_w2", [4, 2, FH, D])
    w_in = din("w_in", [2, D, 1440])
    q_a_norm = din("q_a_norm", [2, 256])
    kv_a_norm = din("kv_a_norm", [2, 128])
    w_qb = din("w_qb", [2, 256, 768])
    w_kvb = din("w_kvb", [2, 128, 1024])
    q_norm = din("q_norm", [2, 96])
    k_norm = din("k_norm", [2, 96])
    gmlp_v_norm = din("gmlp_v_norm", [2, 512])
    gmlp_ws = din("gmlp_ws", [2, 4, 128, 128])
    gmlp_b = din("gmlp_b", [2, 4, 128])
    w_out = din("w_out", [2, D, D])
    pool_w = din("pool_w", [2, 4, 256, 256])
    pool_scale = din("pool_scale", [2, D])
    ropec = din("ropec", [4096, 16])
    ropes = din("ropes", [4096, 16])
    band = din("band", [128, 4, 5, 128])

    y = nc.dram_tensor("y", [NTOK, D], F32, kind="ExternalOutput").ap()
    ckv_o = nc.dram_tensor("ckv_o", [4, 2, 256, 128], F32, kind="ExternalOutput").ap()
    kro_o = nc.dram_tensor("kro_o", [4, 2, 256, 32], F32, kind="ExternalOutput").ap()
    xa = nc.dram_tensor("xa", [NTOK, D], F32, kind="Internal").ap()
    xb = nc.dram_tensor("xb", [NTOK, D], F32, kind="Internal").ap()
    gsc = nc.dram_tensor("gsc", [4, 3, 2, D], F32, kind="Internal").ap()
    klT_d = nc.dram_tensor("klT_d", [44, 128, 128], BF16, kind="Internal").ap()
    krf_d = nc.dram_tensor("krf_d", [44, 128, 32], F32, kind="Internal").ap()
    qlT_d = nc.dram_tensor("qlT_d", [40, 128, 256], BF16, kind="Internal").ap()

    with ExitStack() as es:
        P = Prog(nc, es)

        def sbp(name, shape, dt):
            return es.enter_context(nc.sbuf_tensor(name, list(shape), dt))

        ident = sbp("ident", [128, 128], BF16)
        identf = sbp("identf", [128, 128], F32)
        ones = sbp("ones", [128, 128], F32)
        AS = sbp("AS", [128, 12, 2, 8, 2], F32)
        psum_all = es.enter_context(nc.psum_tensor("psum_all", [128, 4096], F32))
        banks = [psum_all[:, i * 512:(i + 1) * 512] for i in range(8)]
        bb = [Buf("bank%d" % i) for i in range(8)]
        b_ident = Buf("ident")
        b_AS = Buf("AS")
        b_ones = Buf("ones")
        xbufs = {}
        for nm in ("xin", "xa", "xb", "y"):
            xbufs[nm] = [Buf("%s_%d" % (nm, s)) for s in range(NSUB)]
        xaps = {"xin": xin, "xa": xa, "xb": xb, "y": y}
        b_gsc = Buf("gsc")

        P.op("pool", lambda: nc.gpsimd.memset(identf[:], 0.0), W=[b_ident])
        P.op("pool", lambda: nc.gpsimd.affine_select(out=identf[:], in_=identf[:], pattern=[[-1, 128]],
                                                      compare_op=ALU.not_equal, fill=1.0, base=0,
                                                      channel_multiplier=1), R=[b_ident], W=[b_ident])
        P.op("dve", lambda: nc.vector.tensor_copy(ident[:], identf[:]), R=[b_ident], W=[b_ident])
        P.op("dve", lambda: nc.vector.memset(ones[:], 1.0), W=[b_ones])

        def prologue():
            with ExitStack() as st:
                def sb(name, shape, dt):
                    return st.enter_context(nc.sbuf_tensor(uniq(name), list(shape), dt))
                cT = sb("cT", [128, 2, 8], F32)
                sT = sb("sT", [128, 2, 8], F32)
                sTb = sb("sTb", [128, 2, 8], BF16)
                sbc = sb("sbc", [128, 16, 128], BF16)
                wblk = [sb("wblk%d" % i, [128, 8, 1024], BF16) for i in range(3)]
                brow = [sb("brow%d" % i, [1, 1024], F32) for i in range(2)]
                grow = [sb("grow%d" % i, [128, 1024], F32) for i in range(2)]
                psc = sb("psc", [128, 1024], F32)
                b_cT = Buf("cT"); b_sT = Buf("sT"); b_sbc = Buf("sbc")
                b_w = [Buf("wblk0"), Buf("wblk1"), Buf("wblk2")]
                b_br = [Buf("brow0"), Buf("brow1")]
                b_gr = [Buf("grow0"), Buf("grow1")]
                b_psc = Buf("psc")
                allb = [b_cT, b_sT, b_sbc, b_psc] + b_w + b_br + b_gr
                with nc.allow_non_contiguous_dma(reason="tiny transposed cvec load"):
                    for c in range(2):
                        P.dma("sp", cT[:, c, :], cvec[c, :].rearrange("(k p) -> p k", p=128), W=[b_cT], owner=b_cT)
                P.op("act", lambda: nc.scalar.activation(out=sT[:], in_=cT[:], func=AF.Silu), R=[b_cT], W=[b_sT])
                P.op("dve", lambda: nc.vector.tensor_copy(
                    sbc[:], sT[:].rearrange("p c k -> p (c k)").unsqueeze(2).to_broadcast([128, 16, 128])),
                    R=[b_sT], W=[b_sbc])
                P.op("dve", lambda: nc.vector.tensor_copy(sTb[:], sT[:]), R=[b_sT], W=[b_sbc])
                n = 0
                for i in range(4):
                    for k in range(3):
                        for v in range(3):
                            blk = 3 * k + v
                            wi = n % 3
                            bi = n % 2
                            n += 1
                            c0 = blk * 1024
                            P.dma("pool", wblk[wi][:], w_mod[i, :, c0:c0 + 1024].rearrange("(k p) n -> p k n", p=128),
                                  W=[b_w[wi]], owner=b_w[wi])
                            P.dma("sp", brow[bi][:], b_mod[i:i + 1, c0:c0 + 1024], W=[b_br[bi]], owner=b_br[bi])
                            if v < 2:
                                pm = banks[0][:, 0:16].rearrange("p (c v) -> p c v", v=2)
                                for ch in range(8):
                                    for kc in range(8):
                                        P.op("pe", lambda ch=ch, kc=kc: nc.tensor.matmul(
                                            pm[:, ch, :], lhsT=wblk[wi][:, kc, ch * 128:(ch + 1) * 128],
                                            rhs=sTb[:, :, kc], start=(kc == 0), stop=False),
                                            R=[b_w[wi], b_sbc], W=[bb[0]])
                                    P.op("pe", lambda ch=ch: nc.tensor.matmul(
                                        pm[:, ch, :], lhsT=brow[bi][0:1, ch * 128:(ch + 1) * 128],
                                        rhs=ones[0:1, 0:2], start=False, stop=True),
                                        R=[b_br[bi], b_ones], W=[bb[0]])
                                dst = AS[:, i * 3 + k, 1 - v, :, :]
                                if v == 0:
                                    P.op("dve", lambda dst=dst: nc.vector.tensor_copy(dst, pm), R=[bb[0]], W=[b_AS])
                                else:
                                    P.op("dve", lambda dst=dst: nc.vector.tensor_scalar(
                                        out=dst, in0=pm, scalar1=1.0, scalar2=None, op0=ALU.add), R=[bb[0]], W=[b_AS])
                            else:
                                fac = 1.0 if k == 1 else 0.5
                                for cvi in range(2):
                                    gi = cvi
                                    for half in range(2):
                                        bk = 1 + half
                                        for kc in range(8):
                                            P.op("pe", lambda kc=kc, half=half, bk=bk: nc.tensor.matmul(
                                                banks[bk][:], lhsT=sbc[:, cvi * 8 + kc, :],
                                                rhs=wblk[wi][:, kc, half * 512:(half + 1) * 512],
                                                start=(kc == 0), stop=False), R=[b_w[wi], b_sbc], W=[bb[bk]])
                                        P.op("pe", lambda half=half, bk=bk: nc.tensor.matmul(
                                            banks[bk][:], lhsT=ones[0:1, 0:128],
                                            rhs=brow[bi][0:1, half * 512:(half + 1) * 512],
                                            start=False, stop=True), R=[b_br[bi], b_ones], W=[bb[bk]])
                                        P.op("act", lambda half=half, bk=bk: nc.scalar.activation(
                                            out=grow[gi][:, half * 512:(half + 1) * 512], in_=banks[bk][:],
                                            func=AF.Copy, scale=fac), R=[bb[bk]], W=[b_gr[gi]])
                                    if k == 1 and i % 2 == 1:
                                        P.dma("sp", psc[:], pool_scale[i // 2:i // 2 + 1, :].partition_broadcast(128),
                                              W=[b_psc], owner=b_psc)
                                        P.op("dve", lambda: nc.vector.tensor_tensor(
                                            out=grow[gi][:], in0=grow[gi][:], in1=psc[:], op=ALU.mult),
                                            R=[b_gr[gi], b_psc], W=[b_gr[gi]])
                                    P.dma("sp", gsc[i, k, cvi:cvi + 1, :], grow[gi][0:1, :], R=[b_gr[gi]], W=[b_gsc],
                                          owner=b_gr[gi], kind="st")
                P.barrier()
                P.release(allb)

        def front_load(src, s, xt, b_xt):
            P.dma("sp", xt[:], xaps[src][s * 128:(s + 1) * 128, :], R=[xbufs[src][s]], W=[b_xt], owner=b_xt)

        def front_compute(xt, b_xt, xn, b_xn, gn, b_gn, st_, b_st, junk, b_junk):
            P.op("act", lambda: nc.scalar.activation(out=junk[:], in_=xt[:], func=AF.Square, accum_out=st_[:, 0:1]),
                 R=[b_xt], W=[b_junk, b_st])
            rstd(st_, b_st, 1, 1.0 / D)
            P.op("dve", lambda: nc.vector.scalar_tensor_tensor(out=xn[:], in0=xt[:], scalar=st_[:, 2:3], in1=gn[:],
                                                              op0=ALU.mult, op1=ALU.mult),
                 R=[b_xt, b_st, b_gn], W=[b_xn])

        def front(src, s, xt, b_xt, xn, b_xn, gn, b_gn, st_, b_st, junk, b_junk):
            front_load(src, s, xt, b_xt)
            front_compute(xt, b_xt, xn, b_xn, gn, b_gn, st_, b_st, junk, b_junk)

        def rstd(st_, b_st, n, inv):
            P.op("dve", lambda: nc.vector.tensor_scalar(out=st_[:, n:2 * n], in0=st_[:, 0:n], scalar1=inv, scalar2=EPS,
                                                        op0=ALU.mult, op1=ALU.add), R=[b_st], W=[b_st])
            P.op("act", lambda: nc.scalar.activation(out=st_[:, n:2 * n], in_=st_[:, n:2 * n], func=AF.Sqrt),
                 R=[b_st], W=[b_st])
            P.op("dve", lambda: nc.vector.reciprocal(out=st_[:, 2 * n:3 * n], in_=st_[:, n:2 * n]), R=[b_st], W=[b_st])

        def transpose_mod(xn, b_xn, hT_dst, b_hT, bank_i, ik):
            pT = banks[bank_i][:].bitcast(BF16).rearrange("p (k t) -> p k t", k=8)
            for kc in range(8):
                P.op("pe", lambda kc=kc: nc.tensor.transpose(out=pT[:, kc, :], in_=xn[:, kc * 128:(kc + 1) * 128],
                                                             identity=ident[:]), R=[b_xn, b_ident], W=[bb[bank_i]])
            return pT

        def residual_load(src, s, xr, b_xr):
            P.dma("sp", xr[:], xaps[src][s * 128:(s + 1) * 128, :], R=[xbufs[src][s]], W=[b_xr], owner=b_xr)

        def residual_cs(dst, s, cvi, psum_halves, G, b_G, xr, b_xr, tmp, b_tmp):
            for (bk, half) in psum_halves:
                sl = slice(half * 512, (half + 1) * 512)
                P.op("dve", lambda bk=bk, sl=sl, half=half: nc.vector.tensor_tensor(
                    out=tmp[half][:], in0=banks[bk][:], in1=G[cvi][:, sl], op=ALU.mult),
                    R=[bb[bk], b_G], W=[b_tmp[half]])
                P.op("pool", lambda sl=sl, half=half: nc.gpsimd.tensor_tensor(
                    out=xr[:, sl], in0=tmp[half][:], in1=xr[:, sl], op=ALU.add),
                    R=[b_tmp[half], b_xr], W=[b_xr])
            P.dma("sp", xaps[dst][s * 128:(s + 1) * 128, :], xr[:], R=[b_xr], W=[xbufs[dst][s]], owner=b_xr, kind="st")

        def residual(src, dst, s, cvi, psum_halves, G, b_G, xr, b_xr, tmp, b_tmp):
            residual_load(src, s, xr, b_xr)
            residual_cs(dst, s, cvi, psum_halves, G, b_G, xr, b_xr, tmp, b_tmp)

        def ffn_stage(stages):
            with ExitStack() as st:
                def sb(name, shape, dt):
                    return st.enter_context(nc.sbuf_tensor(uniq(name), list(shape), dt))
                w1 = sb("w1", [128, 8, FH], BF16)
                w3 = sb("w3", [128, 8, FH], BF16)
                w2 = sb("w2", [128, NF, D], BF16)
                G = [sb("G%d" % c, [128, D], F32) for c in range(2)]
                gn = sb("gn", [128, D], F32)
                xt = [sb("xt%d" % c, [128, D], F32) for c in range(2)]
                xr = [sb("xr%d" % c, [128, D], F32) for c in range(2)]
                xn = [sb("xn%d" % c, [128, D], BF16) for c in range(4)]
                sts = [sb("st%d" % c, [128, 4], F32) for c in range(2)]
                hT = sb("hT", [128, 8, 512], BF16)
                gT = sb("gT", [128, NF, 512], BF16)
                sa = [sb("sa%d" % c, [128, 512], F32) for c in range(2)]
                tmp0 = sb("tmp0", [128, 512], F32)
                tmp = [tmp0, tmp0]
                b_w1b = [Buf("w1_%d" % c) for c in range(4)]; b_w3b = [Buf("w3_%d" % c) for c in range(4)]
                b_w2 = Buf("w2"); b_G = Buf("G"); b_gn = Buf("gn")

                def wblks(bl, j):
                    return [bl[b] for b in sorted({(j * 128) // 704, (j * 128 + 127) // 704})]
                b_xt = [Buf("xt0"), Buf("xt1")]; b_xr = [Buf("xr0"), Buf("xr1")]; b_xn = [Buf("xn%d" % c) for c in range(4)]
                b_st = [Buf("st0"), Buf("st1")]
                b_hT = [Buf("hT%d" % c) for c in range(4)]
                b_gT = [Buf("gT%d" % c) for c in range(NF)]
                b_t0 = Buf("tmp0")
                b_sa = [Buf("sa0"), Buf("sa1")]; b_tmp = [b_t0, b_t0]
                allb = b_w1b + b_w3b + [b_w2, b_G, b_gn] + b_xt + b_xr + b_xn + b_st + b_hT + b_gT + b_sa + b_tmp
                NT = NSUB // 4
                NG = NT * len(stages)

                def ik_of(n):
                    i, which, _, _ = stages[n]
                    return i * 3 + (0 if which == 0 else 2)

                def load_w13(n):
                    i, which, _, _ = stages[n]
                    w1v = ffn_w1[i, which].rearrange("(k p) n -> p k n", p=128)
                    w3v = ffn_w3[i, which].rearrange("(k p) n -> p k n", p=128)
                    for blk in range(4):
                        cs = slice(blk * 704, (blk + 1) * 704)
                        P.dma("pool", w1[:, :, cs], w1v[:, :, cs], W=[b_w1b[blk]], owner=b_w1b[blk])
                        P.dma("pool", w3[:, :, cs], w3v[:, :, cs], W=[b_w3b[blk]], owner=b_w3b[blk])

                def load_w2(n):
                    i, which, _, _ = stages[n]
                    w2v = ffn_w2[i, which].rearrange("(j p) n -> p j n", p=128)
                    for jb in range(2):
                        js = slice(jb * 11, (jb + 1) * 11)
                        P.dma("pool", w2[:, js, :], w2v[:, js, :], W=[b_w2], owner=b_w2)

                def load_G(n):
                    i, which, _, _ = stages[n]
                    k = 0 if which == 0 else 2
                    for c in range(2):
                        P.dma("sp", G[c][:], gsc[i, k, c:c + 1, :].partition_broadcast(128), R=[b_gsc], W=[b_G], owner=b_G)

                def load_gn(n):
                    i, which, _, _ = stages[n]
                    k = 0 if which == 0 else 2
                    P.dma("sp", gn[:], norm_g[i, k:k + 1, :].partition_broadcast(128), W=[b_gn], owner=b_gn)

                def do_front_load(gt, q):
                    n, t = divmod(gt, NT)
                    front_load(stages[n][2], 4 * t + q, xt[q % 2], b_xt[q % 2])

                def do_front_compute(gt, q):
                    c = q % 2
                    front_compute(xt[c], b_xt[c], xn[q], b_xn[q], gn, b_gn, sts[c], b_st[c], xn[q], b_xn[q])

                def do_tr(gt, q):
                    n, t = divmod(gt, NT)
                    ik = ik_of(n)
                    cvi = cv_of(4 * t)
                    pT = transpose_mod(xn[q], b_xn[q], None, None, 0, ik)
                    for kc in range(8):
                        P.op("act", lambda kc=kc, q=q: nc.scalar.activation(
                            out=hT[:, kc, q * 128:(q + 1) * 128], in_=pT[:, kc, :], func=AF.Identity,
                            scale=AS[:, ik, 0, kc, cvi:cvi + 1], bias=AS[:, ik, 1, kc, cvi:cvi + 1]),
                            R=[bb[0], b_AS], W=[b_hT[q]])

                load_w13(0)
                load_w2(0)
                load_G(0)
                load_gn(0)
                for q in range(4):
                    do_front_load(0, q)
                    do_front_compute(0, q)
                for q in range(4):
                    do_tr(0, q)
                nr = [0]
                for gt in range(NG):
                    n, t = divmod(gt, NT)
                    _, _, src, dst = stages[n]
                    cvi = cv_of(4 * t)
                    nxt_ = gt + 1 < NG
                    if nxt_ and t == NT - 1:
                        load_gn(n + 1)
                    for j in range(NF):
                        if nxt_ and j < 16:
                            if j % 4 == 0:
                                do_front_load(gt + 1, j // 4)
                            elif j % 4 == 3:
                                do_front_compute(gt + 1, j // 4)
                        pa = 1 + 2 * (j % 2)
                        pb = pa + 1
                        for kc in range(8):
                            P.op("pe", lambda kc=kc, j=j, pa=pa: nc.tensor.matmul(
                                banks[pa][:], lhsT=w1[:, kc, j * 128:(j + 1) * 128], rhs=hT[:, kc, :],
                                start=(kc == 0), stop=(kc == 7)), R=wblks(b_w1b, j) + b_hT, W=[bb[pa]])
                        for kc in range(8):
                            P.op("pe", lambda kc=kc, j=j, pb=pb: nc.tensor.matmul(
                                banks[pb][:], lhsT=w3[:, kc, j * 128:(j + 1) * 128], rhs=hT[:, kc, :],
                                start=(kc == 0), stop=(kc == 7)), R=wblks(b_w3b, j) + b_hT, W=[bb[pb]])
                        sc = j % 2
                        P.op("act", lambda pa=pa, sc=sc: nc.scalar.activation(out=sa[sc][:], in_=banks[pa][:], func=AF.Silu),
                             R=[bb[pa]], W=[b_sa[sc]])
                        P.op("dve", lambda pb=pb, sc=sc, j=j: nc.vector.tensor_tensor(
                            out=gT[:, j, :], in0=banks[pb][:], in1=sa[sc][:], op=ALU.mult),
                            R=[bb[pb], b_sa[sc]], W=[b_gT[j]])
                    if nxt_ and t == NT - 1:
                        load_w13(n + 1)
                    for q in range(4):
                        s = 4 * t + q
                        c = nr[0] % 2
                        nr[0] += 1
                        if nxt_:
                            do_tr(gt + 1, q)
                        halves = []
                        for half in range(2):
                            bk = 5 + half
                            for j in range(NF):
                                P.op("pe", lambda j=j, q=q, half=half, bk=bk: nc.tensor.matmul(
                                    banks[bk][:], lhsT=gT[:, j, q * 128:(q + 1) * 128],
                                    rhs=w2[:, j, half * 512:(half + 1) * 512], start=(j == 0), stop=(j == NF - 1)),
                                    R=[b_gT[j], b_w2], W=[bb[bk]])
                            halves.append((bk, half))
                        residual(src, dst, s, cvi, halves, G, b_G, xr[c], b_xr[c], tmp, b_tmp)
                    if nxt_ and t == NT - 1:
                        load_w2(n + 1)
                        load_G(n + 1)
                P.barrier()
                P.release(allb)

        def pool_stage(i, src, dst):
            j = i // 2
            ik = i * 3 + 1
            with ExitStack() as st:
                def sb(name, shape, dt):
                    return st.enter_context(nc.sbuf_tensor(uniq(name), list(shape), dt))
                bandb = sb("bandb", [128, 4, 5, 128], BF16)
                pw = sb("pw", [128, 4, 2, 256], BF16)
                G = [sb("G%d" % c, [128, D], F32) for c in range(2)]
                gn = sb("gn", [128, D], F32)
                xt = [sb("xt%d" % c, [128, D], F32) for c in range(2)]
                xr = [sb("xr%d" % c, [128, D], F32) for c in range(2)]
                xn = sb("xnall", [128, 32, D], BF16)
                junk = sb("junk", [128, D], BF16)
                sts = [sb("st%d" % c, [128, 4], F32) for c in range(2)]
                dT = [sb("dT%d" % c, [128, 8, 128], BF16) for c in range(2)]
                tmp = [sb("tmp%d" % c, [128, 512], F32) for c in range(2)]
                b_band = Buf("band"); b_pw = Buf("pw"); b_G = Buf("G"); b_gn = Buf("gn")
                b_xt = [Buf("xt0"), Buf("xt1")]; b_xr = [Buf("xr0"), Buf("xr1")]
                b_xn = [Buf("xn%d" % c) for c in range(32)]
                b_junk = Buf("junk"); b_st = [Buf("st0"), Buf("st1")]
                b_dT = [Buf("dT0"), Buf("dT1")]; b_tmp = [Buf("tmp0"), Buf("tmp1")]
                allb = [b_band, b_pw, b_G, b_gn, b_junk] + b_xt + b_xr + b_xn + b_st + b_dT + b_tmp
                P.dma("pool", bandb[:].rearrange("p a b c -> p (a b c)"), band.rearrange("p a b c -> p (a b c)"),
                      W=[b_band], owner=b_band)
                for g in range(4):
                    P.dma("pool", pw[:, g, :, :], pool_w[j, g].rearrange("(k p) n -> p k n", p=128), W=[b_pw], owner=b_pw)
                for c in range(2):
                    P.dma("sp", G[c][:], gsc[i, 1, c:c + 1, :].partition_broadcast(128), R=[b_gsc], W=[b_G], owner=b_G)
                P.dma("sp", gn[:], norm_g[i, 1:2, :].partition_broadcast(128), W=[b_gn], owner=b_gn)
                cnt = [0, 0]
                seqs = [[2 * q, 2 * q + 1] for q in range(4)] + [list(range(8, 40))]
                for subs in seqs:
                    n = len(subs)
                    cvi = cv_of(subs[0])

                    def fr(idx):
                        c = cnt[0] % 2
                        cnt[0] += 1
                        front(src, subs[idx], xt[c], b_xt[c], xn[:, idx, :], b_xn[idx], gn, b_gn, sts[c], b_st[c], junk, b_junk)
                    def g_sub(idx, c):
                        first = idx == 0
                        last = idx == n - 1
                        bo = 4 * c
                        for cc in range(8):
                            g = cc // 2
                            bk = bo + cc // 4
                            outp = banks[bk][:, (cc % 4) * 128:(cc % 4 + 1) * 128]
                            nbs = []
                            if not first:
                                nbs.append((idx - 1, 3))
                            nbs.append((idx, 1 if first else (2 if last else 0)))
                            if not last:
                                nbs.append((idx + 1, 4))
                            for m, (ni, var) in enumerate(nbs):
                                P.op("pe", lambda ni=ni, var=var, cc=cc, g=g, outp=outp, m=m, L=len(nbs): nc.tensor.matmul(
                                    outp, lhsT=xn[:, ni, cc * 128:(cc + 1) * 128], rhs=bandb[:, g, var, :],
                                    start=(m == 0), stop=(m == L - 1)), R=[b_xn[ni], b_band], W=[bb[bk]])
                            if cc % 4 == 3:
                                yield
                        for cc in range(8):
                            bk = bo + cc // 4
                            P.op("act", lambda cc=cc, bk=bk, c=c: nc.scalar.activation(
                                out=dT[c][:, cc, :], in_=banks[bk][:, (cc % 4) * 128:(cc % 4 + 1) * 128], func=AF.Copy,
                                scale=AS[:, ik, 0, cc, cvi:cvi + 1]), R=[bb[bk], b_AS], W=[b_dT[c]])
                            if cc % 2 == 1:
                                yield
                        halves = []
                        for half in range(2):
                            bk = bo + 2 + half
                            for gg in range(2):
                                g = half * 2 + gg
                                for kk in range(2):
                                    P.op("pe", lambda g=g, gg=gg, kk=kk, bk=bk, c=c: nc.tensor.matmul(
                                        banks[bk][:, gg * 256:(gg + 1) * 256], lhsT=dT[c][:, 2 * g + kk, :],
                                        rhs=pw[:, g, kk, :], start=(kk == 0), stop=(kk == 1)),
                                        R=[b_dT[c], b_pw], W=[bb[bk]])
                            halves.append((bk, half))
                        yield
                        residual(src, dst, subs[idx], cvi, halves, G, b_G, xr[c], b_xr[c], tmp, b_tmp)
                        yield

                    for k0 in range(min(4, n)):
                        fr(k0)
                    for idx in range(0, n, 2):
                        for k0 in (idx + 4, idx + 5):
                            if k0 < n:
                                fr(k0)
                        gens = [g_sub(idx, 0)]
                        if idx + 1 < n:
                            gens.append(g_sub(idx + 1, 1))
                        while gens:
                            for gg_ in list(gens):
                                try:
                                    next(gg_)
                                except StopIteration:
                                    gens.remove(gg_)
                P.barrier()
                P.release(allb)

        def interleave(gens):
            gens = list(gens)
            while gens:
                for g in list(gens):
                    try:
                        next(g)
                    except StopIteration:
                        gens.remove(g)

        def mla_stage(i, src, dst):
            j = i // 2
            ik = i * 3 + 1
            with ExitStack() as st:
                def sb(name, shape, dt):
                    return st.enter_context(nc.sbuf_tensor(uniq(name), list(shape), dt))
                win = sb("win", [128, 8, 1440], BF16)
                wqb = sb("wqb", [128, 2, 768], BF16)
                wkvb = sb("wkvb", [128, 1024], BF16)
                woA = sb("woA", [64, 8, D], BF16)
                woG = sb("woG", [128, 4, D], BF16)
                wsf = sb("wsf", [128, 4, 128], BF16)
                wsT = sb("wsT", [128, 4, 128], BF16)
                qan = sb("qan", [128, 256], F32)
                kvan = sb("kvan", [128, 128], F32)
                qnr = sb("qnr", [128, 96], F32)
                knr = sb("knr", [128, 96], F32)
                vnr = sb("vnr", [128, 512], F32)
                gb = sb("gb", [128, 4], F32)
                cst = sb("cst", [128, 8], F32)
                rC = sb("rC", [128, 32, 16], F32)
                rS = sb("rS", [128, 32, 16], F32)
                Gt = sb("Gt", [128, D], F32)
                G = [Gt, Gt]
                gn = sb("gn", [128, D], F32)
                KT = sb("KT", [96, 4, 4608], BF16)
                V = sb("V", [128, 36, 4, 65], BF16)
                xr = sb("xr", [128, D], F32)
                QT = sb("QT", [96, 4, 512], BF16)
                goutT = sb("goutT", [128, 4, 512], BF16)
                attnT = sb("attnT", [64, 4, 512], BF16)
                pT = [sb("pT%d" % c, [128, 1024], BF16) for c in range(2)]
                tmp0 = sb("tmp0", [128, 512], F32)
                tmp = [tmp0, tmp0]
                rr = tmp0
                ob = xr[0:64, 0:512]
                names = ["win", "wqb", "wkvb", "woA", "woG", "wsf", "wsT", "rows", "gb", "cst", "rope", "G", "gn", "KT", "V",
                         "xr", "QT", "goutT", "attnT", "pT0", "pT1", "pT2", "ob", "rr", "tmp0"]
                B = {nm: Buf(nm) for nm in names}
                b_tmp = [B["tmp0"], B["tmp0"]]
                b_pT = [B["pT0"], B["pT1"]]
                B["rr"] = B["tmp0"]
                B["ob"] = B["xr"]

                class WS:
                    pass
                wss = []
                for wi in range(2):
                    w = WS()
                    w.xt = sb("xt", [128, D], F32)
                    w.xn = sb("xn", [128, D], BF16)
                    w.sts = sb("st", [128, 4], F32)
                    w.stg = sb("stg", [128, 4], F32)
                    w.st4 = sb("st4", [128, 12], F32)
                    w.hTs = sb("hTs", [128, 8, 128], BF16)
                    w.kvf = sb("kvf", [128, 128], F32)
                    w.krf = sb("krf", [128, 32], F32)
                    w.kvb = sb("kvb", [128, 256], BF16)
                    w.lT = sb("lT", [128, 2, 128], BF16)
                    w.kf = sb("kf", [128, 4, 96], F32)
                    w.kb = sb("kb", [128, 4, 96], BF16)
                    w.rt = [sb("rt%d" % c, [128, 4, 2, 8], F32) for c in range(4)]
                    w.ug = sb("ug", [128, 512], F32)
                    w.g1 = sb("g1", [128, 512], F32)
                    w.g2 = sb("g2", [128, 512], F32)
                    w.vg = w.g1
                    w.junk = sb("junk", [128, 384], F32)
                    w.vnb = sb("vnb", [128, 512], BF16)
                    w.gout = sb("gout", [128, 512], BF16)
                    w.B = {nm: Buf(nm + str(wi)) for nm in ("xt", "xn", "st", "st4", "hTs", "kvf", "krf", "kvb", "lT",
                                                            "kf", "kb", "rt", "ug", "g1", "g2", "vnb", "gout")}
                    w.B["vg"] = w.B["g1"]
                    w.B["junk"] = Buf("junk%d" % wi)
                    w.B["stg"] = Buf("stg%d" % wi)
                    wss.append(w)
                for wi in range(2, 4):
                    w = WS()
                    w.junk = sb("junk", [128, 384], F32)
                    w.st4 = sb("st4", [128, 12], F32)
                    w.krf = sb("krf", [128, 32], F32)
                    w.lT = sb("lT", [128, 2, 128], BF16)
                    w.kf = sb("kf", [128, 4, 96], F32)
                    w.kb = sb("kb", [128, 4, 96], BF16)
                    w.rt = [sb("rt%d" % c, [128, 4, 2, 8], F32) for c in range(4)]
                    w.B = {nm: Buf(nm + str(wi)) for nm in ("junk", "st4", "krf", "lT", "kf", "kb", "rt")}
                    wss.append(w)
                kd_bufs = [Buf("kd%d" % c) for c in range(44)]
                qd_bufs = [Buf("qd%d" % c) for c in range(40)]

                def set_banks(pass_no):
                    for wi, w in enumerate(wss):
                        if pass_no == 0:
                            if wi < 2:
                                w.bT, w.bP, w.bU, w.bQ = (0, 1, 2, 3) if wi == 0 else (4, 5, 6, 7)
                            else:
                                continue
                        else:
                            w.bT, w.bQ = 2 * wi, 2 * wi + 1
                        w.pTb = banks[w.bT][:].bitcast(BF16).rearrange("p (k t) -> p k t", k=8)
                set_banks(0)
                for kc in range(8):
                    P.dma("pool", win[:, kc, :], w_in[j, kc * 128:(kc + 1) * 128, :], W=[B["win"]], owner=B["win"])
                P.dma("pool", wqb[:], w_qb[j].rearrange("(k p) n -> p k n", p=128), W=[B["wqb"]], owner=B["wqb"])
                P.dma("pool", wkvb[:], w_kvb[j], W=[B["wkvb"]], owner=B["wkvb"])
                P.dma("pool", woA[:], w_out[j, 0:512, :].rearrange("(h p) n -> p h n", p=64), W=[B["woA"]], owner=B["woA"])
                P.dma("pool", woG[:], w_out[j, 512:1024, :].rearrange("(k p) n -> p k n", p=128), W=[B["woG"]], owner=B["woG"])
                P.dma("pool", wsf[:], gmlp_ws[j].rearrange("g p q -> p g q"), W=[B["wsf"]], owner=B["wsf"])
                P.dma("sp", qan[:], q_a_norm[j:j + 1, :].partition_broadcast(128), W=[B["rows"]], owner=B["rows"])
                P.dma("sp", kvan[:], kv_a_norm[j:j + 1, :].partition_broadcast(128), W=[B["rows"]], owner=B["rows"])
                P.dma("sp", qnr[:], q_norm[j:j + 1, :].partition_broadcast(128), W=[B["rows"]], owner=B["rows"])
                P.dma("sp", knr[:], k_norm[j:j + 1, :].partition_broadcast(128), W=[B["rows"]], owner=B["rows"])
                P.dma("sp", vnr[:], gmlp_v_norm[j:j + 1, :].partition_broadcast(128), W=[B["rows"]], owner=B["rows"])
                with nc.allow_non_contiguous_dma(reason="tiny transposed bias load"):
                    P.dma("sp", gb[:], gmlp_b[j].rearrange("g p -> p g"), W=[B["gb"]], owner=B["gb"])
                P.dma("sp", rC[:], ropec.rearrange("(s p) f -> p s f", p=128), W=[B["rope"]], owner=B["rope"])
                P.dma("sp", rS[:], ropes.rearrange("(s p) f -> p s f", p=128), W=[B["rope"]], owner=B["rope"])
                P.dma("sp", Gt[:], gsc[i, 1, 0:1, :].partition_broadcast(128), R=[b_gsc], W=[B["G"]], owner=B["G"])
                P.dma("sp", gn[:], norm_g[i, 1:2, :].partition_broadcast(128), W=[B["gn"]], owner=B["gn"])
                w0 = wss[0]
                for g in range(4):
                    P.op("pe", lambda g=g: nc.tensor.transpose(out=w0.pTb[:, g, :], in_=wsf[:, g, :], identity=ident[:]),
                         R=[B["wsf"], b_ident], W=[bb[w0.bT]])
                P.op("dve", lambda: nc.vector.tensor_copy(wsT[:], w0.pTb[:, 0:4, :]), R=[bb[w0.bT]], W=[B["wsT"]])
                P.op("act", lambda: nc.scalar.activation(out=w0.junk[:, 0:96], in_=qnr[:], func=AF.Square), R=[B["rows"]], W=[w0.B["junk"]])
                P.op("act", lambda: nc.scalar.activation(out=w0.junk[:, 96:192], in_=knr[:], func=AF.Square), R=[B["rows"]], W=[w0.B["junk"]])
                P.op("dve", lambda: nc.vector.tensor_reduce(out=cst[:, 0:2], in_=w0.junk[:, 0:192].rearrange("p (a b) -> p a b", a=2),
                                                            axis=AX.X, op=ALU.max), R=[w0.B["junk"]], W=[B["cst"]])
                P.op("dve", lambda: nc.vector.tensor_tensor(out=cst[:, 2:3], in0=cst[:, 0:1], in1=cst[:, 1:2], op=ALU.mult),
                     R=[B["cst"]], W=[B["cst"]])
                P.op("act", lambda: nc.scalar.activation(out=cst[:, 3:4], in_=cst[:, 2:3], func=AF.Sqrt), R=[B["cst"]], W=[B["cst"]])
                P.op("dve", lambda: nc.vector.tensor_scalar(out=cst[:, 4:5], in0=cst[:, 3:4], scalar1=-(96.0 ** 0.5), scalar2=None,
                                                            op0=ALU.mult), R=[B["cst"]], W=[B["cst"]])
                P.op("dve", lambda: nc.vector.tensor_scalar(out=qnr[:], in0=qnr[:], scalar1=96.0 ** -0.5, scalar2=None, op0=ALU.mult),
                     R=[B["rows"], w0.B["junk"]], W=[B["rows"]])
                P.op("pool", lambda: nc.gpsimd.memset(V[:], 1.0), W=[B["V"]])

                def g_rstd(st_, b_st, n, inv):
                    P.op("dve", lambda: nc.vector.tensor_scalar(out=st_[:, n:2 * n], in0=st_[:, 0:n], scalar1=inv, scalar2=EPS,
                                                                op0=ALU.mult, op1=ALU.add), R=[b_st], W=[b_st])
                    yield
                    P.op("act", lambda: nc.scalar.activation(out=st_[:, n:2 * n], in_=st_[:, n:2 * n], func=AF.Sqrt),
                         R=[b_st], W=[b_st])
                    yield
                    P.op("dve", lambda: nc.vector.reciprocal(out=st_[:, 2 * n:3 * n], in_=st_[:, n:2 * n]), R=[b_st], W=[b_st])
                    yield

                def g_front(w, s, cvi):
                    WB = w.B
                    P.dma("sp", w.xt[:], xaps[src][s * 128:(s + 1) * 128, :], R=[xbufs[src][s]], W=[WB["xt"]], owner=WB["xt"])
                    yield
                    P.op("act", lambda: nc.scalar.activation(out=w.xn[:], in_=w.xt[:], func=AF.Square, accum_out=w.sts[:, 0:1]),
                         R=[WB["xt"]], W=[WB["xn"], WB["st"]])
                    yield
                    yield from g_rstd(w.sts, WB["st"], 1, 1.0 / D)
                    P.op("dve", lambda: nc.vector.scalar_tensor_tensor(out=w.xn[:], in0=w.xt[:], scalar=w.sts[:, 2:3], in1=gn[:],
                                                                      op0=ALU.mult, op1=ALU.mult),
                         R=[WB["xt"], WB["st"], B["gn"]], W=[WB["xn"]])
                    yield
                    for kc in range(8):
                        P.op("pe", lambda kc=kc: nc.tensor.transpose(out=w.pTb[:, kc, :], in_=w.xn[:, kc * 128:(kc + 1) * 128],
                                                                     identity=ident[:]), R=[WB["xn"], b_ident], W=[bb[w.bT]])
                    yield
                    for kc in range(8):
                        eng = "act" if kc % 2 == 0 else "dve"
                        if eng == "act":
                            P.op("act", lambda kc=kc: nc.scalar.activation(
                                out=w.hTs[:, kc, :], in_=w.pTb[:, kc, :], func=AF.Identity,
                                scale=AS[:, ik, 0, kc, cvi:cvi + 1], bias=AS[:, ik, 1, kc, cvi:cvi + 1]),
                                R=[bb[w.bT], b_AS], W=[WB["hTs"]])
                        else:
                            P.op("dve", lambda kc=kc: nc.vector.tensor_scalar(
                                out=w.hTs[:, kc, :], in0=w.pTb[:, kc, :], scalar1=AS[:, ik, 0, kc, cvi:cvi + 1],
                                scalar2=AS[:, ik, 1, kc, cvi:cvi + 1], op0=ALU.mult, op1=ALU.add),
                                R=[bb[w.bT], b_AS], W=[WB["hTs"]])
                        if kc % 2 == 1:
                            yield

                def g_headnorm(w, row, rope_idx):
                    WB = w.B
                    T = w.kf
                    bT_ = WB["kf"]
                    P.op("act", lambda: nc.scalar.activation(out=w.junk[:, 0:384], in_=T[:].rearrange("p h d -> p (h d)"), func=AF.Square),
                         R=[bT_], W=[WB["junk"]])
                    yield
                    P.op("dve", lambda: nc.vector.tensor_reduce(out=w.st4[:, 0:4], in_=w.junk[:, 0:384].rearrange("p (h d) -> p h d", h=4),
                                                                axis=AX.X, op=ALU.add), R=[WB["junk"]], W=[WB["st4"]])
                    yield
                    yield from g_rstd(w.st4, WB["st4"], 4, 1.0 / 96)
                    P.op("dve", lambda: nc.vector.tensor_tensor(out=T[:], in0=T[:], in1=w.st4[:, 8:12].unsqueeze(2).to_broadcast([128, 4, 96]),
                                                                op=ALU.mult), R=[bT_, WB["st4"]], W=[bT_])
                    yield
                    P.op("dve", lambda: nc.vector.tensor_tensor(out=T[:], in0=T[:], in1=row[:].unsqueeze(1).to_broadcast([128, 4, 96]),
                                                                op=ALU.mult), R=[bT_, B["rows"]], W=[bT_])
                    yield
                    if rope_idx is not None:
                        r = T[:, :, 64:96].rearrange("p h (a t f) -> p h a t f", a=2, t=2)
                        x1 = r[:, :, :, 0, :]
                        x2 = r[:, :, :, 1, :]
                        cb = rC[:, rope_idx, :].rearrange("p (a f) -> p a f", a=2).unsqueeze(1).to_broadcast([128, 4, 2, 8])
                        sbk = rS[:, rope_idx, :].rearrange("p (a f) -> p a f", a=2).unsqueeze(1).to_broadcast([128, 4, 2, 8])
                        rt = w.rt
                        P.op("dve", lambda: nc.vector.tensor_tensor(out=rt[0][:], in0=x1, in1=cb, op=ALU.mult), R=[bT_, B["rope"]], W=[WB["rt"]])
                        P.op("pool", lambda: nc.gpsimd.tensor_tensor(out=rt[1][:], in0=x2, in1=sbk, op=ALU.mult), R=[bT_, B["rope"]], W=[WB["rt"]])
                        P.op("dve", lambda: nc.vector.tensor_tensor(out=rt[2][:], in0=x1, in1=sbk, op=ALU.mult), R=[bT_, B["rope"]], W=[WB["rt"]])
                        P.op("pool", lambda: nc.gpsimd.tensor_tensor(out=rt[3][:], in0=x2, in1=cb, op=ALU.mult), R=[bT_, B["rope"]], W=[WB["rt"]])
                        yield
                        P.op("dve", lambda: nc.vector.tensor_tensor(out=x1, in0=rt[0][:], in1=rt[1][:], op=ALU.subtract), R=[WB["rt"]], W=[bT_])
                        P.op("dve", lambda: nc.vector.tensor_tensor(out=x2, in0=rt[2][:], in1=rt[3][:], op=ALU.add), R=[WB["rt"]], W=[bT_])
                        yield

                def g_kpath(w, chunk, rope_idx, h0, save_idx=None, load_idx=None):
                    WB = w.B
                    if load_idx is None:
                        P.op("act", lambda: nc.scalar.copy(out=w.kvb[:, 0:128], in_=w.kvf[:]), R=[WB["kvf"]], W=[WB["kvb"]])
                        yield
                        P.op("pe", lambda: nc.tensor.transpose(out=w.pTb[:, 0, :], in_=w.kvb[:, 0:128], identity=ident[:]),
                             R=[WB["kvb"], b_ident], W=[bb[w.bT]])
                        yield
                        P.op("dve", lambda: nc.vector.tensor_copy(w.lT[:, 0, :], w.pTb[:, 0, :]), R=[bb[w.bT]], W=[WB["lT"]])
                        yield
                        if save_idx is not None:
                            P.dma("sp", klT_d[save_idx], w.lT[:, 0, :], R=[WB["lT"]], W=[kd_bufs[save_idx]], owner=WB["lT"], kind="st")
                            P.dma("sp", krf_d[save_idx], w.krf[:], R=[WB["krf"]], W=[kd_bufs[save_idx]], owner=WB["krf"], kind="st")
                    else:
                        P.dma("sp", w.lT[:, 0, :], klT_d[load_idx], R=[kd_bufs[load_idx]], W=[WB["lT"]], owner=WB["lT"])
                        P.dma("sp", w.krf[:], krf_d[load_idx], R=[kd_bufs[load_idx]], W=[WB["krf"]], owner=WB["krf"])
                        yield
                    P.op("pe", lambda: nc.tensor.matmul(banks[w.bQ][:], lhsT=w.lT[:, 0, :], rhs=wkvb[:, h0 * 128:(h0 + 4) * 128],
                                                       start=True, stop=True), R=[WB["lT"], B["wkvb"]], W=[bb[w.bQ]])
                    yield
                    kvv = banks[w.bQ][:].rearrange("p (h d) -> p h d", h=4)
                    P.op("act", lambda: nc.scalar.copy(out=w.kf[:, :, 0:64], in_=kvv[:, :, 0:64]), R=[bb[w.bQ]], W=[WB["kf"]])
                    P.op("dve", lambda: nc.vector.tensor_copy(w.kf[:, :, 64:96], w.krf[:].unsqueeze(1).to_broadcast([128, 4, 32])),
                         R=[WB["krf"]], W=[WB["kf"]])
                    P.op("dve", lambda: nc.vector.tensor_copy(V[:, chunk, :, 0:64], kvv[:, :, 64:128]), R=[bb[w.bQ]], W=[B["V"]])
                    yield
                    yield from g_headnorm(w, knr, rope_idx)
                    P.op("act", lambda: nc.scalar.copy(out=w.kb[:], in_=w.kf[:]), R=[WB["kf"]], W=[WB["kb"]])
                    yield
                    for hh in range(4):
                        P.op("pe", lambda hh=hh: nc.tensor.transpose(out=w.pTb[0:96, hh, :], in_=w.kb[:, hh, :], identity=ident[:]),
                             R=[WB["kb"], b_ident], W=[bb[w.bT]])
                    yield
                    P.op("dve", lambda: nc.vector.tensor_copy(KT[:, :, chunk * 128:(chunk + 1) * 128], w.pTb[0:96, 0:4, :]),
                         R=[bb[w.bT]], W=[B["KT"]])
                    yield

                def g_cache(w, c, h0):
                    WB = w.B
                    P.dma("sp", w.kvf[:], cckv[j, c * 128:(c + 1) * 128, :], W=[WB["kvf"]], owner=WB["kvf"])
                    P.dma("sp", w.krf[:], ckro[j, c * 128:(c + 1) * 128, :], W=[WB["krf"]], owner=WB["krf"])
                    yield
                    yield from g_kpath(w, c, None, h0, save_idx=40 + c)

                def g_phaseA(w, s, idx, chunk, cvi, rope_idx, h0, out_seq):
                    WB = w.B
                    yield from g_front(w, s, cvi)
                    for kc in range(8):
                        P.op("pe", lambda kc=kc: nc.tensor.matmul(banks[w.bP][:, 0:160], lhsT=w.hTs[:, kc, :], rhs=win[:, kc, 256:416],
                                                                  start=(kc == 0), stop=(kc == 7)), R=[WB["hTs"], B["win"]], W=[bb[w.bP]])
                    yield
                    P.op("act", lambda: nc.scalar.activation(out=w.junk[:, 0:128], in_=banks[w.bP][:, 0:128], func=AF.Square,
                                                             accum_out=w.sts[:, 0:1]), R=[bb[w.bP]], W=[WB["junk"], WB["st"]])
                    yield
                    yield from g_rstd(w.sts, WB["st"], 1, 1.0 / 128)
                    P.op("dve", lambda: nc.vector.scalar_tensor_tensor(out=w.kvf[:], in0=banks[w.bP][:, 0:128], scalar=w.sts[:, 2:3],
                                                                      in1=kvan[:], op0=ALU.mult, op1=ALU.mult),
                         R=[bb[w.bP], WB["st"], B["rows"]], W=[WB["kvf"]])
                    P.op("act", lambda: nc.scalar.copy(out=w.krf[:], in_=banks[w.bP][:, 128:160]), R=[bb[w.bP]], W=[WB["krf"]])
                    yield
                    if out_seq is not None:
                        P.dma("sp", ckv_o[out_seq, j, idx * 128:(idx + 1) * 128, :], w.kvf[:], R=[WB["kvf"]], owner=WB["kvf"], kind="st")
                        P.dma("sp", kro_o[out_seq, j, idx * 128:(idx + 1) * 128, :], w.krf[:], R=[WB["krf"]], owner=WB["krf"], kind="st")
                        yield
                    yield from g_kpath(w, chunk, rope_idx, h0, save_idx=s)

                def g_phaseB1(w, s, q, rope_idx, h0):
                    WB = w.B
                    P.dma("sp", w.lT[:].rearrange("p k t -> p (k t)"), qlT_d[s], R=[qd_bufs[s]], W=[WB["lT"]], owner=WB["lT"])
                    yield
                    yield from g_qtail(w, q, rope_idx, h0)

                def g_qtail(w, q, rope_idx, h0):
                    WB = w.B
                    for kc in range(2):
                        P.op("pe", lambda kc=kc: nc.tensor.matmul(banks[w.bQ][:, 0:384], lhsT=w.lT[:, kc, :],
                                                                  rhs=wqb[:, kc, h0 * 96:(h0 + 4) * 96],
                                                                  start=(kc == 0), stop=(kc == 1)), R=[WB["lT"], B["wqb"]], W=[bb[w.bQ]])
                    yield
                    P.op("act", lambda: nc.scalar.copy(out=w.kf[:].rearrange("p h d -> p (h d)"), in_=banks[w.bQ][:, 0:384]),
                         R=[bb[w.bQ]], W=[WB["kf"]])
                    yield
                    yield from g_headnorm(w, qnr, rope_idx)
                    P.op("act", lambda: nc.scalar.copy(out=w.kb[:], in_=w.kf[:]), R=[WB["kf"]], W=[WB["kb"]])
                    yield
                    for hh in range(4):
                        P.op("pe", lambda hh=hh: nc.tensor.transpose(out=w.pTb[0:96, hh, :], in_=w.kb[:, hh, :], identity=ident[:]),
                             R=[WB["kb"], b_ident], W=[bb[w.bT]])
                    yield
                    P.op("dve", lambda: nc.vector.tensor_copy(QT[:, :, q * 128:(q + 1) * 128], w.pTb[0:96, 0:4, :]),
                         R=[bb[w.bT]], W=[B["QT"]])
                    yield

                def g_qpart(w, s, q, rope_idx, h0):
                    WB = w.B
                    for kc in range(8):
                        P.op("pe", lambda kc=kc: nc.tensor.matmul(banks[w.bP][:, 0:256], lhsT=w.hTs[:, kc, :], rhs=win[:, kc, 0:256],
                                                                  start=(kc == 0), stop=(kc == 7)), R=[WB["hTs"], B["win"]], W=[bb[w.bP]])
                    yield
                    P.op("act", lambda: nc.scalar.activation(out=w.junk[:, 0:256], in_=banks[w.bP][:, 0:256], func=AF.Square,
                                                             accum_out=w.sts[:, 0:1]), R=[bb[w.bP]], W=[WB["junk"], WB["st"]])
                    yield
                    yield from g_rstd(w.sts, WB["st"], 1, 1.0 / 256)
                    P.op("dve", lambda: nc.vector.scalar_tensor_tensor(out=w.kvb[:], in0=banks[w.bP][:, 0:256], scalar=w.sts[:, 2:3],
                                                                      in1=qan[:], op0=ALU.mult, op1=ALU.mult),
                         R=[bb[w.bP], WB["st"], B["rows"]], W=[WB["kvb"]])
                    yield
                    for kc in range(2):
                        P.op("pe", lambda kc=kc: nc.tensor.transpose(out=w.pTb[:, kc, :], in_=w.kvb[:, kc * 128:(kc + 1) * 128],
                                                                     identity=ident[:]), R=[WB["kvb"], b_ident], W=[bb[w.bT]])
                    yield
                    P.op("dve", lambda: nc.vector.tensor_copy(w.lT[:], w.pTb[:, 0:2, :]), R=[bb[w.bT]], W=[WB["lT"]])
                    yield
                    P.dma("sp", qlT_d[s], w.lT[:].rearrange("p k t -> p (k t)"), R=[WB["lT"]], W=[qd_bufs[s]], owner=WB["lT"], kind="st")
                    yield from g_qtail(w, q, rope_idx, h0)

                def g_gpart(w, s, q):
                    WB = w.B
                    for blk, dstt, bd in ((0, w.ug, WB["ug"]), (1, w.vg, WB["vg"])):
                        for kc in range(8):
                            P.op("pe", lambda kc=kc, blk=blk: nc.tensor.matmul(
                                banks[w.bU][:], lhsT=w.hTs[:, kc, :], rhs=win[:, kc, 416 + blk * 512:928 + blk * 512],
                                start=(kc == 0), stop=(kc == 7)), R=[WB["hTs"], B["win"]], W=[bb[w.bU]])
                        yield
                        P.op("act", lambda: nc.scalar.activation(out=w.g1[:], in_=banks[w.bU][:], func=AF.Square),
                             R=[bb[w.bU]], W=[WB["g1"]])
                        yield
                        P.op("dve", lambda: nc.vector.tensor_scalar(out=w.g1[:], in0=w.g1[:], scalar1=0.044715, scalar2=1.0,
                                                                    op0=ALU.mult, op1=ALU.add), R=[WB["g1"]], W=[WB["g1"]])
                        yield
                        P.op("dve", lambda: nc.vector.tensor_tensor(out=w.g2[:], in0=banks[w.bU][:], in1=w.g1[:], op=ALU.mult),
                             R=[bb[w.bU], WB["g1"]], W=[WB["g2"]])
                        yield
                        P.op("act", lambda: nc.scalar.activation(out=w.g2[:], in_=w.g2[:], func=AF.Sigmoid, scale=1.5957691216057308),
                             R=[WB["g2"]], W=[WB["g2"]])
                        yield
                        P.op("dve", lambda dstt=dstt: nc.vector.tensor_tensor(out=dstt[:], in0=banks[w.bU][:], in1=w.g2[:], op=ALU.mult),
                             R=[bb[w.bU], WB["g2"]], W=[bd])
                        yield
                    P.op("act", lambda: nc.scalar.activation(out=w.g2[:], in_=w.vg[:], func=AF.Square, accum_out=w.stg[:, 0:1]),
                         R=[WB["vg"]], W=[WB["g2"], WB["stg"]])
                    yield
                    yield from g_rstd(w.stg, WB["stg"], 1, 1.0 / 512)
                    P.op("dve", lambda: nc.vector.scalar_tensor_tensor(out=w.vnb[:], in0=w.vg[:], scalar=w.stg[:, 2:3], in1=vnr[:],
                                                                      op0=ALU.mult, op1=ALU.mult),
                         R=[WB["vg"], WB["stg"], B["rows"]], W=[WB["vnb"]])
                    yield
                    for g in range(4):
                        P.op("pe", lambda g=g: nc.tensor.matmul(banks[w.bU][:, g * 128:(g + 1) * 128], lhsT=wsT[:, g, :],
                                                                rhs=w.vnb[:, g * 128:(g + 1) * 128], start=True, stop=True),
                             R=[B["wsT"], WB["vnb"]], W=[bb[w.bU]])
                    yield
                    for g in range(4):
                        P.op("dve", lambda g=g: nc.vector.scalar_tensor_tensor(
                            out=w.gout[:, g * 128:(g + 1) * 128], in0=banks[w.bU][:, g * 128:(g + 1) * 128], scalar=gb[:, g:g + 1],
                            in1=w.ug[:, g * 128:(g + 1) * 128], op0=ALU.add, op1=ALU.mult),
                            R=[bb[w.bU], B["gb"], WB["ug"]], W=[WB["gout"]])
                    yield
                    for g in range(4):
                        P.op("pe", lambda g=g: nc.tensor.transpose(out=w.pTb[:, 4 + g, :], in_=w.gout[:, g * 128:(g + 1) * 128],
                                                                   identity=ident[:]), R=[WB["gout"], b_ident], W=[bb[w.bT]])
                    yield
                    P.op("dve", lambda: nc.vector.tensor_copy(goutT[:, :, q * 128:(q + 1) * 128], w.pTb[:, 4:8, :]),
                         R=[bb[w.bT]], W=[B["goutT"]])
                    yield


                def g_phaseB(w, s, q, cvi, rope_idx, h0, pass_no):
                    yield from g_front(w, s, cvi)
                    parts = [g_qpart(w, s, q, rope_idx, h0)]
                    if pass_no == 0:
                        parts.append(g_gpart(w, s, q))
                    while parts:
                        for g in list(parts):
                            try:
                                next(g)
                            except StopIteration:
                                parts.remove(g)
                        yield

                def run_seq(subs, ncache, use_rope, pass_no, seq_id):
                    h0 = 4 * pass_no
                    cvi = cv_of(subs[0])
                    n = len(subs)
                    nchunks = n + ncache
                    base = src if pass_no == 0 else dst
                    set_banks(pass_no)
                    jobs = []
                    for c in range(ncache):
                        jobs.append(("c", c))
                    for idx, s in enumerate(subs):
                        jobs.append(("s", idx))
                    if pass_no == 0:
                        for p0 in range(0, len(jobs), 2):
                            gens = []
                            for wi, jb in enumerate(jobs[p0:p0 + 2]):
                                if jb[0] == "c":
                                    gens.append(g_cache(wss[wi], jb[1], h0))
                                else:
                                    idx = jb[1]
                                    s = subs[idx]
                                    gens.append(g_phaseA(wss[wi], s, idx, ncache + idx, cvi, (s - 8) if use_rope else None, h0,
                                                         seq_id if ncache == 0 else None))
                            interleave(gens)
                    else:
                        for p0 in range(0, len(jobs), 4):
                            gens = []
                            for wi, jb in enumerate(jobs[p0:p0 + 4]):
                                if jb[0] == "c":
                                    gens.append(g_kpath(wss[wi], jb[1], None, h0, load_idx=40 + jb[1]))
                                else:
                                    idx = jb[1]
                                    s = subs[idx]
                                    gens.append(g_kpath(wss[wi], ncache + idx, (s - 8) if use_rope else None, h0, load_idx=s))
                            interleave(gens)
                    QW = min(512, n * 128)
                    nq = QW // 128
                    for t0 in range(0, n, nq):
                        if pass_no == 0:
                            for q0 in range(0, nq, 2):
                                gens = []
                                for wi in range(2):
                                    q = q0 + wi
                                    s = subs[t0 + q]
                                    gens.append(g_phaseB(wss[wi], s, q, cvi, (s - 8) if use_rope else None, h0, pass_no))
                                if pend_wout[0] is not None:
                                    gens.append(pend_wout[0])
                                    pend_wout[0] = None
                                interleave(gens)
                        else:
                            gens = []
                            for q in range(nq):
                                s = subs[t0 + q]
                                gens.append(g_phaseB1(wss[q], s, q, (s - 8) if use_rope else None, h0))
                            if pend_wout[0] is not None:
                                gens.append(pend_wout[0])
                                pend_wout[0] = None
                            interleave(gens)
                        items = [(hh, c) for hh in range(4) for c in range(0, nchunks, 2)]
                        sc_pairs = [(5, 6), (1, 2)]

                        def emit_score(ii):
                            hh, c = items[ii]
                            pr = sc_pairs[ii % 2]
                            pc = ii % 2
                            for u in range(2):
                                P.op("pe", lambda u=u: nc.tensor.matmul(
                                    banks[pr[u]][:, 0:QW], lhsT=KT[:, hh, (c + u) * 128:(c + u + 1) * 128], rhs=QT[:, hh, 0:QW],
                                    start=True, stop=True), R=[B["KT"], B["QT"]], W=[bb[pr[u]]])
                            if HAM_FILL and QW == 512:
                                P.op("pe", lambda: nc.tensor.matmul(
                                    banks[0][:, 0:HAM_N], lhsT=KT[:, hh, c * 128:(c + 1) * 128], rhs=QT[:, hh, 0:HAM_N],
                                    start=True, stop=True), R=[B["KT"], B["QT"]], W=[bb[0]])
                            src2 = psum_all[:, pr[0] * 512:(pr[0] + 2) * 512].rearrange("p (b n) -> p b n", b=2)[:, :, 0:QW]
                            dst2 = pT[pc][:].rearrange("p (b n) -> p b n", b=2)[:, :, 0:QW]
                            P.op("act", lambda: nc.scalar.activation(out=dst2, in_=src2, func=AF.Exp, bias=cst[:, 4:5], scale=1.0),
                                 R=[bb[pr[0]], bb[pr[1]], B["cst"]], W=[b_pT[pc]])

                        def emit_pv(ii):
                            hh, c = items[ii]
                            pc = ii % 2
                            obk = 7 if hh % 2 == 0 else 3
                            for u in range(2):
                                P.op("pe", lambda u=u: nc.tensor.matmul(
                                    banks[obk][0:65, 0:QW], lhsT=V[:, c + u, hh, :], rhs=pT[pc][:, u * 512:u * 512 + QW],
                                    start=(c + u == 0), stop=(c + u == nchunks - 1)), R=[B["V"], b_pT[pc]], W=[bb[obk]])
                            if c + 2 >= nchunks:
                                def fin(hh=hh, obk=obk):
                                    P.op("dve", lambda: nc.vector.reciprocal(out=rr[64:65, 0:QW], in_=banks[obk][64:65, 0:QW]),
                                         R=[bb[obk]], W=[B["rr"]])
                                    P.op("pe", lambda: nc.tensor.matmul(banks[4][0:64, 0:QW], lhsT=ones[64:65, 0:64], rhs=rr[64:65, 0:QW],
                                                                       start=True, stop=True), R=[B["rr"], b_ones], W=[bb[4]])
                                    P.op("dve", lambda: nc.vector.tensor_copy(ob[:, 0:QW], banks[obk][0:64, 0:QW]), R=[bb[obk]], W=[B["ob"]])
                                    P.op("dve", lambda: nc.vector.tensor_tensor(out=attnT[:, hh, 0:QW], in0=banks[4][0:64, 0:QW],
                                                                                in1=ob[:, 0:QW], op=ALU.mult),
                                         R=[bb[4], B["ob"]], W=[B["attnT"]])
                                pending.append([fin, min(2, nchunks // 2), ii_cur[0]])

                        pending = []
                        ii_cur = [0]
                        emit_score(0)
                        for ii in range(len(items)):
                            ii_cur[0] = ii
                            if ii + 1 < len(items):
                                emit_score(ii + 1)
                            emit_pv(ii)
                            for pn in list(pending):
                                if pn[2] == ii:
                                    continue
                                pn[1] -= 1
                                if pn[1] <= 0:
                                    pn[0]()
                                    pending.remove(pn)
                        for pn in pending:
                            pn[0]()
                        interleave([g_wout(t0, nq, subs, h0, pass_no, base, cvi)])
                    if pend_wout[0] is not None:
                        interleave([pend_wout[0]])
                        pend_wout[0] = None

                def g_wout(t0, nq, subs, h0, pass_no, base, cvi):
                    xr_l = [wss[0].xt, xr]
                    bxr_l = [wss[0].B["xt"], B["xr"]]
                    residual_load(base, subs[t0], xr_l[0], bxr_l[0])
                    for q in range(nq):
                        s = subs[t0 + q]
                        if q + 1 < nq:
                            residual_load(base, subs[t0 + q + 1], xr_l[(q + 1) % 2], bxr_l[(q + 1) % 2])
                        halves = []
                        for half in range(2):
                            bk = 1 + half
                            nmm = 4 + (4 if pass_no == 0 else 0)
                            m = 0
                            for hh in range(4):
                                P.op("pe", lambda hh=hh, q=q, half=half, bk=bk, m=m, nmm=nmm: nc.tensor.matmul(
                                    banks[bk][:], lhsT=attnT[:, hh, q * 128:(q + 1) * 128],
                                    rhs=woA[:, h0 + hh, half * 512:(half + 1) * 512], start=(m == 0), stop=(m == nmm - 1)),
                                    R=[B["attnT"], B["woA"]], W=[bb[bk]])
                                m += 1
                            if pass_no == 0:
                                for cg in range(4):
                                    P.op("pe", lambda cg=cg, q=q, half=half, bk=bk, m=m, nmm=nmm: nc.tensor.matmul(
                                        banks[bk][:], lhsT=goutT[:, cg, q * 128:(q + 1) * 128],
                                        rhs=woG[:, cg, half * 512:(half + 1) * 512], start=(m == 0), stop=(m == nmm - 1)),
                                        R=[B["goutT"], B["woG"]], W=[bb[bk]])
                                    m += 1
                            halves.append((bk, half))
                        residual_cs(dst, s, cvi, halves, G, B["G"], xr_l[q % 2], bxr_l[q % 2], tmp, b_tmp)
                        yield

                pend_wout = [None]

                seqs = [([2 * q, 2 * q + 1], 0, False, q) for q in range(4)] + [(list(range(8, 40)), 4, True, -1)]
                for (subs, ncache, use_rope, seq_id) in seqs:
                    if ncache > 0:
                        P.dma("sp", Gt[:], gsc[i, 1, 1:2, :].partition_broadcast(128), R=[b_gsc], W=[B["G"]], owner=B["G"])
                    for pass_no in range(2):
                        run_seq(subs, ncache, use_rope, pass_no, seq_id)
                P.barrier()
                allw = []
                for w in wss:
                    allw += list(w.B.values())
                P.release(list(B.values()) + allw)

        prologue()
        chain = ["xin"] + ["xa" if (n % 2 == 0) else "xb" for n in range(11)] + ["y"]
        stage_no = [0]

        def nxt():
            n = stage_no[0]
            stage_no[0] += 1
            return chain[n], chain[n + 1]

        import os
        nst = int(os.environ.get("MK_STAGES", "12"))
        plan = []
        for i in range(4):
            plan += [("ffn", i, 0), ("mla" if i % 2 == 0 else "pool", i, 0), ("ffn", i, 1)]
        plan = plan[:nst]
        resolved = []
        for idx, (kind, i, which) in enumerate(plan):
            src, dst = nxt()
            if idx == len(plan) - 1:
                dst = "y"
            resolved.append((kind, i, which, src, dst))
        idx = 0
        while idx < len(resolved):
            kind, i, which, src, dst = resolved[idx]
            if kind == "ffn":
                grp = [(i, which, src, dst)]
                while idx + 1 < len(resolved) and resolved[idx + 1][0] == "ffn":
                    idx += 1
                    grp.append((resolved[idx][1], resolved[idx][2], resolved[idx][3], resolved[idx][4]))
                ffn_stage(grp)
            elif kind == "mla":
                mla_stage(i, src, dst)
            else:
                pool_stage(i, src, dst)
            idx += 1
        P.barrier()
    return nc


def _consts():
    rows = np.repeat(np.arange(64), 64).astype(np.float32)
    cols = np.tile(np.arange(64), 64).astype(np.float32)
    inv = (10000.0 ** (-np.arange(0, 16, 2, dtype=np.float32) / 16)).astype(np.float32)
    ang = np.concatenate([rows[:, None] * inv, cols[:, None] * inv], axis=-1).astype(np.float32)
    ropec = np.cos(ang).astype(np.float32)
    ropes = np.sin(ang).astype(np.float32)
    band = np.zeros((128, 4, 5, 128), np.float32)
    for wi, w in enumerate(WINS):
        h = w // 2
        for t in range(128):
            for var in range(5):
                if var in (0, 3, 4):
                    lo, hi = t - h, t + w - h
                    cnt = w
                elif var == 1:
                    lo, hi = max(t - h, 0), t + w - h
                    cnt = hi - lo
                else:
                    lo, hi = t - h, min(t + w - h, 128)
                    cnt = hi - lo
                for tp in range(lo, hi):
                    if var in (0, 1, 2):
                        if 0 <= tp < 128:
                            band[tp, wi, var, t] += 1.0 / cnt
                    elif var == 3:
                        if tp < 0:
                            band[tp + 128, wi, var, t] += 1.0 / cnt
                    else:
                        if tp >= 128:
                            band[tp - 128, wi, var, t] += 1.0 / cnt
                if var in (0, 1, 2):
                    band[t, wi, var, t] -= 1.0
    return ropec, ropes, band


_NC_CACHE = {}


def kernel(**inp):
    f = lambda a: np.ascontiguousarray(np.asarray(a, dtype=np.float32))
    nc = build_nc()
    ropec, ropes, band = _consts()
    shared = {k: f(inp[k]) for k in ("w_mod", "b_mod", "norm_g", "ffn_w1", "ffn_w3", "ffn_w2", "w_in", "q_a_norm",
                                     "kv_a_norm", "w_qb", "w_kvb", "q_norm", "k_norm", "gmlp_v_norm", "gmlp_ws",
                                     "gmlp_b", "w_out", "pool_w", "pool_scale")}
    shared.update(ropec=ropec, ropes=ropes, band=band)
    xp = f(inp["x_prompt"]); xs = f(inp["x_sample"])
    in_maps = []
    for c in range(8):
        m = dict(shared)
        m["xin"] = np.ascontiguousarray(np.concatenate([xp[4 * c:4 * c + 4].reshape(1024, D), xs[c]], axis=0))
        m["cckv"] = f(inp["cache_ckv"][c])
        m["ckro"] = f(inp["cache_krope"][c])
        m["cvec"] = np.ascontiguousarray(np.stack([f(inp["c_ctx"]), f(inp["c"][c])], axis=0))
        in_maps.append(m)
    res = run_bass_kernel_spmd(nc, in_maps, core_ids=list(range(8)))
    r = res.results
    y_prompt = np.concatenate([r[c]["y"][:1024].reshape(4, 256, D) for c in range(8)], axis=0)
    y_sample = np.stack([r[c]["y"][1024:] for c in range(8)], axis=0)
    new_ckv = np.concatenate([r[c]["ckv_o"] for c in range(8)], axis=0)
    new_krope = np.concatenate([r[c]["kro_o"] for c in range(8)], axis=0)
    return (y_prompt.astype(np.float32), y_sample.astype(np.float32), new_ckv.astype(np.float32),
            new_krope.astype(np.float32))
```
